# Optimizing a Trainium2 kernel written in Bass

```python
import jax, jax.numpy as jnp
from jax import lax
import numpy as np

D_MODEL = 4096
BATCH = 2
SEQ = 4096
DEPTH = 2

N_META = 16
N_BRANCH = 4
MIX_W = D_MODEL // 4
ML_HEADS = 4
ML_HD = MIX_W // ML_HEADS
ML_CHUNK = 64
SB_HEADS = 8
SB_HD = MIX_W // SB_HEADS
SB_BLOCK = 128
HG_HEADS = 8
HG_HD = MIX_W // HG_HEADS
HG_CHUNK = 64
RG_BLOCKS = 16
RG_BD = MIX_W // RG_BLOCKS
RG_CONV = 4
RG_C = 8.0
D_FF = -(-8 * D_MODEL // (3 * 256)) * 256
EPS = 1e-6
NEG = -1e30

IN_SIZES = (MIX_W, MIX_W, MIX_W, MIX_W, ML_HEADS, ML_HEADS,
            MIX_W, MIX_W, MIX_W,
            MIX_W, MIX_W, MIX_W, MIX_W,
            MIX_W, MIX_W,
            N_BRANCH * D_MODEL)
N_IN = sum(IN_SIZES)
ML_F_OFF = 4 * MIX_W + ML_HEADS

kernel_name = "hybrid_mlstm_stickbreak_hgrn2_rglru_block"


def rmsnorm(x, g):
    xf = x.astype(jnp.float32)
    y = xf * lax.rsqrt(jnp.mean(xf * xf, axis=-1, keepdims=True) + EPS)
    return (y * g.astype(jnp.float32)).astype(x.dtype)


def split_cols(h):
    offs, acc = [], 0
    for s in IN_SIZES[:-1]:
        acc += s
        offs.append(acc)
    return jnp.split(h, offs, axis=-1)


def pad_front(t, n):
    return jnp.pad(t, ((0, 0), (n, 0), (0, 0)))


def to_chunks(t, n_heads, c):
    B, Lp, W = t.shape
    return t.reshape(B, Lp // c, c, n_heads, W // n_heads).transpose(1, 0, 3, 2, 4)


def gate_chunks(g, c):
    B, Lp, H = g.shape
    return g.reshape(B, Lp // c, c, H).transpose(1, 0, 3, 2)


def from_chunks(o):
    nc, B, H, c, d = o.shape
    return o.transpose(1, 0, 3, 2, 4).reshape(B, nc * c, H, d)


def mlstm(q, k, v, o_pre, i_pre, f_pre, g_norm):
    f32 = jnp.float32
    B, L, _ = q.shape
    pad = ML_CHUNK - N_META
    Lp = L + pad
    valid = (jnp.arange(Lp) >= pad)[None, :, None]
    qc = to_chunks(pad_front(q.astype(f32) * ML_HD ** -0.5, pad), ML_HEADS, ML_CHUNK)
    kc = to_chunks(pad_front(k.astype(f32), pad), ML_HEADS, ML_CHUNK)
    vc = to_chunks(pad_front(v.astype(f32), pad), ML_HEADS, ML_CHUNK)
    log_i = gate_chunks(jnp.where(valid, pad_front(i_pre.astype(f32), pad), NEG), ML_CHUNK)
    log_f = gate_chunks(jnp.where(valid, jax.nn.log_sigmoid(pad_front(f_pre.astype(f32), pad)), 0.0), ML_CHUNK)
    causal = jnp.tril(jnp.ones((ML_CHUNK, ML_CHUNK), dtype=bool))

    def step(carry, inp):
        C, n, m = carry
        qb, kb, vb, li, lf = inp
        b = jnp.cumsum(lf, axis=-1)
        D = jnp.where(causal, b[..., :, None] - b[..., None, :] + li[..., None, :], NEG)
        g = b + m[..., None]
        m_out = jnp.maximum(g, jnp.max(D, axis=-1))
        s = jnp.einsum('bhtd,bhsd->bhts', qb, kb) * jnp.exp(D - m_out[..., None])
        inter = jnp.exp(g - m_out)
        num = jnp.einsum('bhts,bhse->bhte', s, vb) + inter[..., None] * jnp.einsum('bhtd,bhde->bhte', qb, C)
        den = jnp.sum(s, axis=-1) + inter * jnp.einsum('bhtd,bhd->bht', qb, n)
        h = num / jnp.maximum(jnp.abs(den), jnp.exp(-m_out))[..., None]
        b_end = b[..., -1]
        wlog = b_end[..., None] - b + li
        m_new = jnp.maximum(b_end + m, jnp.max(wlog, axis=-1))
        w = jnp.exp(wlog - m_new[..., None])
        decay = jnp.exp(b_end + m - m_new)
        C_new = decay[..., None, None] * C + jnp.einsum('bhs,bhsd,bhse->bhde', w, kb, vb)
        n_new = decay[..., None] * n + jnp.einsum('bhs,bhsd->bhd', w, kb)
        return (C_new, n_new, m_new), h

    init = (jnp.zeros((B, ML_HEADS, ML_HD, ML_HD), f32),
            jnp.zeros((B, ML_HEADS, ML_HD), f32),
            jnp.zeros((B, ML_HEADS), f32))
    _, hs = lax.scan(step, init, (qc, kc, vc, log_i, log_f))
    h = rmsnorm(from_chunks(hs)[:, pad:], g_norm)
    return (h.reshape(B, L, MIX_W) * jax.nn.sigmoid(o_pre.astype(f32))).astype(q.dtype)


def stick_breaking(q, k, v, gq, gk):
    f32 = jnp.float32
    B, L, _ = q.shape
    pad = SB_BLOCK - N_META
    Lp = L + pad
    nb = Lp // SB_BLOCK

    def heads(t):
        return pad_front(t.astype(f32), pad).reshape(B, Lp, SB_HEADS, SB_HD).transpose(0, 2, 1, 3)

    qh = rmsnorm(heads(q), gq)
    kh = rmsnorm(heads(k), gk)
    vh = heads(v)
    qb = qh.reshape(B, SB_HEADS, nb, SB_BLOCK, SB_HD).transpose(2, 0, 1, 3, 4)
    key_pos = jnp.arange(Lp)

    def block(args):
        q_blk, blk = args
        z = jnp.einsum('bhtd,bhsd->bhts', q_blk, kh) * SB_HD ** -0.5
        t_pos = blk * SB_BLOCK + jnp.arange(SB_BLOCK)
        vis = (key_pos[None, :] < t_pos[:, None]) & (key_pos[None, :] >= pad)
        log_keep = jnp.where(vis, jax.nn.log_sigmoid(-z), 0.0)
        log_between = lax.cumsum(log_keep, axis=3, reverse=True) - log_keep
        a = jnp.where(vis, jnp.exp(jax.nn.log_sigmoid(z) + log_between), 0.0)
        return jnp.einsum('bhts,bhsd->bhtd', a, vh)

    o = lax.map(block, (qb, jnp.arange(nb)))
    o = o.transpose(1, 0, 3, 2, 4).reshape(B, Lp, MIX_W)[:, pad:]
    return o.astype(q.dtype)


def hgrn2(q, f_pre, i_in, g_pre, lb, g_norm):
    f32 = jnp.float32
    B, L, _ = q.shape
    pad = HG_CHUNK - N_META
    Lp = L + pad
    valid = (jnp.arange(Lp) >= pad)[None, :, None]
    fp = f_pre.astype(f32)
    sig = jax.nn.sigmoid(fp)
    log_f = jnp.log(lb + (1.0 - lb) * sig)
    kk = (1.0 - lb) * (1.0 - sig)
    fc = to_chunks(jnp.where(valid, pad_front(log_f, pad), 0.0), HG_HEADS, HG_CHUNK)
    kc = to_chunks(jnp.where(valid, pad_front(kk, pad), 0.0), HG_HEADS, HG_CHUNK)
    qc = to_chunks(pad_front(jax.nn.silu(q.astype(f32)), pad), HG_HEADS, HG_CHUNK)
    ic = to_chunks(pad_front(i_in.astype(f32), pad), HG_HEADS, HG_CHUNK)
    causal = jnp.tril(jnp.ones((HG_CHUNK, HG_CHUNK), dtype=bool))

    def step(S, inp):
        qb, kb, ib, lf = inp
        b = jnp.cumsum(lf, axis=2)
        diff = b[:, :, :, None, :] - b[:, :, None, :, :]
        decay = jnp.exp(jnp.where(causal[:, :, None], diff, NEG))
        att = jnp.einsum('bhtd,bhtsd,bhsd->bhts', qb, decay, kb)
        o = jnp.einsum('bhts,bhse->bhte', att, ib) + jnp.einsum('bhtd,bhde->bhte', qb * jnp.exp(b), S)
        b_end = b[:, :, -1]
        S_new = jnp.exp(b_end)[..., None] * S + jnp.einsum('bhsd,bhse->bhde', kb * jnp.exp(b_end[:, :, None] - b), ib)
        return S_new, o

    S0 = jnp.zeros((B, HG_HEADS, HG_HD, HG_HD), f32)
    _, os_ = lax.scan(step, S0, (qc, kc, ic, fc))
    o = rmsnorm(from_chunks(os_)[:, pad:], g_norm).reshape(B, L, MIX_W)
    return (o * jax.nn.sigmoid(g_pre.astype(f32))).astype(q.dtype)


def _lin_combine(e1, e2):
    a1, b1 = e1
    a2, b2 = e2
    return a1 * a2, a2 * b1 + b2


def rglru(xb, yb, conv_w, conv_b, wa, ba, wx, bx, lam):
    f32 = jnp.float32
    B, L, W = xb.shape
    xc = lax.conv_general_dilated(xb.astype(f32), conv_w.astype(f32)[:, None, :],
                                  window_strides=(1,), padding=[(RG_CONV - 1, 0)],
                                  dimension_numbers=('NWC', 'WIO', 'NWC'),
                                  feature_group_count=W) + conv_b
    xblk = xc.reshape(B, L, RG_BLOCKS, RG_BD)
    r = jax.nn.sigmoid(jnp.einsum('blni,nij->blnj', xblk, wa).reshape(B, L, W) + ba)
    ig = jax.nn.sigmoid(jnp.einsum('blni,nij->blnj', xblk, wx).reshape(B, L, W) + bx)
    log_a = -RG_C * r * jax.nn.softplus(-lam)
    a = jnp.exp(log_a)
    u = jnp.sqrt(-jnp.expm1(2.0 * log_a)) * (ig * xc)
    _, hseq = lax.associative_scan(_lin_combine, (a, u), axis=1)
    return (hseq * jax.nn.gelu(yb.astype(f32))).astype(xb.dtype)


def setup_inputs(seed: int = 0) -> dict:
    key = jax.random.key(seed)
    ks = jax.random.split(key, 24)

    def nrm(k, shape, scale):
        return jax.random.normal(k, shape, jnp.float32) * scale

    x = nrm(ks[0], (BATCH, SEQ, D_MODEL), 1.0)
    meta = nrm(ks[1], (N_META, D_MODEL), 1.0)
    norm_mix = 1.0 + nrm(ks[2], (DEPTH, D_MODEL), 0.02)
    norm_ffn = 1.0 + nrm(ks[3], (DEPTH, D_MODEL), 0.02)
    w_in = nrm(ks[4], (DEPTH, D_MODEL, N_IN), D_MODEL ** -0.5)
    b_in = nrm(ks[5], (DEPTH, N_IN), 0.02)
    b_in = b_in.at[:, ML_F_OFF:ML_F_OFF + ML_HEADS].add(jnp.linspace(3.0, 6.0, ML_HEADS))
    ml_norm = 1.0 + nrm(ks[6], (DEPTH, ML_HEADS, ML_HD), 0.02)
    sb_q_norm = 1.0 + nrm(ks[7], (DEPTH, SB_HD), 0.02)
    sb_k_norm = 1.0 + nrm(ks[8], (DEPTH, SB_HD), 0.02)
    hg_lb = nrm(ks[9], (DEPTH, MIX_W), 0.1)
    hg_norm = 1.0 + nrm(ks[10], (DEPTH, HG_HEADS, HG_HD), 0.02)
    rg_conv_w = nrm(ks[11], (DEPTH, RG_CONV, MIX_W), RG_CONV ** -0.5)
    rg_conv_b = nrm(ks[12], (DEPTH, MIX_W), 0.02)
    rg_wa = nrm(ks[13], (DEPTH, RG_BLOCKS, RG_BD, RG_BD), RG_BD ** -0.5)
    rg_ba = nrm(ks[14], (DEPTH, MIX_W), 0.02)
    rg_wx = nrm(ks[15], (DEPTH, RG_BLOCKS, RG_BD, RG_BD), RG_BD ** -0.5)
    rg_bx = nrm(ks[16], (DEPTH, MIX_W), 0.02)
    a_pow = jax.random.uniform(ks[17], (DEPTH, MIX_W), jnp.float32, minval=0.9, maxval=0.999)
    p = a_pow ** (1.0 / RG_C)
    rg_lambda = jnp.log(p) - jnp.log1p(-p)
    w_up = nrm(ks[18], (DEPTH, N_BRANCH, MIX_W, D_MODEL), MIX_W ** -0.5)
    w_out = nrm(ks[19], (DEPTH, D_MODEL, D_MODEL), D_MODEL ** -0.5)
    w_ffn_gate = nrm(ks[20], (DEPTH, D_MODEL, D_FF), D_MODEL ** -0.5)
    w_ffn_up = nrm(ks[21], (DEPTH, D_MODEL, D_FF), D_MODEL ** -0.5)
    w_ffn_down = nrm(ks[22], (DEPTH, D_FF, D_MODEL), D_FF ** -0.5)
    return {"x": x, "meta": meta, "norm_mix": norm_mix, "norm_ffn": norm_ffn,
            "w_in": w_in, "b_in": b_in, "ml_norm": ml_norm,
            "sb_q_norm": sb_q_norm, "sb_k_norm": sb_k_norm,
            "hg_lb": hg_lb, "hg_norm": hg_norm,
            "rg_conv_w": rg_conv_w, "rg_conv_b": rg_conv_b, "rg_wa": rg_wa, "rg_ba": rg_ba,
            "rg_wx": rg_wx, "rg_bx": rg_bx, "rg_lambda": rg_lambda,
            "w_up": w_up, "w_out": w_out,
            "w_ffn_gate": w_ffn_gate, "w_ffn_up": w_ffn_up, "w_ffn_down": w_ffn_down}


def reference(x, meta, norm_mix, norm_ffn, w_in, b_in, ml_norm, sb_q_norm, sb_k_norm,
              hg_lb, hg_norm, rg_conv_w, rg_conv_b, rg_wa, rg_ba, rg_wx, rg_bx, rg_lambda,
              w_up, w_out, w_ffn_gate, w_ffn_up, w_ffn_down):
    B = x.shape[0]
    h = jnp.concatenate([jnp.broadcast_to(meta[None].astype(x.dtype), (B, N_META, D_MODEL)), x], axis=1)
    L = h.shape[1]
    p_lb = jax.nn.softmax(hg_lb.astype(jnp.float32), axis=0)
    lower_bounds = jnp.clip(jnp.cumsum(p_lb, axis=0) - p_lb[0:1], 0.0, 0.999)
    for l in range(DEPTH):
        xn = rmsnorm(h, norm_mix[l])
        proj = xn @ w_in[l] + b_in[l]
        (ml_q, ml_k, ml_v, ml_o, ml_i, ml_f, sb_q, sb_k, sb_v,
         hg_q, hg_f, hg_i, hg_g, rg_x, rg_y, gate_pre) = split_cols(proj)
        branches = (
            mlstm(ml_q, ml_k, ml_v, ml_o, ml_i, ml_f, ml_norm[l]),
            stick_breaking(sb_q, sb_k, sb_v, sb_q_norm[l], sb_k_norm[l]),
            hgrn2(hg_q, hg_f, hg_i, hg_g, lower_bounds[l], hg_norm[l]),
            rglru(rg_x, rg_y, rg_conv_w[l], rg_conv_b[l], rg_wa[l], rg_ba[l],
                  rg_wx[l], rg_bx[l], rg_lambda[l]),
        )
        gates = jax.nn.sigmoid(gate_pre.reshape(B, L, N_BRANCH, D_MODEL))
        merged = sum(gates[:, :, kb] * (branches[kb] @ w_up[l, kb]) for kb in range(N_BRANCH))
        h = h + merged @ w_out[l]
        hn = rmsnorm(h, norm_ffn[l])
        h = h + (jax.nn.silu(hn @ w_ffn_gate[l]) * (hn @ w_ffn_up[l])) @ w_ffn_down[l]
    return h[:, N_META:]
```

```python
import time
import numpy as np
from contextlib import ExitStack
import concourse.bass as bass
import concourse.mybir as mybir
from concourse.bass_utils import run_bass_kernel_spmd

F32 = mybir.dt.float32
BF16 = mybir.dt.bfloat16
AF = mybir.ActivationFunctionType
ALU = mybir.AluOpType
AX = mybir.AxisListType

ENGS = ("pe", "act", "dve", "pool", "sp")
SAME_ENGINE_SYNC = True


class Buf:
    __slots__ = ("name", "w", "r")

    def __init__(self, name):
        self.name = name
        self.w = None
        self.r = []


class Prog:
    _count = 0

    def __init__(self, nc, same_engine_sync=SAME_ENGINE_SYNC):
        Prog._count += 1
        self.pfx = f"P{Prog._count}_"
        self.nc = nc
        self.es = ExitStack()
        self.ops = {e: [] for e in ENGS}
        self.ninc = {e: 0 for e in ENGS}
        self.same = same_engine_sync
        self.sems = []
        self.esem = {e: self._sem(self.pfx + "s_" + e) for e in ENGS}
        self.dsems = []
        self.nbuf = 0
        self.gseq = 0

    def _sem(self, name):
        h = self.nc.alloc_semaphore(name=name)
        self.sems.append(h)
        return h

    def sb(self, name, shape, dt):
        return self.es.enter_context(self.nc.sbuf_tensor(self.pfx + name, list(shape), dt))

    def ps(self, name, shape, dt=F32):
        return self.es.enter_context(self.nc.psum_tensor(self.pfx + name, list(shape), dt))

    def buf(self, name=None):
        self.nbuf += 1
        return Buf(name or f"b{self.nbuf}")

    def dsem(self, name):
        h = self._sem(self.pfx + name)
        self.dsems.append([h, 0])
        return len(self.dsems) - 1

    def _deps(self, eng, reads, writes):
        deps = []
        for b in reads:
            if b.w is not None:
                deps.append(b.w)
        for b in writes:
            if b.w is not None:
                deps.append(b.w)
            deps.extend(b.r)
        return deps

    def op(self, eng, fn, reads=(), writes=(), inc=True):
        deps = self._deps(eng, reads, writes)
        idx = len(self.ops[eng])
        self.gseq += 1
        self.ops[eng].append({"fn": fn, "deps": deps, "inc": inc, "dma": None, "g": self.gseq})
        tok = ("e", eng, idx)
        for b in reads:
            b.r.append(tok)
        for b in writes:
            b.w = tok
            b.r = []
        return tok

    def dma(self, eng, fn, sem, reads=(), writes=()):
        deps = self._deps(eng, reads, writes)
        if self.dsems[sem][1] > 0:
            deps.append(("d", sem, self.dsems[sem][1]))
        self.dsems[sem][1] += 16
        val = self.dsems[sem][1]
        self.gseq += 1
        self.ops[eng].append({"fn": fn, "deps": deps, "inc": False, "dma": sem, "g": self.gseq})
        tok = ("d", sem, val)
        for b in reads:
            b.r.append(tok)
        for b in writes:
            b.w = tok
            b.r = []
        return tok

    def wait_all(self, eng, bufs):
        deps = self._deps(eng, (), bufs)
        self.gseq += 1
        self.ops[eng].append({"fn": None, "deps": deps, "inc": False, "dma": None, "g": self.gseq})

    def emit(self):
        nc = self.nc
        incval = {}
        for e in ENGS:
            vals = [0] * len(self.ops[e])
            c = 0
            for i, o in enumerate(self.ops[e]):
                if o["inc"]:
                    c += 1
                vals[i] = c
            need = [None] * len(self.ops[e])
            nxt = None
            for i in range(len(self.ops[e]) - 1, -1, -1):
                if self.ops[e][i]["inc"]:
                    nxt = (vals[i], i)
                need[i] = nxt
            incval[e] = need

        def run(e, h):
            waited = {}
            for i, o in enumerate(self.ops[e]):
                for d in o["deps"]:
                    if d[0] == "e":
                        _, e2, i2 = d
                        if e2 == e:
                            if not self.same:
                                continue
                        assert incval[e2][i2] is not None, (e, i, d)
                        v, j2 = incval[e2][i2]
                        if e2 == e and j2 >= i:
                            continue
                        assert self.ops[e2][j2]["g"] < o["g"], ("deadlock: wait on future op", e, i, e2, i2, j2)
                        key = ("e", e2)
                        sem = self.esem[e2]
                    else:
                        _, s, v = d
                        key = ("d", s)
                        sem = self.dsems[s][0]
                    if waited.get(key, 0) >= v:
                        continue
                    waited[key] = v
                    h.wait_ge(sem, v)
                if o["fn"] is None:
                    continue
                ins = o["fn"](h)
                if o["dma"] is not None:
                    ins.then_inc(self.dsems[o["dma"]][0], 16)
                elif o["inc"]:
                    ins.then_inc(self.esem[e], 1)

        with nc.Block() as block:
            @block.tensor
            def _(h):
                run("pe", h)

            @block.scalar
            def _(h):
                run("act", h)

            @block.vector
            def _(h):
                run("dve", h)

            @block.gpsimd
            def _(h):
                run("pool", h)

            @block.sync
            def _(h):
                run("sp", h)
        self.nc.clear_and_free_semaphores(self.sems)
        self.nc.all_engine_barrier()
        self.es.close()


EPS = 1e-6


class Ring:
    def __init__(self, p, name, n, shape, dt):
        self.t = [p.sb(f"{name}{i}", shape, dt) for i in range(n)]
        self.b = [p.buf() for _ in range(n)]
        self.s = [p.dsem(f"{name}s{i}") for i in range(n)]
        self.i = 0

    def next(self):
        i = self.i % len(self.t)
        self.i += 1
        return self.t[i], self.b[i], self.s[i]


def segs_of(n):
    out = []
    o = 0
    while o < n:
        m = min(512, n - o)
        out.append((o, m))
        o += m
    return out


def build_B(KD=32, NF=86, chunks=((0, 528), (528, 512)), stop=99):
    D = 128 * KD
    TT = sum(n for _, n in chunks)
    NH = NF // 2
    CB = KD // 4
    nc = bass.Bass("TRN2", target_bir_lowering=False)
    hT = nc.dram_tensor("hT", [D, TT], F32, kind="ExternalInput").ap()
    yT = nc.dram_tensor("yT", [D, TT], F32, kind="ExternalInput").ap()
    wg = nc.dram_tensor("wg", [KD * 4, 128, KD, 128], F32, kind="ExternalInput").ap()
    bg = nc.dram_tensor("bg", [128, KD * 4], F32, kind="ExternalInput").ap()
    wu = nc.dram_tensor("wu", [KD * 4, 128, CB, 128], F32, kind="ExternalInput").ap()
    wo = nc.dram_tensor("wo", [KD, 128, KD, 128], F32, kind="ExternalInput").ap()
    gm = nc.dram_tensor("gm", [128, KD], F32, kind="ExternalInput").ap()
    gf = nc.dram_tensor("gf", [128, KD], F32, kind="ExternalInput").ap()
    wfg = nc.dram_tensor("wfg", [NF, 128, KD, 128], F32, kind="ExternalInput").ap()
    wfu = nc.dram_tensor("wfu", [NF, 128, KD, 128], F32, kind="ExternalInput").ap()
    wfd = nc.dram_tensor("wfd", [KD, 128, NF, 128], F32, kind="ExternalInput").ap()
    hout = nc.dram_tensor("hout", [D, TT], F32, kind="ExternalOutput").ap()

    p = Prog(nc)
    NMAX = max(n for _, n in chunks)
    WK = max(KD, NH)
    xn = p.sb("xn", [128, KD, NMAX], BF16)
    R = p.sb("R", [128, 2 * KD, NMAX], BF16)
    yt = R[:, 0:KD, :]
    mg = R[:, KD:2 * KD, :]
    at = R[:, 0:NH, :]
    assert NH <= 2 * KD
    wring = Ring(p, "w", 4, [128, WK, 128], BF16)
    uring = Ring(p, "wu", 3, [128, CB, 128], BF16)
    hring = Ring(p, "hs", 4, [128, NMAX], F32)
    oring = Ring(p, "ho", 3, [128, NMAX], F32)
    sqring = [(p.sb(f"sq{i}", [128, NMAX], BF16), p.buf()) for i in range(2)]
    sgring = [(p.sb(f"sg{i}", [128, NMAX], F32), p.buf()) for i in range(3)]
    tmring = [(p.sb(f"tm{i}", [128, NMAX], F32), p.buf()) for i in range(2)]
    macc = p.sb("macc", [128, NMAX], F32); b_macc = p.buf()
    rstd = p.sb("rstd", [128, NMAX], F32); b_rstd = p.buf()
    ones = p.sb("ones", [128, 128], BF16); b_ones = p.buf()
    gmt = p.sb("gmt", [128, KD], F32); gft = p.sb("gft", [128, KD], F32); b_g = p.buf()
    bgt = p.sb("bgt", [128, KD * 4], F32)
    accs = [(p.ps(f"acc{i}", [128, 1024], F32), p.buf()) for i in range(4)]
    cnt = {"acc": 0, "sq": 0, "sg": 0, "tm": 0}

    def nxt(lst, key):
        i = cnt[key] % len(lst)
        cnt[key] += 1
        return lst[i]

    s_c = p.dsem("const")
    p.op("dve", lambda e: e.memset(ones[:], 1.0), writes=[b_ones])
    p.dma("sp", lambda e: e.dma_start(out=gmt[:], in_=gm[:, :]), s_c, writes=[b_g])
    p.dma("sp", lambda e: e.dma_start(out=gft[:], in_=gf[:, :]), s_c, writes=[b_g])
    p.dma("sp", lambda e: e.dma_start(out=bgt[:], in_=bg[:, :]), s_c, writes=[b_g])

    b_xn = [p.buf() for _ in range(KD)]
    b_yt = p.buf()
    b_mg = [p.buf() for _ in range(KD)]
    b_at = [p.buf() for _ in range(NH)]
    out_toks = []

    def mm_group(acc, lhs_list, rhs_fn, n, reads):
        a, b_a = acc
        sg = segs_of(n)
        L = len(lhs_list)
        for i, lt in enumerate(lhs_list):
            for si, (o, m) in enumerate(sg):
                last = (i == L - 1) and (si == len(sg) - 1)
                p.op("pe", (lambda e, lt=lt, i=i, o=o, m=m: e.matmul(
                    a[:, o:o + m], lt, rhs_fn(i, o, m), start=(i == 0), stop=(i == L - 1))),
                    reads=reads, writes=[b_a], inc=last)

    def rmsnorm(src, src_toks, c0, n, gt):
        acc = nxt(accs, "acc")
        a, b_a = acc
        sg = segs_of(n)
        for kc in range(KD):
            ht, b_h, s_h = hring.next()
            p.dma("sp", (lambda e, ht=ht, kc=kc: e.dma_start(out=ht[:, 0:n], in_=src[kc * 128:(kc + 1) * 128, c0:c0 + n])),
                  s_h, reads=[src_toks[kc]] if src_toks else [], writes=[b_h])
            sq, b_sq = nxt(sqring, "sq")
            p.op("act", (lambda e, sq=sq, ht=ht: e.activation(sq[:, 0:n], ht[:, 0:n], AF.Square)),
                 reads=[b_h], writes=[b_sq])
            for si, (o, m) in enumerate(sg):
                last = (si == len(sg) - 1)
                p.op("pe", (lambda e, sq=sq, o=o, m=m, kc=kc: e.matmul(
                    a[:, o:o + m], ones[:], sq[:, o:o + m], start=(kc == 0), stop=(kc == KD - 1))),
                    reads=[b_sq, b_ones], writes=[b_a], inc=last)
        p.op("act", lambda e: e.activation(rstd[:, 0:n], a[:, 0:n], AF.Sqrt, bias=EPS, scale=1.0 / D),
             reads=[b_a], writes=[b_rstd])
        p.op("dve", lambda e: e.reciprocal(rstd[:, 0:n], rstd[:, 0:n]),
             reads=[b_rstd], writes=[b_rstd])
        for kc in range(KD):
            ht, b_h, s_h = hring.next()
            p.dma("sp", (lambda e, ht=ht, kc=kc: e.dma_start(out=ht[:, 0:n], in_=src[kc * 128:(kc + 1) * 128, c0:c0 + n])),
                  s_h, reads=[src_toks[kc]] if src_toks else [], writes=[b_h])
            p.op("dve", (lambda e, ht=ht, kc=kc: e.scalar_tensor_tensor(
                xn[:, kc, 0:n], ht[:, 0:n], gt[:, kc:kc + 1], rstd[:, 0:n], ALU.mult, ALU.mult)),
                reads=[b_h, b_rstd, b_g], writes=[b_xn[kc]])

    def do_chunk(c0, n):
        nonlocal out_toks
        p.wait_all("pool", b_at + [b_yt])
        p.wait_all("dve", b_at + b_mg)
        rmsnorm(hT, None, c0, n, gmt)
        if stop <= 1:
            return
        s_y = p.dsem(f"y{c0}")
        for k in range(4):
            p.dma("pool", (lambda e, k=k: e.dma_start(
                out=yt[:, k * CB:(k + 1) * CB, 0:n],
                in_=yT[k * CB * 128:(k + 1) * CB * 128, c0:c0 + n].rearrange("(c p) t -> p c t", p=128))),
                s_y, writes=[b_yt])
        for m in range(KD):
            for k in range(4):
                u = m * 4 + k
                wt, b_w, s_w = wring.next()
                p.dma("pool", (lambda e, wt=wt, u=u: e.dma_start(out=wt[:, 0:KD, :], in_=wg[u])), s_w, writes=[b_w])
                ut, b_u, s_u = uring.next()
                p.dma("pool", (lambda e, ut=ut, u=u: e.dma_start(out=ut[:, :, :], in_=wu[u])), s_u, writes=[b_u])
                ga = nxt(accs, "acc")
                mm_group(ga, [wt[:, kc, :] for kc in range(KD)], lambda i, o, mm_: xn[:, i, o:o + mm_], n,
                         reads=[b_w] + b_xn)
                sg, b_sg = nxt(sgring, "sg")
                p.op("act", (lambda e, sg=sg, ga=ga, u=u: e.activation(
                    sg[:, 0:n], ga[0][:, 0:n], AF.Sigmoid, bias=bgt[:, u:u + 1], scale=1.0)),
                    reads=[ga[1], b_g], writes=[b_sg])
                ua = nxt(accs, "acc")
                mm_group(ua, [ut[:, cc, :] for cc in range(CB)],
                         lambda i, o, mm_, k=k: yt[:, k * CB + i, o:o + mm_], n, reads=[b_u, b_yt])
                if k == 0:
                    p.op("dve", (lambda e, sg=sg, ua=ua: e.tensor_tensor(macc[:, 0:n], ua[0][:, 0:n], sg[:, 0:n], ALU.mult)),
                         reads=[ua[1], b_sg], writes=[b_macc])
                else:
                    tm, b_tm = nxt(tmring, "tm")
                    p.op("dve", (lambda e, sg=sg, ua=ua, tm=tm: e.tensor_tensor(tm[:, 0:n], ua[0][:, 0:n], sg[:, 0:n], ALU.mult)),
                         reads=[ua[1], b_sg], writes=[b_tm])
                    if k < 3:
                        p.op("dve", (lambda e, tm=tm: e.tensor_tensor(macc[:, 0:n], macc[:, 0:n], tm[:, 0:n], ALU.add)),
                             reads=[b_tm, b_macc], writes=[b_macc])
                    else:
                        p.op("dve", (lambda e, tm=tm, m=m: e.tensor_tensor(mg[:, m, 0:n], macc[:, 0:n], tm[:, 0:n], ALU.add)),
                             reads=[b_tm, b_macc], writes=[b_mg[m]])
        if stop <= 2:
            return
        b_ho = [p.buf() for _ in range(KD)]
        for nt in range(KD):
            wt, b_w, s_w = wring.next()
            p.dma("pool", (lambda e, wt=wt, nt=nt: e.dma_start(out=wt[:, 0:KD, :], in_=wo[nt])), s_w, writes=[b_w])
            a = nxt(accs, "acc")
            mm_group(a, [wt[:, mc, :] for mc in range(KD)], lambda i, o, mm_: mg[:, i, o:o + mm_], n,
                     reads=[b_w] + b_mg)
            ht, b_h, s_h = hring.next()
            p.dma("sp", (lambda e, ht=ht, nt=nt: e.dma_start(out=ht[:, 0:n], in_=hT[nt * 128:(nt + 1) * 128, c0:c0 + n])),
                  s_h, writes=[b_h])
            ot, b_o, s_o = oring.next()
            p.op("dve", (lambda e, ot=ot, ht=ht, a=a: e.tensor_tensor(ot[:, 0:n], a[0][:, 0:n], ht[:, 0:n], ALU.add)),
                 reads=[a[1], b_h], writes=[b_o])
            p.dma("sp", (lambda e, ot=ot, nt=nt: e.dma_start(out=hout[nt * 128:(nt + 1) * 128, c0:c0 + n], in_=ot[:, 0:n])),
                  s_o, reads=[b_o], writes=[b_ho[nt]])
        if stop <= 3:
            out_toks += b_ho
            return
        rmsnorm(hout, b_ho, c0, n, gft)
        if stop <= 4:
            out_toks += b_ho
            return
        for half in range(2):
            if half == 0:
                p.wait_all("dve", [b_yt] + b_mg)
            for fi in range(NH):
                f = half * NH + fi
                wt, b_w, s_w = wring.next()
                p.dma("pool", (lambda e, wt=wt, f=f: e.dma_start(out=wt[:, 0:KD, :], in_=wfg[f])), s_w, writes=[b_w])
                wt2, b_w2, s_w2 = wring.next()
                p.dma("pool", (lambda e, wt2=wt2, f=f: e.dma_start(out=wt2[:, 0:KD, :], in_=wfu[f])), s_w2, writes=[b_w2])
                ga = nxt(accs, "acc")
                mm_group(ga, [wt[:, kc, :] for kc in range(KD)], lambda i, o, mm_: xn[:, i, o:o + mm_], n,
                         reads=[b_w] + b_xn)
                sg, b_sg = nxt(sgring, "sg")
                p.op("act", (lambda e, sg=sg, ga=ga: e.activation(sg[:, 0:n], ga[0][:, 0:n], AF.Silu)),
                     reads=[ga[1]], writes=[b_sg])
                ua = nxt(accs, "acc")
                mm_group(ua, [wt2[:, kc, :] for kc in range(KD)], lambda i, o, mm_: xn[:, i, o:o + mm_], n,
                         reads=[b_w2] + b_xn)
                p.op("dve", (lambda e, sg=sg, ua=ua, fi=fi: e.tensor_tensor(at[:, fi, 0:n], ua[0][:, 0:n], sg[:, 0:n], ALU.mult)),
                     reads=[ua[1], b_sg], writes=[b_at[fi]])
            for nt in range(KD):
                wt, b_w, s_w = wring.next()
                p.dma("pool", (lambda e, wt=wt, nt=nt, half=half: e.dma_start(
                    out=wt[:, 0:NH, :], in_=wfd[nt][:, half * NH:(half + 1) * NH, :])), s_w, writes=[b_w])
                a = nxt(accs, "acc")
                mm_group(a, [wt[:, fi, :] for fi in range(NH)], lambda i, o, mm_: at[:, i, o:o + mm_], n,
                         reads=[b_w] + b_at)
                ht, b_h, s_h = hring.next()
                p.dma("sp", (lambda e, ht=ht, nt=nt: e.dma_start(out=ht[:, 0:n], in_=hout[nt * 128:(nt + 1) * 128, c0:c0 + n])),
                      s_h, reads=[b_ho[nt]], writes=[b_h])
                ot, b_o, s_o = oring.next()
                p.op("dve", (lambda e, ot=ot, ht=ht, a=a: e.tensor_tensor(ot[:, 0:n], a[0][:, 0:n], ht[:, 0:n], ALU.add)),
                     reads=[a[1], b_h], writes=[b_o])
                p.dma("sp", (lambda e, ot=ot, nt=nt: e.dma_start(out=hout[nt * 128:(nt + 1) * 128, c0:c0 + n], in_=ot[:, 0:n])),
                      s_o, reads=[b_o], writes=[b_ho[nt]])
        out_toks += b_ho

    for (c0_, n_) in chunks:
        do_chunk(c0_, n_)
    p.wait_all("sp", out_toks)
    p.emit()
    return nc


def tile_w(w, nk, nm):
    return np.ascontiguousarray(w.reshape(nk, 128, nm, 128).transpose(2, 1, 0, 3))


def prep_B_weights(wgate, bgate, wup, wout, gmix, gffn, wfg, wfu, wfd, KD, NF):
    D = 128 * KD
    CB = KD // 4
    g = wgate.reshape(KD, 128, 4, KD, 128)
    wg = np.ascontiguousarray(g.transpose(3, 2, 1, 0, 4)).reshape(KD * 4, 128, KD, 128)
    bg = np.ascontiguousarray(bgate.reshape(4, KD, 128).transpose(2, 1, 0)).reshape(128, KD * 4)
    u = wup.reshape(4, CB, 128, KD, 128)
    wu = np.ascontiguousarray(u.transpose(3, 0, 2, 1, 4)).reshape(KD * 4, 128, CB, 128)
    return {
        "wg": wg, "bg": bg, "wu": wu, "wo": tile_w(wout, KD, KD),
        "gm": np.ascontiguousarray(gmix.reshape(KD, 128).T), "gf": np.ascontiguousarray(gffn.reshape(KD, 128).T),
        "wfg": tile_w(wfg, KD, NF), "wfu": tile_w(wfu, KD, NF), "wfd": tile_w(wfd, NF, KD),
    }


def chunks_of(L, c):
    return [(o, min(c, L - o)) for o in range(0, L, c)]


HG_MUL_ENG = "pool"


class Lane:
    def __init__(self):
        self.ops = []

    def op(self, *a, **k):
        self.ops.append((a, k))


def interleave(p, lanes):
    n = max(len(l.ops) for l in lanes)
    for i in range(n):
        for l in lanes:
            if i < len(l.ops):
                a, k = l.ops[i]
                p.op(*a, **k)


def phase_norm(nc, hT, xnd, gm, KD, L):
    D = 128 * KD
    p = Prog(nc)
    NP = 4 if KD % 4 == 0 else 1
    KP = KD // NP
    HR = [p.sb(f"hres{i}", [128, KD, 512], F32) for i in range(2)]
    b_HR = [[p.buf() for _ in range(NP)] for _ in range(2)]
    s_HR = [[p.dsem(f"hr{i}_{j}") for j in range(NP)] for i in range(2)]
    XO = [p.sb(f"xo{i}", [128, KD, 512], BF16) for i in range(2)]
    b_XO = [[p.buf() for _ in range(NP)] for _ in range(2)]
    s_XO = [[p.dsem(f"xo{i}_{j}") for j in range(NP)] for i in range(2)]
    sqr = [(p.sb(f"sq{i}", [128, 512], BF16), p.buf()) for i in range(3)]
    rstd = [(p.sb(f"rstd{i}", [128, 512], F32), p.buf()) for i in range(2)]
    ones = p.sb("ones", [128, 128], BF16); b_c = p.buf()
    gmt = p.sb("gmt", [128, KD], F32)
    accs = [(p.ps(f"acc{i}", [128, 512], F32), p.buf()) for i in range(2)]
    s_c = p.dsem("c")
    p.op("dve", lambda e: e.memset(ones[:], 1.0), writes=[b_c])
    p.dma("sp", lambda e: e.dma_start(out=gmt[:], in_=gm[:, :]), s_c, writes=[b_c])
    outs = []

    def chunk(ci, c0, n):
        sl = ci % 2
        hr = HR[sl]; xo = XO[sl]
        acc, b_acc = accs[sl]
        rs, b_rs = rstd[sl]
        for j in range(NP):
            q = "sp" if j % 2 == 0 else "act"
            p.dma(q, (lambda e, j=j: e.dma_start(out=hr[:, j * KP:(j + 1) * KP, 0:n],
                                                 in_=hT[j * KP * 128:(j + 1) * KP * 128, c0:c0 + n].rearrange("(k q) t -> q k t", q=128))),
                  s_HR[sl][j], writes=[b_HR[sl][j]])
        for kc in range(KD):
            sq, b_sq = sqr[kc % 3]
            p.op("act", (lambda e, sq=sq, kc=kc: e.activation(sq[:, 0:n], hr[:, kc, 0:n], AF.Square)), reads=[b_HR[sl][kc // KP]], writes=[b_sq])
            p.op("pe", (lambda e, sq=sq, kc=kc: e.matmul(acc[:, 0:n], ones[:], sq[:, 0:n], start=(kc == 0), stop=(kc == KD - 1))),
                 reads=[b_sq, b_c], writes=[b_acc])
        p.op("act", lambda e: e.activation(rs[:, 0:n], acc[:, 0:n], AF.Sqrt, bias=EPS, scale=1.0 / D), reads=[b_acc], writes=[b_rs])
        p.op("dve", lambda e: e.reciprocal(rs[:, 0:n], rs[:, 0:n]), reads=[b_rs], writes=[b_rs])
        for kc in range(KD):
            p.op("dve", (lambda e, kc=kc: e.scalar_tensor_tensor(xo[:, kc, 0:n], hr[:, kc, 0:n], gmt[:, kc:kc + 1], rs[:, 0:n], ALU.mult, ALU.mult)),
                 reads=[b_HR[sl][kc // KP], b_rs, b_c], writes=[b_XO[sl][kc // KP]])
        for j in range(NP):
            b_x = p.buf()
            p.dma("sp", (lambda e, j=j: e.dma_start(out=xnd[j * KP * 128:(j + 1) * KP * 128, c0:c0 + n].rearrange("(k q) t -> q k t", q=128),
                                                     in_=xo[:, j * KP:(j + 1) * KP, 0:n])),
                  s_XO[sl][j], reads=[b_XO[sl][j]], writes=[b_x])
            outs.append(b_x)

    for ci, (c0, n) in enumerate(chunks_of(L, 512)):
        chunk(ci, c0, n)
    p.wait_all("sp", outs)
    p.emit()


class XnStream:
    def __init__(self, p, xnd, KD, nslots=2, cs=256):
        self.p, self.xnd, self.KD, self.cs = p, xnd, KD, cs
        self.ring = Ring(p, "xn", nslots, [128, KD, cs], BF16)

    def load(self, c0, n):
        p, xnd, KD = self.p, self.xnd, self.KD
        xt, b_x, s_x = self.ring.next()
        p.dma("sp", lambda e: e.dma_start(out=xt[:, :, 0:n], in_=xnd[:, c0:c0 + n].rearrange("(k q) t -> q k t", q=128)), s_x, writes=[b_x])
        return xt, b_x


def load_w(p, name, wd, ntiles, KD, sem):
    wt = p.sb(name, [128, ntiles, KD, 128], BF16)
    b = p.buf()
    sem = p.dsem("w_" + name)
    for j in range(ntiles):
        p.dma("pool", (lambda e, j=j: e.dma_start(out=wt[:, j, :, :], in_=wd[j])), sem, writes=[b])
    return wt, b


def phase_rg(nc, xnd, d, KD, L, y_rg):
    for CT in range(2):
        _phase_rg_ct(nc, xnd, d, KD, L, y_rg, CT)


def _phase_rg_ct(nc, xnd, d, KD, L, y_rg, CT):
    p = Prog(nc)
    s_c = p.dsem("c")
    w, b_w = load_w(p, "w", d["w_rg_fm"], 4, KD, s_c)
    b_c = p.buf()
    small = {}
    for nm, shp in (("b_rg_fm", [128, 4]), ("rg_cw", [128, 8]), ("rg_cb", [128, 2]), ("rg_ba", [128, 2]),
                    ("rg_bx", [128, 2]), ("rg_lam", [128, 2])):
        t = p.sb("s_" + nm, shp, F32)
        p.dma("sp", (lambda e, t=t, nm=nm: e.dma_start(out=t[:], in_=d[nm][:, :])), s_c, writes=[b_c])
        small[nm] = t
    wab = p.sb("wab", [128, 2, 128], BF16); wxb = p.sb("wxb", [128, 2, 128], BF16)
    s_p = p.dsem("cp"); b_cp = p.buf()
    for ct in (CT,):
        p.dma("pool", (lambda e, ct=ct: e.dma_start(out=wab[:, ct, :], in_=d["rg_wa"][ct])), s_p, writes=[b_cp])
        p.dma("pool", (lambda e, ct=ct: e.dma_start(out=wxb[:, ct, :], in_=d["rg_wx"][ct])), s_p, writes=[b_cp])
    XB = p.sb("XB", [128, 1, L + 3], F32)
    YB = p.sb("YB", [128, 1, L], F32)
    XC = p.sb("XC", [128, 1, L], F32)
    XCr = [(p.sb(f"XCr{i}", [128, 512], BF16), p.buf()) for i in range(2)]
    RR = p.sb("RR", [128, 1, L], F32)
    IG = p.sb("IG", [128, 1, L], F32)
    c8 = p.sb("c8", [128, 2], F32)
    b_XB = [p.buf(), p.buf()]; b_YB = [p.buf(), p.buf()]; b_XC = [p.buf(), p.buf()]
    b_RR = [p.buf(), p.buf()]; b_IG = [p.buf(), p.buf()]; b_c8 = p.buf()
    accs = [(p.ps(f"acc{i}", [128, 512], F32), p.buf()) for i in range(4)]
    ai = [0]

    def nacc():
        a = accs[ai[0] % 4]; ai[0] += 1
        return a
    bfm = small["b_rg_fm"]
    p.op("act", lambda e: e.activation(c8[:], small["rg_lam"][:], AF.Exp, scale=-1.0), reads=[b_c], writes=[b_c8])
    p.op("act", lambda e: e.activation(c8[:], c8[:], AF.Ln, bias=1.0, scale=1.0), reads=[b_c8], writes=[b_c8])
    p.op("dve", lambda e: e.tensor_scalar(c8[:], c8[:], -8.0, None, ALU.mult), reads=[b_c8], writes=[b_c8])
    for ct in (CT,):
        p.op("dve", (lambda e, ct=ct: e.memset(XB[:, 0, 0:3], 0.0)), writes=[b_XB[ct]])
    xs = XnStream(p, xnd, KD)
    for (c0, n) in chunks_of(L, xs.cs):
        xt, b_x = xs.load(c0, n)
        for j in (CT, 2 + CT):
            a, b_a = nacc()
            for kc in range(KD):
                p.op("pe", (lambda e, a=a, j=j, kc=kc, xt=xt, n=n: e.matmul(a[:, 0:n], w[:, j, kc, :], xt[:, kc, 0:n], start=(kc == 0), stop=(kc == KD - 1))),
                     reads=[b_w, b_x], writes=[b_a], inc=(kc == KD - 1))
            ct = j % 2
            if j < 2:
                p.op("act", (lambda e, a=a, j=j, ct=ct, c0=c0, n=n: e.activation(XB[:, 0, 3 + c0:3 + c0 + n], a[:, 0:n], AF.Identity, bias=bfm[:, j:j + 1], scale=1.0)),
                     reads=[b_a, b_c], writes=[b_XB[ct]])
            else:
                p.op("act", (lambda e, a=a, j=j, ct=ct, c0=c0, n=n: e.activation(YB[:, 0, c0:c0 + n], a[:, 0:n], AF.Identity, bias=bfm[:, j:j + 1], scale=1.0)),
                     reads=[b_a, b_c], writes=[b_YB[ct]])
    cw = small["rg_cw"]
    for ct in (CT,):
        p.op("dve", (lambda e, ct=ct: e.tensor_scalar(XC[:, 0, :], XB[:, 0, 0:L], cw[:, ct * 4:ct * 4 + 1], small["rg_cb"][:, ct:ct + 1], ALU.mult, ALU.add)),
             reads=[b_XB[ct], b_c], writes=[b_XC[ct]])
        for j in range(1, 4):
            p.op("dve", (lambda e, ct=ct, j=j: e.scalar_tensor_tensor(XC[:, 0, :], XB[:, 0, j:j + L], cw[:, ct * 4 + j:ct * 4 + j + 1], XC[:, 0, :], ALU.mult, ALU.add)),
                 reads=[b_XB[ct], b_XC[ct], b_c], writes=[b_XC[ct]])
    xi = [0]
    for ct in (CT,):
        for (c0, n) in chunks_of(L, 512):
            xcb, b_xcb = XCr[xi[0] % 2]; xi[0] += 1
            p.op("act", (lambda e, xcb=xcb, ct=ct, c0=c0, n=n: e.activation(xcb[:, 0:n], XC[:, 0, c0:c0 + n], AF.Copy)), reads=[b_XC[ct]], writes=[b_xcb])
            a, b_a = nacc()
            p.op("pe", (lambda e, a=a, ct=ct, xcb=xcb, n=n: e.matmul(a[:, 0:n], wab[:, ct, :], xcb[:, 0:n], start=True, stop=True)),
                 reads=[b_cp, b_xcb], writes=[b_a])
            p.op("act", (lambda e, a=a, ct=ct, c0=c0, n=n: e.activation(RR[:, 0, c0:c0 + n], a[:, 0:n], AF.Sigmoid, bias=small["rg_ba"][:, ct:ct + 1], scale=1.0)),
                 reads=[b_a, b_c], writes=[b_RR[ct]])
            a, b_a = nacc()
            p.op("pe", (lambda e, a=a, ct=ct, xcb=xcb, n=n: e.matmul(a[:, 0:n], wxb[:, ct, :], xcb[:, 0:n], start=True, stop=True)),
                 reads=[b_cp, b_xcb], writes=[b_a])
            p.op("act", (lambda e, a=a, ct=ct, c0=c0, n=n: e.activation(IG[:, 0, c0:c0 + n], a[:, 0:n], AF.Sigmoid, bias=small["rg_bx"][:, ct:ct + 1], scale=1.0)),
                 reads=[b_a, b_c], writes=[b_IG[ct]])
    s_o = p.dsem("o")
    outs = []
    for ct in (CT,):
        A_ = RR[:, 0, :]; U_ = IG[:, 0, :]; X_ = XC[:, 0, :]; Y_ = YB[:, 0, :]; T_ = XB[:, 0, 0:L]
        bA, bU, bX, bY, bT = b_RR[ct], b_IG[ct], b_XC[ct], b_YB[ct], b_XB[ct]
        p.op("act", (lambda e, A_=A_, ct=ct: e.activation(A_, A_, AF.Exp, scale=c8[:, ct:ct + 1])), reads=[bA, b_c8], writes=[bA])
        p.op("dve", (lambda e, T_=T_, A_=A_: e.tensor_tensor(T_, A_, A_, ALU.mult)), reads=[bA], writes=[bT])
        p.op("dve", (lambda e, T_=T_: e.tensor_scalar(T_, T_, -1.0, 1.0, ALU.mult, ALU.add)), reads=[bT], writes=[bT])
        p.op("act", (lambda e, T_=T_: e.activation(T_, T_, AF.Sqrt)), reads=[bT], writes=[bT])
        p.op("dve", (lambda e, U_=U_, X_=X_: e.tensor_tensor(U_, U_, X_, ALU.mult)), reads=[bU, bX], writes=[bU])
        p.op("dve", (lambda e, U_=U_, T_=T_: e.tensor_tensor(U_, U_, T_, ALU.mult)), reads=[bU, bT], writes=[bU])
        for si, (s0, sn) in enumerate(chunks_of(L, 2048)):
            init = 0.0 if si == 0 else XC[:, 0, s0 - 1:s0]
            p.op("dve", (lambda e, ct=ct, s0=s0, sn=sn, init=init: e.tensor_tensor_scan(
                XC[:, 0, s0:s0 + sn], RR[:, 0, s0:s0 + sn], IG[:, 0, s0:s0 + sn], init, ALU.mult, ALU.add)),
                reads=[bA, bU, bX], writes=[bX])
        p.op("dve", (lambda e, T_=T_, Y_=Y_: e.tensor_tensor(T_, Y_, Y_, ALU.mult)), reads=[bY], writes=[bT])
        p.op("dve", (lambda e, T_=T_: e.tensor_scalar(T_, T_, 0.044715, 1.0, ALU.mult, ALU.add)), reads=[bT], writes=[bT])
        p.op("dve", (lambda e, T_=T_, Y_=Y_: e.tensor_tensor(T_, T_, Y_, ALU.mult)), reads=[bT, bY], writes=[bT])
        p.op("act", (lambda e, T_=T_: e.activation(T_, T_, AF.Sigmoid, scale=1.5957691216057308)), reads=[bT], writes=[bT])
        p.op("dve", (lambda e, T_=T_, Y_=Y_: e.tensor_tensor(T_, T_, Y_, ALU.mult)), reads=[bT, bY], writes=[bT])
        p.op("dve", (lambda e, T_=T_, X_=X_: e.tensor_tensor(T_, T_, X_, ALU.mult)), reads=[bT, bX], writes=[bT])
        b_o = p.buf()
        p.dma("sp", (lambda e, ct=ct, T_=T_: e.dma_start(out=y_rg[ct * 128:(ct + 1) * 128, :], in_=T_)), s_o, reads=[bT], writes=[b_o])
        outs.append(b_o)
    p.wait_all("sp", outs)
    p.emit()


def make_masks(p, strict):
    ones = p.sb("mones", [128, 512], F32)
    b = p.buf()
    p.op("pool", lambda e: e.memset(ones[:], 1.0), writes=[b])
    ms = []
    for r in range(4):
        m = p.sb(f"mask{r}", [128, 512], F32)
        p.op("pool", (lambda e, m=m, r=r: e.affine_select(m[:], ones[:], [[1, 512]], ALU.is_gt if strict else ALU.is_ge, 0.0,
                                                           base=-128 * r, channel_multiplier=-1)), reads=[b], writes=[b])
        ms.append(m)
    return ms, b


def phase_sb(nc, xnd, d, KD, L, y_sb):
    p = Prog(nc)
    NTL = (L + 127) // 128
    s_c = p.dsem("c")
    wf, b_wf = load_w(p, "wf", d["w_sb_fm"], 4, KD, s_c)
    wt, b_wt = load_w(p, "wt", d["w_sb_tm"], 2, KD, s_c)
    b_c = p.buf()
    bfm = p.sb("bfm", [128, 4], F32); btm = p.sb("btm", [128, 256], F32); gqk = p.sb("gqk", [128, 2], F32)
    p.dma("sp", lambda e: e.dma_start(out=bfm[:], in_=d["b_sb_fm"][:, :]), s_c, writes=[b_c])
    p.dma("sp", lambda e: e.dma_start(out=btm[:], in_=d["b_sb_tm"][:, :]), s_c, writes=[b_c])
    p.dma("sp", lambda e: e.dma_start(out=gqk[:], in_=d["sb_gqk"][:, :]), s_c, writes=[b_c])
    ones_b = p.sb("ones_b", [128, 128], BF16)
    onesf = p.sb("onesf", [128, 128], F32); utri = p.sb("utri", [128, 128], F32)
    p.op("dve", lambda e: e.memset(ones_b[:], 1.0), writes=[b_c])
    p.op("dve", lambda e: e.memset(onesf[:], 1.0), writes=[b_c])
    p.op("pool", lambda e: e.affine_select(utri[:], onesf[:], [[-1, 128]], ALU.is_ge, 0.0, base=0, channel_multiplier=1), reads=[b_c], writes=[b_c])
    masks, b_m = make_masks(p, strict=True)
    p.op("dve", lambda e: e.tensor_scalar(gqk[:, 0:1], gqk[:, 0:1], 128.0 ** -0.5, None, ALU.mult), reads=[b_c], writes=[b_c])
    QK = p.sb("QK", [128, 4, L], BF16)
    V = p.sb("V", [128, NTL, 256], BF16)
    b_QK = [p.buf() for _ in range(4)]; b_V = p.buf()
    accs = [(p.ps(f"acc{i}", [128, 512], F32), p.buf()) for i in range(6)]
    ai = [0]

    def nacc():
        a = accs[ai[0] % 6]; ai[0] += 1
        return a
    tq = [(p.sb(f"tq{i}", [128, 512], F32), p.buf()) for i in range(2)]
    ts = [(p.sb(f"ts{i}", [128, 512], BF16), p.buf()) for i in range(2)]
    rs = [(p.sb(f"rs{i}", [128, 512], F32), p.buf()) for i in range(2)]
    xs = XnStream(p, xnd, KD)
    ti = [0]
    for (c0, n) in chunks_of(L, xs.cs):
        xt, b_x = xs.load(c0, n)
        for j in range(4):
            a, b_a = nacc()
            for kc in range(KD):
                p.op("pe", (lambda e, a=a, j=j, kc=kc, xt=xt, n=n: e.matmul(a[:, 0:n], wf[:, j, kc, :], xt[:, kc, 0:n], start=(kc == 0), stop=(kc == KD - 1))),
                     reads=[b_wf, b_x], writes=[b_a], inc=(kc == KD - 1))
            q, b_q = tq[ti[0] % 2]; sq, b_sq = ts[ti[0] % 2]; r, b_r = rs[ti[0] % 2]; ti[0] += 1
            p.op("act", (lambda e, a=a, q=q, j=j, n=n: e.activation(q[:, 0:n], a[:, 0:n], AF.Identity, bias=bfm[:, j:j + 1], scale=1.0)),
                 reads=[b_a, b_c], writes=[b_q])
            p.op("act", (lambda e, q=q, sq=sq, n=n: e.activation(sq[:, 0:n], q[:, 0:n], AF.Square)), reads=[b_q], writes=[b_sq])
            a2, b_a2 = nacc()
            p.op("pe", (lambda e, a2=a2, sq=sq, n=n: e.matmul(a2[:, 0:n], ones_b[:], sq[:, 0:n], start=True, stop=True)),
                 reads=[b_sq, b_c], writes=[b_a2])
            p.op("act", (lambda e, a2=a2, r=r, n=n: e.activation(r[:, 0:n], a2[:, 0:n], AF.Sqrt, bias=EPS, scale=1.0 / 128)), reads=[b_a2], writes=[b_r])
            p.op("dve", (lambda e, r=r, n=n: e.reciprocal(r[:, 0:n], r[:, 0:n])), reads=[b_r], writes=[b_r])
            gcol = 0 if j < 2 else 1
            p.op("dve", (lambda e, q=q, r=r, j=j, c0=c0, n=n, gcol=gcol: e.scalar_tensor_tensor(QK[:, j, c0:c0 + n], q[:, 0:n], gqk[:, gcol:gcol + 1], r[:, 0:n], ALU.mult, ALU.mult)),
                 reads=[b_q, b_r, b_c], writes=[b_QK[j]])
        for (o, m) in chunks_of(n, 128):
            tl = (c0 + o) // 128
            a, b_a = nacc()
            for kc in range(KD):
                p.op("pe", (lambda e, a=a, kc=kc, xt=xt, o=o, m=m: e.matmul(a[0:m, 0:128], xt[:, kc, o:o + m], wt[:, 0, kc, :], start=(kc == 0), stop=(kc == KD - 1))),
                     reads=[b_wt, b_x], writes=[b_a], inc=False)
            for kc in range(KD):
                p.op("pe", (lambda e, a=a, kc=kc, xt=xt, o=o, m=m: e.matmul(a[0:m, 256:384], xt[:, kc, o:o + m], wt[:, 1, kc, :], start=(kc == 0), stop=(kc == KD - 1))),
                     reads=[b_wt, b_x], writes=[b_a], inc=(kc == KD - 1))
            p.op("dve", (lambda e, a=a, tl=tl, m=m: e.tensor_tensor(V[0:m, tl, 0:128], a[0:m, 0:128], btm[0:m, 0:128], ALU.add)), reads=[b_a, b_c], writes=[b_V])
            p.op("dve", (lambda e, a=a, tl=tl, m=m: e.tensor_tensor(V[0:m, tl, 128:256], a[0:m, 256:384], btm[0:m, 128:256], ALU.add)), reads=[b_a, b_c], writes=[b_V])
    NR = 3
    EZ = [(p.sb(f"ez{i}", [128, 512], F32), p.buf()) for i in range(NR)]
    SP = [(p.sb(f"sp{i}", [128, 512], F32), p.buf()) for i in range(NR)]
    ER = [(p.sb(f"er{i}", [128, 512], F32), p.buf()) for i in range(NR)]
    AA = [(p.sb(f"aa{i}", [128, 512], BF16), p.buf()) for i in range(NR)]
    ACC = p.sb("ACC", [128, 512], F32); b_ACC = p.buf()
    OT = [(p.sb(f"ot{i}", [128, 512], F32), p.buf()) for i in range(2)]
    pos = [(p.ps(f"po{i}", [128, 512], F32), p.buf()) for i in range(2)]
    s_o = p.dsem("o")
    outs = []
    items = []
    gi = 0
    for h in range(2):
        for (c0, n) in chunks_of(L, 512):
            jmax = (c0 + n - 1) // 128
            jl = list(range(jmax, -1, -1))
            for idx, j in enumerate(jl):
                k0 = j * 128
                kb = min(128, L - k0)
                diag = (k0 + kb - 1) >= c0
                items.append(dict(h=h, c0=c0, n=n, j=j, k0=k0, kb=kb, diag=diag, r=((k0 - c0) // 128 if diag else None),
                                  first=(idx == 0), last=(idx == len(jl) - 1), slot=len(items) % NR, grp=gi))
            gi += 1

    def st1(it):
        h, c0, n, k0, kb = it["h"], it["c0"], it["n"], it["k0"], it["kb"]
        ez, b_ez = EZ[it["slot"]]; sp, b_sp = SP[it["slot"]]
        z, b_z = nacc()
        p.op("pe", (lambda e: e.matmul(z[0:kb, 0:n], QK[:, 2 + h, k0:k0 + kb], QK[:, h, c0:c0 + n], start=True, stop=True)),
             reads=[b_QK[2 + h], b_QK[h]], writes=[b_z])
        p.op("act", (lambda e: e.activation(ez[0:kb, 0:n], z[0:kb, 0:n], AF.Exp)), reads=[b_z], writes=[b_ez])
        p.op("act", (lambda e: e.activation(sp[0:kb, 0:n], ez[0:kb, 0:n], AF.Ln, bias=1.0, scale=1.0)), reads=[b_ez], writes=[b_sp])
        if it["diag"]:
            mk = masks[it["r"]]
            p.op("dve", (lambda e: e.tensor_tensor(sp[0:kb, 0:n], sp[0:kb, 0:n], mk[0:kb, 0:n], ALU.mult)), reads=[b_sp, b_m], writes=[b_sp])

    def st2(it):
        n, kb, first = it["n"], it["kb"], it["first"]
        ez, b_ez = EZ[it["slot"]]; sp, b_sp = SP[it["slot"]]; er, b_er = ER[it["slot"]]; aa, b_aa = AA[it["slot"]]
        rp, b_rp = nacc()
        p.op("pe", (lambda e: e.matmul(rp[0:kb, 0:n], utri[0:kb, 0:kb], sp[0:kb, 0:n], start=True, stop=first)),
             reads=[b_sp, b_c], writes=[b_rp], inc=first)
        if not first:
            p.op("pe", (lambda e: e.matmul(rp[0:kb, 0:n], onesf[:, 0:kb], ACC[:, 0:n], start=False, stop=True)),
                 reads=[b_ACC, b_c], writes=[b_rp])
        if first:
            if kb < 128:
                p.op("pool", (lambda e: e.memset(ACC[:, 0:n], 0.0)), writes=[b_ACC])
            p.op("pool", (lambda e: e.tensor_copy(ACC[0:kb, 0:n], sp[0:kb, 0:n])), reads=[b_sp], writes=[b_ACC])
        elif not it["last"]:
            p.op("pool", (lambda e: e.tensor_tensor(ACC[0:kb, 0:n], ACC[0:kb, 0:n], sp[0:kb, 0:n], ALU.add)), reads=[b_sp, b_ACC], writes=[b_ACC])
        p.op("act", (lambda e: e.activation(er[0:kb, 0:n], rp[0:kb, 0:n], AF.Exp, scale=-1.0)), reads=[b_rp], writes=[b_er])
        if it["diag"]:
            mk = masks[it["r"]]
            p.op("dve", (lambda e: e.tensor_tensor(er[0:kb, 0:n], ez[0:kb, 0:n], er[0:kb, 0:n], ALU.mult)), reads=[b_ez, b_er], writes=[b_er])
            p.op("dve", (lambda e: e.tensor_tensor(aa[0:kb, 0:n], er[0:kb, 0:n], mk[0:kb, 0:n], ALU.mult)), reads=[b_er, b_m], writes=[b_aa])
        else:
            p.op("dve", (lambda e: e.tensor_tensor(aa[0:kb, 0:n], ez[0:kb, 0:n], er[0:kb, 0:n], ALU.mult)), reads=[b_ez, b_er], writes=[b_aa])

    def st3(it):
        h, c0, n, j, kb = it["h"], it["c0"], it["n"], it["j"], it["kb"]
        aa, b_aa = AA[it["slot"]]
        po, b_po = pos[it["grp"] % 2]
        p.op("pe", (lambda e: e.matmul(po[:, 0:n], V[0:kb, j, h * 128:(h + 1) * 128], aa[0:kb, 0:n], start=it["first"], stop=it["last"])),
             reads=[b_aa, b_V], writes=[b_po], inc=True)
        if it["last"]:
            ot, b_ot = OT[it["grp"] % 2]
            p.op("act", (lambda e: e.activation(ot[:, 0:n], po[:, 0:n], AF.Copy)), reads=[b_po], writes=[b_ot])
            b_o = p.buf()
            p.dma("sp", (lambda e: e.dma_start(out=y_sb[h * 128:(h + 1) * 128, c0:c0 + n], in_=ot[:, 0:n])), s_o, reads=[b_ot], writes=[b_o])
            outs.append(b_o)

    stages = [st1, st2, st3]
    for t in range(len(items) + len(stages) - 1):
        for si, fn in enumerate(stages):
            i = t - si
            if 0 <= i < len(items):
                fn(items[i])
    p.wait_all("sp", outs)
    p.emit()


def phase_ml(nc, xnd, d, KD, L, y_ml):
    NTL = (L + 127) // 128
    with nc.sbuf_tensor("mlQK", [128, 4, L], BF16) as QK, \
            nc.sbuf_tensor("mlV1", [128, NTL, 257], BF16) as V1, \
            nc.sbuf_tensor("mlOG", [128, NTL, 256], BF16) as OG, \
            nc.sbuf_tensor("mlIF", [1, 2, L], F32) as IF:
        _ml_proj_fm(nc, xnd, d, KD, L, QK, IF)
        _ml_proj_tm(nc, xnd, d, KD, L, V1, OG)
        _ml_attn(nc, d, KD, L, QK, V1, OG, IF, y_ml)


def _ml_proj_fm(nc, xnd, d, KD, L, QK, IF):
    p = Prog(nc)
    wf, b_wf = load_w(p, "wf", d["w_ml_fm"], 4, KD, None)
    s_c = p.dsem("c"); b_c = p.buf()
    wif = p.sb("wif", [128, KD, 2], BF16)
    s_p = p.dsem("cp"); b_cp = p.buf()
    p.dma("pool", lambda e: e.dma_start(out=wif[:], in_=d["w_ml_if"][:, :, :]), s_p, writes=[b_cp])
    bfm = p.sb("bfm", [128, 4], F32); bif = p.sb("bif", [1, 2], F32)
    p.dma("sp", lambda e: e.dma_start(out=bfm[:], in_=d["b_ml_fm"][:, :]), s_c, writes=[b_c])
    p.dma("sp", lambda e: e.dma_start(out=bif[:], in_=d["b_ml_if"][:, :]), s_c, writes=[b_c])
    accs = [(p.ps(f"acc{i}", [128, 512], F32), p.buf()) for i in range(4)]
    ai = [0]

    def nacc():
        a = accs[ai[0] % 4]; ai[0] += 1
        return a
    b_QK = [p.buf() for _ in range(4)]; b_IF = p.buf()
    xs = XnStream(p, xnd, KD)
    for (c0, n) in chunks_of(L, xs.cs):
        xt, b_x = xs.load(c0, n)
        for j in range(4):
            a, b_a = nacc()
            for kc in range(KD):
                p.op("pe", (lambda e, a=a, j=j, kc=kc, xt=xt, n=n: e.matmul(a[:, 0:n], wf[:, j, kc, :], xt[:, kc, 0:n], start=(kc == 0), stop=(kc == KD - 1))),
                     reads=[b_wf, b_x], writes=[b_a], inc=(kc == KD - 1))
            p.op("act", (lambda e, a=a, j=j, c0=c0, n=n: e.activation(QK[:, j, c0:c0 + n], a[:, 0:n], AF.Identity, bias=bfm[:, j:j + 1], scale=1.0)),
                 reads=[b_a, b_c], writes=[b_QK[j]])
        for q in range(2):
            a, b_a = nacc()
            for kc in range(KD):
                p.op("pe", (lambda e, a=a, q=q, kc=kc, xt=xt, n=n: e.matmul(a[0:1, 0:n], wif[:, kc, q:q + 1], xt[:, kc, 0:n], start=(kc == 0), stop=(kc == KD - 1))),
                     reads=[b_cp, b_x], writes=[b_a], inc=(kc == KD - 1))
            p.op("act", (lambda e, a=a, q=q, c0=c0, n=n: e.activation(IF[0:1, q, c0:c0 + n], a[0:1, 0:n], AF.Identity, bias=bif[0:1, q:q + 1], scale=1.0)),
                 reads=[b_a, b_c], writes=[b_IF])
    p.wait_all("act", b_QK + [b_IF])
    p.emit()


def _ml_proj_tm(nc, xnd, d, KD, L, V1, OG):
    p = Prog(nc)
    wt, b_wt = load_w(p, "wt", d["w_ml_tm"], 4, KD, None)
    s_c = p.dsem("c"); b_c = p.buf()
    btm = p.sb("btm", [128, 512], F32)
    p.dma("sp", lambda e: e.dma_start(out=btm[:], in_=d["b_ml_tm"][:, :]), s_c, writes=[b_c])
    accs = [(p.ps(f"acc{i}", [128, 512], F32), p.buf()) for i in range(4)]
    ai = [0]

    def nacc():
        a = accs[ai[0] % 4]; ai[0] += 1
        return a
    b_V = p.buf(); b_O = p.buf()
    tmp = [(p.sb(f"tmp{i}", [128, 256], F32), p.buf()) for i in range(2)]
    p.op("dve", lambda e: e.memset(V1[:, :, 256:257], 1.0), writes=[b_V])
    xs = XnStream(p, xnd, KD)
    for (c0, n) in chunks_of(L, xs.cs):
        xt, b_x = xs.load(c0, n)
        for (o, m) in chunks_of(n, 128):
            tl = (c0 + o) // 128
            a, b_a = nacc()
            for j in range(4):
                for kc in range(KD):
                    p.op("pe", (lambda e, a=a, j=j, kc=kc, xt=xt, o=o, m=m: e.matmul(a[0:m, j * 128:(j + 1) * 128], xt[:, kc, o:o + m], wt[:, j, kc, :], start=(kc == 0), stop=(kc == KD - 1))),
                         reads=[b_wt, b_x], writes=[b_a], inc=(kc == KD - 1 and j == 3))
            p.op("dve", (lambda e, a=a, tl=tl, m=m: e.tensor_tensor(V1[0:m, tl, 0:256], a[0:m, 0:256], btm[0:m, 0:256], ALU.add)), reads=[b_a, b_c], writes=[b_V])
            t, b_t = tmp[tl % 2]
            p.op("dve", (lambda e, a=a, t=t, m=m: e.tensor_tensor(t[0:m, :], a[0:m, 256:512], btm[0:m, 256:512], ALU.add)), reads=[b_a, b_c], writes=[b_t])
            p.op("act", (lambda e, t=t, tl=tl, m=m: e.activation(OG[0:m, tl, :], t[0:m, :], AF.Sigmoid)), reads=[b_t], writes=[b_O])
    p.wait_all("act", [b_V, b_O])
    p.wait_all("dve", [b_V, b_O])
    p.emit()


def _ml_attn(nc, d, KD, L, QK, V1, OG, IF, y_ml):
    p = Prog(nc)
    NTL = (L + 127) // 128
    s_c = p.dsem("c"); b_c = p.buf()
    gt = p.sb("gt", [128, 256], F32)
    p.dma("sp", lambda e: e.dma_start(out=gt[:], in_=d["ml_g"][:, :]), s_c, writes=[b_c])
    masks, b_m = make_masks(p, strict=False)
    R1 = p.sb("R1", [1, L], F32); R2 = p.sb("R2", [1, L], F32)
    irow = IF[0:1, 0, :]; frow = IF[0:1, 1, :]
    b_i, b_f, b_r1, b_r2 = p.buf(), p.buf(), p.buf(), p.buf()
    one1 = p.sb("one1", [1, 128], F32); b_one = p.buf()
    p.op("dve", lambda e: e.memset(one1[:], 1.0), writes=[b_one])
    p.op("dve", lambda e: e.memset(R2[:], 1.0), writes=[b_r2])
    p.op("act", lambda e: e.activation(frow, frow, AF.Exp, scale=-1.0), writes=[b_f])
    p.op("act", lambda e: e.activation(frow, frow, AF.Ln, bias=1.0, scale=1.0), reads=[b_f], writes=[b_f])
    for si, (s0, sn) in enumerate(chunks_of(L, 2048)):
        init = 0.0 if si == 0 else R1[0:1, s0 - 1:s0]
        p.op("dve", (lambda e, s0=s0, sn=sn, init=init: e.tensor_tensor_scan(R1[0:1, s0:s0 + sn], R2[0:1, s0:s0 + sn], IF[0:1, 1, s0:s0 + sn], init, ALU.mult, ALU.add)),
             reads=[b_f, b_r2, b_r1], writes=[b_r1])
    p.op("dve", lambda e: e.tensor_tensor(irow, irow, R1[0:1, :], ALU.add), reads=[b_r1], writes=[b_i])
    for si, (s0, sn) in enumerate(chunks_of(L, 2048)):
        init = 0.0 if si == 0 else IF[0:1, 1, s0 - 1:s0]
        p.op("dve", (lambda e, s0=s0, sn=sn, init=init: e.tensor_tensor_scan(IF[0:1, 1, s0:s0 + sn], IF[0:1, 0, s0:s0 + sn], IF[0:1, 0, s0:s0 + sn], init, ALU.max, ALU.max)),
             reads=[b_i, b_f, b_r1], writes=[b_f])
    p.op("dve", lambda e: e.tensor_tensor(R2[0:1, :], R1[0:1, :], frow, ALU.subtract), reads=[b_r1, b_f], writes=[b_r2])
    p.op("act", lambda e: e.activation(R2[0:1, :], R2[0:1, :], AF.Exp), reads=[b_r2], writes=[b_r2])
    p.op("dve", lambda e: e.tensor_scalar(frow, frow, -1.0, None, ALU.mult), reads=[b_f, b_r2], writes=[b_f])
    accs = [(p.ps(f"acc{i}", [128, 512], F32), p.buf()) for i in range(3)]
    ai = [0]

    def nacc():
        a = accs[ai[0] % 3]; ai[0] += 1
        return a
    NMX = p.sb("NMX", [128, L], F32); b_nmx = p.buf()
    ATK = p.sb("ATK", [128, NTL], F32); ETK = p.sb("ETK", [128, NTL], F32); b_tk = p.buf()
    for (c0, n) in chunks_of(L, 512):
        a, b_a = nacc()
        p.op("pe", (lambda e, a=a, c0=c0, n=n: e.matmul(a[:, 0:n], one1[0:1, :], IF[0:1, 1, c0:c0 + n], start=True, stop=True)),
             reads=[b_f, b_one], writes=[b_a])
        p.op("act", (lambda e, a=a, c0=c0, n=n: e.activation(NMX[:, c0:c0 + n], a[:, 0:n], AF.Copy)), reads=[b_a], writes=[b_nmx])
    for (src, dst, bsrc) in ((IF[0:1, 0, :], ATK, b_i), (R2[0:1, :], ETK, b_r2)):
        a, b_a = nacc()
        for tl, (k0, m) in enumerate(chunks_of(L, 128)):
            p.op("pe", (lambda e, a=a, src=src, tl=tl, k0=k0, m=m: e.matmul(a[0:m, tl:tl + 1], src[0:1, k0:k0 + m], one1[0:1, 0:1], start=True, stop=True)),
                 reads=[bsrc, b_one], writes=[b_a], inc=(tl == NTL - 1))
        p.op("dve", lambda e, dst=dst: e.memset(dst[:], 0.0), writes=[b_tk])
        for tl, (k0, m) in enumerate(chunks_of(L, 128)):
            if m == 128 and tl > 0:
                continue
            if tl == 0:
                nf = L // 128
                p.op("dve", (lambda e, a=a, dst=dst, nf=nf: e.tensor_copy(dst[:, 0:nf], a[:, 0:nf])), reads=[b_a], writes=[b_tk])
            else:
                p.op("dve", (lambda e, a=a, dst=dst, tl=tl, m=m: e.tensor_copy(dst[0:m, tl:tl + 1], a[0:m, tl:tl + 1])), reads=[b_a], writes=[b_tk])
    NR = 3
    WW = [(p.sb(f"ww{i}", [128, 512], F32), p.buf()) for i in range(NR)]
    PP = [(p.sb(f"pp{i}", [128, 512], BF16), p.buf()) for i in range(NR)]
    nums = [(p.ps(f"num{i}", [128, 512], F32), p.buf()) for i in range(4)]
    HH = [(p.sb(f"hh{i}", [128, 256], F32), p.buf()) for i in range(4)]
    SQ = [(p.sb(f"sqh{i}", [128, 256], F32), p.buf()) for i in range(4)]
    SM = [(p.sb(f"sm{i}", [128, 4], F32), p.buf()) for i in range(4)]
    YO = Ring(p, "yo", 4, [128, 256], F32)
    outs = []
    fi = [0]
    b_QK = p.buf(); b_V = p.buf(); b_O = p.buf()
    items = []
    for (c0, n) in chunks_of(L, 512):
        qbs = chunks_of(n, 128)
        jmax = (c0 + n - 1) // 128
        for j in range(jmax + 1):
            k0 = j * 128
            kb = min(128, L - k0)
            diag = (k0 + kb - 1) >= c0
            items.append(dict(c0=c0, n=n, j=j, k0=k0, kb=kb, diag=diag, r=((k0 - c0) // 128 if diag else None),
                              qbs=qbs, last=(j == jmax), slot=len(items) % NR))

    def st1(it):
        c0, n, j, k0, kb = it["c0"], it["n"], it["j"], it["k0"], it["kb"]
        ww, b_ww = WW[it["slot"]]
        st, b_st = nacc()
        it["st"] = (st, b_st)
        for dt in range(2):
            p.op("pe", (lambda e, dt=dt: e.matmul(st[0:kb, 0:n], QK[:, 2 + dt, k0:k0 + kb], QK[:, dt, c0:c0 + n], start=(dt == 0), stop=(dt == 1))),
                 reads=[b_QK], writes=[b_st], inc=(dt == 1))
        p.op("act", (lambda e: e.activation(ww[0:kb, 0:n], NMX[0:kb, c0:c0 + n], AF.Exp, bias=ATK[0:kb, j:j + 1], scale=1.0)),
             reads=[b_nmx, b_tk], writes=[b_ww])

    def st2(it):
        n, kb = it["n"], it["kb"]
        ww, b_ww = WW[it["slot"]]; pp, b_pp = PP[it["slot"]]
        st, b_st = it["st"]
        if it["diag"]:
            mk = masks[it["r"]]
            p.op("dve", (lambda e: e.scalar_tensor_tensor(ww[0:kb, 0:n], st[0:kb, 0:n], 0.0625, ww[0:kb, 0:n], ALU.mult, ALU.mult)),
                 reads=[b_st, b_ww], writes=[b_ww])
            p.op("dve", (lambda e: e.tensor_tensor(pp[0:kb, 0:n], ww[0:kb, 0:n], mk[0:kb, 0:n], ALU.mult)), reads=[b_ww, b_m], writes=[b_pp])
        else:
            p.op("dve", (lambda e: e.scalar_tensor_tensor(pp[0:kb, 0:n], st[0:kb, 0:n], 0.0625, ww[0:kb, 0:n], ALU.mult, ALU.mult)),
                 reads=[b_st, b_ww], writes=[b_pp])

    def st3(it):
        c0, n, j, kb = it["c0"], it["n"], it["j"], it["kb"]
        pp, b_pp = PP[it["slot"]]
        for qb, (qo, mq) in enumerate(it["qbs"]):
            q0 = c0 + qo
            jl = (q0 + mq - 1) // 128
            if j > jl:
                continue
            nu, b_nu = nums[qb]
            p.op("pe", (lambda e, nu=nu, qo=qo, mq=mq, jl=jl: e.matmul(nu[0:mq, 0:257], pp[0:kb, qo:qo + mq], V1[0:kb, j, :], start=(j == 0), stop=(j == jl))),
                 reads=[b_pp, b_V], writes=[b_nu])
        if not it["last"]:
            return
        lanes = []
        for qb, (qo, mq) in enumerate(it["qbs"]):
            ln_ = Lane(); lanes.append(ln_)
            q0 = c0 + qo
            tl = q0 // 128
            nu, b_nu = nums[qb]
            hh, b_hh = HH[fi[0] % 4]; sq, b_sq = SQ[fi[0] % 4]; sm, b_sm = SM[fi[0] % 4]; fi[0] += 1
            ln_.op("act", (lambda e, nu=nu, sm=sm, mq=mq: e.activation(sm[0:mq, 0:1], nu[0:mq, 256:257], AF.Abs)), reads=[b_nu, b_tk], writes=[b_sm])
            ln_.op("dve", (lambda e, sm=sm, mq=mq, tl=tl: e.tensor_tensor(sm[0:mq, 0:1], sm[0:mq, 0:1], ETK[0:mq, tl:tl + 1], ALU.max)), reads=[b_sm, b_tk], writes=[b_sm])
            ln_.op("dve", (lambda e, sm=sm, mq=mq: e.reciprocal(sm[0:mq, 0:1], sm[0:mq, 0:1])), reads=[b_sm], writes=[b_sm])
            ln_.op("act", (lambda e, nu=nu, hh=hh, sm=sm, mq=mq: e.activation(hh[0:mq, :], nu[0:mq, 0:256], AF.Copy, scale=sm[0:mq, 0:1])), reads=[b_nu, b_sm], writes=[b_hh])
            ln_.op("dve", (lambda e, hh=hh, sq=sq, mq=mq: e.tensor_tensor(sq[0:mq, :], hh[0:mq, :], hh[0:mq, :], ALU.mult)), reads=[b_hh], writes=[b_sq])
            ln_.op("dve", (lambda e, sq=sq, sm=sm, mq=mq: e.reduce_sum(sm[0:mq, 1:2], sq[0:mq, :], AX.X)), reads=[b_sq, b_sm], writes=[b_sm])
            ln_.op("act", (lambda e, sm=sm, mq=mq: e.activation(sm[0:mq, 2:3], sm[0:mq, 1:2], AF.Sqrt, bias=EPS, scale=1.0 / 256)), reads=[b_sm], writes=[b_sm])
            ln_.op("dve", (lambda e, sm=sm, mq=mq: e.reciprocal(sm[0:mq, 2:3], sm[0:mq, 2:3])), reads=[b_sm], writes=[b_sm])
            ln_.op("dve", (lambda e, hh=hh, sq=sq, sm=sm, mq=mq: e.scalar_tensor_tensor(sq[0:mq, :], hh[0:mq, :], sm[0:mq, 2:3], gt[0:mq, :], ALU.mult, ALU.mult)),
                   reads=[b_hh, b_sm, b_c], writes=[b_sq])
            yo, b_yo, s_yo = YO.next()
            ln_.op("dve", (lambda e, sq=sq, yo=yo, tl=tl, mq=mq: e.tensor_tensor(yo[0:mq, :], sq[0:mq, :], OG[0:mq, tl, :], ALU.mult)), reads=[b_sq, b_O], writes=[b_yo])
            it.setdefault("fin", []).append((yo, b_yo, s_yo, q0, mq))
        interleave(p, lanes)
        for (yo, b_yo, s_yo, q0, mq) in it["fin"]:
            b_o = p.buf()
            p.dma("sp", (lambda e, yo=yo, q0=q0, mq=mq: e.dma_start(out=y_ml[q0:q0 + mq, :], in_=yo[0:mq, :])), s_yo, reads=[b_yo], writes=[b_o])
            outs.append(b_o)

    stages = [st1, st2, st3]
    for t in range(len(items) + len(stages) - 1):
        for si, fn in enumerate(stages):
            i = t - si
            if 0 <= i < len(items):
                fn(items[i])
    p.wait_all("sp", outs)
    p.emit()


def phase_hg(nc, xnd, d, KD, L, y_hg, layer0=True):
    with nc.sbuf_tensor("hgQS", [128, 2, L], BF16) as QS, \
            nc.sbuf_tensor("hgKK", [128, 2, L], BF16) as KK, \
            nc.sbuf_tensor("hgLF", [128, 2, L], F32) as LF:
        _hg_proj(nc, xnd, d, KD, L, QS, KK, LF, layer0)
        _hg_main(nc, xnd, d, KD, L, QS, KK, LF, y_hg)


def _hg_proj(nc, xnd, d, KD, L, QS, KK, LF, layer0):
    p = Prog(nc)
    wf, b_wf = load_w(p, "wf", d["w_hg_fm"], 4, KD, None)
    s_c = p.dsem("c"); b_c = p.buf()
    bfm = p.sb("bfm", [128, 4], F32); lbt = p.sb("lbt", [128, 4], F32)
    p.dma("sp", lambda e: e.dma_start(out=bfm[:], in_=d["b_hg_fm"][:, :]), s_c, writes=[b_c])
    p.dma("sp", lambda e: e.dma_start(out=lbt[:], in_=d["hg_lb"][:, :]), s_c, writes=[b_c])
    lb = p.sb("lb", [128, 2], F32); oml = p.sb("oml", [128, 2], F32); noml = p.sb("noml", [128, 2], F32); b_lb = p.buf()
    if layer0:
        p.op("dve", lambda e: e.memset(lb[:], 0.0), writes=[b_lb])
    else:
        for h in range(2):
            p.op("dve", (lambda e, h=h: e.tensor_tensor(lb[:, h:h + 1], lbt[:, 2 * h + 1:2 * h + 2], lbt[:, 2 * h:2 * h + 1], ALU.subtract)), reads=[b_c], writes=[b_lb])
        p.op("act", lambda e: e.activation(lb[:], lb[:], AF.Sigmoid), reads=[b_lb], writes=[b_lb])
        p.op("dve", lambda e: e.tensor_scalar(lb[:], lb[:], 0.999, None, ALU.min), reads=[b_lb], writes=[b_lb])
    p.op("dve", lambda e: e.tensor_scalar(oml[:], lb[:], -1.0, 1.0, ALU.mult, ALU.add), reads=[b_lb], writes=[b_lb])
    p.op("dve", lambda e: e.tensor_scalar(noml[:], oml[:], -1.0, None, ALU.mult), reads=[b_lb], writes=[b_lb])
    accs = [(p.ps(f"acc{i}", [128, 512], F32), p.buf()) for i in range(4)]
    ai = [0]

    def nacc():
        a = accs[ai[0] % 4]; ai[0] += 1
        return a
    sg = [(p.sb(f"sg{i}", [128, 256], F32), p.buf()) for i in range(2)]
    fv = [(p.sb(f"fv{i}", [128, 256], F32), p.buf()) for i in range(2)]
    b_QS, b_KK, b_LF = p.buf(), p.buf(), p.buf()
    xs = XnStream(p, xnd, KD)
    it = [0]
    for (c0, n) in chunks_of(L, xs.cs):
        xt, b_x = xs.load(c0, n)
        for j in range(4):
            h = j % 2
            a, b_a = nacc()
            for kc in range(KD):
                p.op("pe", (lambda e, a=a, j=j, kc=kc, xt=xt, n=n: e.matmul(a[:, 0:n], wf[:, j, kc, :], xt[:, kc, 0:n], start=(kc == 0), stop=(kc == KD - 1))),
                     reads=[b_wf, b_x], writes=[b_a], inc=(kc == KD - 1))
            if j < 2:
                p.op("act", (lambda e, a=a, j=j, h=h, c0=c0, n=n: e.activation(QS[:, h, c0:c0 + n], a[:, 0:n], AF.Silu, bias=bfm[:, j:j + 1], scale=1.0)),
                     reads=[b_a, b_c], writes=[b_QS])
            else:
                s_, b_s = sg[it[0] % 2]; f_, b_f = fv[it[0] % 2]; it[0] += 1
                p.op("act", (lambda e, a=a, j=j, s_=s_, n=n: e.activation(s_[:, 0:n], a[:, 0:n], AF.Sigmoid, bias=bfm[:, j:j + 1], scale=1.0)),
                     reads=[b_a, b_c], writes=[b_s])
                p.op("dve", (lambda e, s_=s_, f_=f_, h=h, n=n: e.tensor_scalar(f_[:, 0:n], s_[:, 0:n], oml[:, h:h + 1], lb[:, h:h + 1], ALU.mult, ALU.add)),
                     reads=[b_s, b_lb], writes=[b_f])
                p.op("act", (lambda e, f_=f_, h=h, c0=c0, n=n: e.activation(LF[:, h, c0:c0 + n], f_[:, 0:n], AF.Ln)), reads=[b_f], writes=[b_LF])
                p.op("dve", (lambda e, s_=s_, h=h, c0=c0, n=n: e.tensor_scalar(KK[:, h, c0:c0 + n], s_[:, 0:n], noml[:, h:h + 1], oml[:, h:h + 1], ALU.mult, ALU.add)),
                     reads=[b_s, b_lb], writes=[b_KK])
    p.wait_all("act", [b_QS, b_KK, b_LF])
    p.wait_all("dve", [b_QS, b_KK, b_LF])
    p.emit()


def _hg_main(nc, xnd, d, KD, L, QS, KK, LF, y_hg):
    p = Prog(nc)
    s_c = p.dsem("c"); b_c = p.buf()
    s_p = p.dsem("cp"); b_wt = p.buf()
    wt = p.sb("wt", [128, KD, 4, 128], BF16)
    for j in range(4):
        p.dma("pool", (lambda e, j=j: e.dma_start(out=wt[:, :, j, :], in_=d["w_hg_tm"][j])), s_p, writes=[b_wt])
    btm = p.sb("btm", [64, 512], F32); gt = p.sb("gt", [64, 256], F32)
    p.dma("sp", lambda e: e.dma_start(out=btm[:], in_=d["b_hg_tm"][0:64, :]), s_c, writes=[b_c])
    p.dma("sp", lambda e: e.dma_start(out=gt[:], in_=d["hg_g"][0:64, :]), s_c, writes=[b_c])
    onesf = p.sb("onesf", [128, 64], F32); onesb = p.sb("onesb", [128, 128], BF16); ident = p.sb("ident", [128, 128], BF16)
    m64 = p.sb("m64", [64, 64], F32)
    p.op("pool", lambda e: e.memset(onesf[:], 1.0), writes=[b_c])
    p.op("pool", lambda e: e.memset(onesb[:], 1.0), reads=[b_c], writes=[b_c])
    p.op("pool", lambda e: e.affine_select(ident[:], onesb[:], [[-1, 128]], ALU.is_equal, 0.0, base=0, channel_multiplier=1), reads=[b_c], writes=[b_c])
    p.op("pool", lambda e: e.affine_select(m64[:], onesf[0:64, :], [[1, 64]], ALU.is_ge, 0.0, base=0, channel_multiplier=-1), reads=[b_c], writes=[b_c])
    Sf = p.sb("Sf", [128, 2, 128], F32); Sb = p.sb("Sb", [128, 2, 128], BF16)
    b_Sf = [p.buf(), p.buf()]; b_Sb = [p.buf(), p.buf()]
    for h in range(2):
        p.op("dve", (lambda e, h=h: e.memset(Sf[:, h, :], 0.0)), writes=[b_Sf[h]])
        p.op("dve", (lambda e, h=h: e.memset(Sb[:, h, :], 0.0)), writes=[b_Sb[h]])
    ps_ig = (p.ps("ps_ig", [128, 512], F32), p.buf())
    ps_a = [(p.ps(f"ps_a{i}", [128, 512], F32), p.buf()) for i in range(2)]
    ps_t = [(p.ps(f"ps_t{i}", [128, 1024], BF16), p.buf()) for i in range(1)]
    ps_o = [(p.ps(f"ps_o{i}", [128, 512], F32), p.buf()) for i in range(2)]
    ps_s = [(p.ps(f"ps_s{i}", [128, 512], F32), p.buf()) for i in range(2)]
    IT = [(p.sb(f"IT{i}", [64, 256], BF16), p.buf()) for i in range(2)]
    GT = [(p.sb(f"GT{i}", [64, 256], F32), p.buf()) for i in range(2)]
    def mk(name, shape, dt, n=2):
        return [(p.sb(f"{name}{i}", shape, dt), p.buf()) for i in range(n)]
    BT = mk("BT", [128, 64], F32, 4); EX = mk("EX", [128, 4, 64], F32, 4)
    QT = mk("QT", [128, 64], BF16, 4); KT = mk("KT", [128, 64], BF16, 4); QH = mk("QH", [128, 64], BF16, 4); KH = mk("KH", [128, 64], BF16, 4)
    NB = mk("NB", [128, 2], F32, 4)
    AM = mk("AM", [64, 64], BF16, 4); KHT = mk("KHT", [64, 128], BF16, 4)
    OO = mk("OO", [64, 128], F32, 4); SQ = mk("SQ", [64, 128], F32, 4); SM = mk("SM", [64, 4], F32, 4)
    YT = Ring(p, "yt", 2, [64, 256], F32)
    b_in = p.buf()
    xs = XnStream(p, xnd, KD, nslots=2, cs=64)
    outs = []
    cks = chunks_of(L, 64)
    it = [0]
    st = {}

    def pre(ci):
        t0, m = cks[ci]
        xt, b_x = xs.load(t0, m)
        a, b_a = ps_ig
        for kc in range(KD):
            p.op("pe", (lambda e, kc=kc: e.matmul(a[0:m, 0:512], xt[:, kc, 0:m], wt[:, kc, :, :].rearrange("p j c -> p (j c)"), start=(kc == 0), stop=(kc == KD - 1))),
                 reads=[b_wt, b_x], writes=[b_a], inc=(kc == KD - 1))
        it_, b_it = IT[ci % 2]; g_, b_g = GT[ci % 2]
        p.op("dve", (lambda e: e.tensor_tensor(it_[0:m, :], a[0:m, 0:256], btm[0:m, 0:256], ALU.add)), reads=[b_a, b_c], writes=[b_it])
        p.op("dve", (lambda e: e.tensor_tensor(g_[0:m, :], a[0:m, 256:512], btm[0:m, 256:512], ALU.add)), reads=[b_a, b_c], writes=[b_g])
        p.op("act", (lambda e: e.activation(g_[0:m, :], g_[0:m, :], AF.Sigmoid)), reads=[b_g], writes=[b_g])
        ks = []
        lanes = []
        for h in range(2):
            ln_ = Lane(); lanes.append(ln_)
            k = it[0] % 4; it[0] += 1
            ks.append(k)
            bt, b_bt = BT[k]; ex, b_ex = EX[k]; b_ex1, b_ex2, b_ex3 = EXT[k]; qt, b_qt = QT[k]; kt, b_kt = KT[k]; qh, b_qh = QH[k]; kh, b_kh = KH[k]
            nb, b_nb = NB[k]
            mid = max(m // 2 - 1, 0)
            ln_.op("dve", (lambda e, bt=bt, h=h: e.tensor_tensor_scan(bt[:, 0:m], onesf[:, 0:m], LF[:, h, t0:t0 + m], 0.0, ALU.mult, ALU.add)),
                 reads=[b_in, b_c], writes=[b_bt])
            ln_.op("dve", (lambda e, bt=bt, nb=nb: e.tensor_scalar(nb[:, 0:1], bt[:, mid:mid + 1], -1.0, None, ALU.mult)), reads=[b_bt], writes=[b_nb])
            ln_.op("act", (lambda e, ex=ex, bt=bt, nb=nb: e.activation(ex[:, 0, 0:m], bt[:, 0:m], AF.Exp, bias=nb[:, 0:1], scale=1.0)), reads=[b_bt, b_nb], writes=[b_ex])
            ln_.op(HG_MUL_ENG, (lambda e, ex=ex, qt=qt, h=h: e.tensor_tensor(qt[:, 0:m], ex[:, 0, 0:m], QS[:, h, t0:t0 + m], ALU.mult)), reads=[b_ex, b_in], writes=[b_qt])
            ln_.op("act", (lambda e, ex=ex, bt=bt: e.activation(ex[:, 1, 0:m], bt[:, 0:m], AF.Exp, bias=bt[:, mid:mid + 1], scale=-1.0)), reads=[b_bt], writes=[b_ex1])
            ln_.op(HG_MUL_ENG, (lambda e, ex=ex, kt=kt, h=h: e.tensor_tensor(kt[:, 0:m], ex[:, 1, 0:m], KK[:, h, t0:t0 + m], ALU.mult)), reads=[b_ex1, b_in], writes=[b_kt])
            ln_.op("act", (lambda e, ex=ex, bt=bt: e.activation(ex[:, 2, 0:m], bt[:, 0:m], AF.Exp, bias=bt[:, m - 1:m], scale=-1.0)), reads=[b_bt], writes=[b_ex2])
            ln_.op(HG_MUL_ENG, (lambda e, ex=ex, kh=kh, h=h: e.tensor_tensor(kh[:, 0:m], ex[:, 2, 0:m], KK[:, h, t0:t0 + m], ALU.mult)), reads=[b_ex2, b_in], writes=[b_kh])
            ln_.op("act", (lambda e, ex=ex, bt=bt: e.activation(ex[:, 3, 0:m], bt[:, 0:m], AF.Exp)), reads=[b_bt], writes=[b_ex3])
            ln_.op(HG_MUL_ENG, (lambda e, ex=ex, qh=qh, h=h: e.tensor_tensor(qh[:, 0:m], ex[:, 3, 0:m], QS[:, h, t0:t0 + m], ALU.mult)), reads=[b_ex3, b_in], writes=[b_qh])
        st[ci] = ks
        return lanes

    def tail(ci):
        t0, m = cks[ci]
        lastc = (ci == len(cks) - 1)
        it_, b_it = IT[ci % 2]; g_, b_g = GT[ci % 2]
        yt, b_yt, s_yt = YT.next()
        lanes = []
        for h in range(2):
            ln_ = Lane(); lanes.append(ln_)
            k = st[ci][h]
            ex, b_ex = EX[k]; b_ex1, b_ex2, b_ex3 = EXT[k]; qt, b_qt = QT[k]; kt, b_kt = KT[k]; qh, b_qh = QH[k]; kh, b_kh = KH[k]
            am, b_am = AM[k]; kht, b_kht = KHT[k]; oo, b_oo = OO[k]; sq, b_sq = SQ[k]; sm, b_sm = SM[k]
            pa, b_pa = ps_a[h]
            ln_.op("pe", (lambda e, pa=pa, kt=kt, qt=qt: e.matmul(pa[0:m, 0:m], kt[:, 0:m], qt[:, 0:m], start=True, stop=True)), reads=[b_kt, b_qt], writes=[b_pa])
            ln_.op("dve", (lambda e, pa=pa, am=am: e.tensor_tensor(am[0:m, 0:m], pa[0:m, 0:m], m64[0:m, 0:m], ALU.mult)), reads=[b_pa, b_c], writes=[b_am])
            po, b_po = ps_o[h]
            ln_.op("pe", (lambda e, po=po, am=am, h=h: e.matmul(po[0:m, 0:128], am[0:m, 0:m], it_[0:m, h * 128:(h + 1) * 128], start=True, stop=False)),
                 reads=[b_am, b_it], writes=[b_po], inc=False)
            ln_.op("pe", (lambda e, po=po, qh=qh, h=h: e.matmul(po[0:m, 0:128], qh[:, 0:m], Sb[:, h, :], start=False, stop=True)),
                 reads=[b_qh, b_Sb[h]], writes=[b_po])
            if not lastc:
                pt, b_pt = ps_t[0]
                ln_.op("pe", (lambda e, pt=pt, kh=kh, h=h: e.transpose(pt[0:m, h * 128:(h + 1) * 128], kh[:, 0:m], ident[:])), reads=[b_kh, b_c], writes=[b_pt])
                ln_.op("act", (lambda e, pt=pt, kht=kht, h=h: e.activation(kht[0:m, :], pt[0:m, h * 128:(h + 1) * 128], AF.Copy)), reads=[b_pt], writes=[b_kht])
                pss, b_pss = ps_s[h]
                ln_.op("pe", (lambda e, pss=pss, kht=kht, h=h: e.matmul(pss[:, 0:128], kht[0:m, :], it_[0:m, h * 128:(h + 1) * 128], start=True, stop=True)),
                     reads=[b_kht, b_it], writes=[b_pss])
                ln_.op("dve", (lambda e, pss=pss, ex=ex, h=h: e.scalar_tensor_tensor(Sf[:, h, :], Sf[:, h, :], ex[:, 3, m - 1:m], pss[:, 0:128], ALU.mult, ALU.add)),
                     reads=[b_pss, b_ex3, b_Sf[h]], writes=[b_Sf[h]])
                ln_.op("act", (lambda e, h=h: e.activation(Sb[:, h, :], Sf[:, h, :], AF.Copy)), reads=[b_Sf[h]], writes=[b_Sb[h]])
            ln_.op("act", (lambda e, po=po, oo=oo: e.activation(oo[0:m, :], po[0:m, 0:128], AF.Copy)), reads=[b_po], writes=[b_oo])
            ln_.op("dve", (lambda e, oo=oo, sq=sq: e.tensor_tensor(sq[0:m, :], oo[0:m, :], oo[0:m, :], ALU.mult)), reads=[b_oo], writes=[b_sq])
            ln_.op("dve", (lambda e, sq=sq, sm=sm: e.reduce_sum(sm[0:m, 0:1], sq[0:m, :], AX.X)), reads=[b_sq], writes=[b_sm])
            ln_.op("act", (lambda e, sm=sm: e.activation(sm[0:m, 1:2], sm[0:m, 0:1], AF.Sqrt, bias=EPS, scale=1.0 / 128)), reads=[b_sm], writes=[b_sm])
            ln_.op("dve", (lambda e, sm=sm: e.reciprocal(sm[0:m, 1:2], sm[0:m, 1:2])), reads=[b_sm], writes=[b_sm])
            ln_.op("dve", (lambda e, oo=oo, sq=sq, sm=sm, h=h: e.scalar_tensor_tensor(sq[0:m, :], oo[0:m, :], sm[0:m, 1:2], gt[0:m, h * 128:(h + 1) * 128], ALU.mult, ALU.mult)),
                 reads=[b_oo, b_sm, b_c], writes=[b_sq])
            ln_.op("dve", (lambda e, sq=sq, yt=yt, h=h: e.tensor_tensor(yt[0:m, h * 128:(h + 1) * 128], sq[0:m, :], g_[0:m, h * 128:(h + 1) * 128], ALU.mult)),
                 reads=[b_sq, b_g], writes=[b_yt])
        def fin():
            b_o = p.buf()
            p.dma("sp", (lambda e: e.dma_start(out=y_hg[t0:t0 + m, :], in_=yt[0:m, :])), s_yt, reads=[b_yt], writes=[b_o])
            outs.append(b_o)
        return lanes, fin

    EXT = [(p.buf(), p.buf(), p.buf()) for _ in range(4)]
    interleave(p, pre(0))
    for ci in range(len(cks)):
        pl = pre(ci + 1) if ci + 1 < len(cks) else []
        tl, fin = tail(ci)
        interleave(p, pl + tl)
        fin()
    p.wait_all("sp", outs)
    p.emit()


def prep_A(w_in, b_in, g, l, P, KD):
    D = 128 * KD
    def cols(base, width=256):
        return slice(base + g * width, base + (g + 1) * width)
    def fm(sl):
        return tile_w(w_in[:, sl], KD, 2)
    def bfm(sl):
        return np.ascontiguousarray(b_in[sl].reshape(2, 128).T)
    def rep(v, n=128):
        return np.ascontiguousarray(np.broadcast_to(v[None, :], (n, v.shape[0])))
    d = {}
    d["w_ml_fm"] = np.concatenate([fm(cols(0)), fm(cols(1024))], 0)
    d["b_ml_fm"] = np.concatenate([bfm(cols(0)), bfm(cols(1024))], 1)
    d["w_ml_tm"] = np.concatenate([fm(cols(2048)), fm(cols(3072))], 0)
    d["b_ml_tm"] = rep(np.concatenate([b_in[cols(2048)], b_in[cols(3072)]]))
    ifc = [4096 + g, 4100 + g]
    d["w_ml_if"] = np.ascontiguousarray(w_in[:, ifc].reshape(KD, 128, 2).transpose(1, 0, 2))
    d["b_ml_if"] = np.ascontiguousarray(b_in[ifc].reshape(1, 2))
    d["ml_g"] = rep(P["ml_norm"][g])
    sb = 4104
    d["w_sb_fm"] = np.concatenate([fm(cols(sb)), fm(cols(sb + 1024))], 0)
    d["b_sb_fm"] = np.concatenate([bfm(cols(sb)), bfm(cols(sb + 1024))], 1)
    d["w_sb_tm"] = fm(cols(sb + 2048))
    d["b_sb_tm"] = rep(b_in[cols(sb + 2048)])
    d["sb_gqk"] = np.ascontiguousarray(np.stack([P["sb_q_norm"], P["sb_k_norm"]], 1))
    hg = 7176
    d["w_hg_fm"] = np.concatenate([fm(cols(hg)), fm(cols(hg + 1024))], 0)
    d["b_hg_fm"] = np.concatenate([bfm(cols(hg)), bfm(cols(hg + 1024))], 1)
    d["w_hg_tm"] = np.concatenate([fm(cols(hg + 2048)), fm(cols(hg + 3072))], 0)
    d["b_hg_tm"] = rep(np.concatenate([b_in[cols(hg + 2048)], b_in[cols(hg + 3072)]]))
    d["hg_lb"] = np.ascontiguousarray(np.stack([P["hg_lb_all"][0][cols(0)].reshape(2, 128).T, P["hg_lb_all"][1][cols(0)].reshape(2, 128).T], 1).reshape(128, 4))
    d["hg_g"] = rep(P["hg_norm"][2 * g:2 * g + 2].reshape(-1))
    rg = 11272
    d["w_rg_fm"] = np.concatenate([fm(cols(rg)), fm(cols(rg + 1024))], 0)
    d["b_rg_fm"] = np.concatenate([bfm(cols(rg)), bfm(cols(rg + 1024))], 1)
    cw = P["rg_conv_w"][:, cols(0)]
    d["rg_cw"] = np.ascontiguousarray(cw.reshape(4, 2, 128).transpose(2, 1, 0).reshape(128, 8))
    for nm, key in (("rg_cb", "rg_conv_b"), ("rg_ba", "rg_ba"), ("rg_bx", "rg_bx"), ("rg_lam", "rg_lambda")):
        d[nm] = np.ascontiguousarray(P[key][cols(0)].reshape(2, 128).T)
    for nm, key in (("rg_wa", "rg_wa"), ("rg_wx", "rg_wx")):
        blk = P[key][4 * g:4 * g + 4]
        bd = np.zeros((2, 128, 128), np.float32)
        for ct in range(2):
            for q in range(2):
                bd[ct, q * 64:(q + 1) * 64, q * 64:(q + 1) * 64] = blk[ct * 2 + q]
        d[nm] = bd
    return d


N_META = 16
MIXW = 13320
_CACHE = {}


def _build_A(KD, L, shapes, layer0):
    D = 128 * KD
    nc = bass.Bass("TRN2", target_bir_lowering=False)
    hT = nc.dram_tensor("hT", [D, L], F32, kind="ExternalInput").ap()
    gm = nc.dram_tensor("gm", [128, KD], F32, kind="ExternalInput").ap()
    d = {k: nc.dram_tensor(k, list(v), F32, kind="ExternalInput").ap() for k, v in shapes.items()}
    xnd = nc.dram_tensor("xnd", [D, L], BF16, kind="Internal").ap()
    y_ml = nc.dram_tensor("y_ml", [L, 256], F32, kind="ExternalOutput").ap()
    y_sb = nc.dram_tensor("y_sb", [256, L], F32, kind="ExternalOutput").ap()
    y_hg = nc.dram_tensor("y_hg", [L, 256], F32, kind="ExternalOutput").ap()
    y_rg = nc.dram_tensor("y_rg", [256, L], F32, kind="ExternalOutput").ap()
    phase_norm(nc, hT, xnd, gm, KD, L)
    phase_rg(nc, xnd, d, KD, L, y_rg)
    phase_sb(nc, xnd, d, KD, L, y_sb)
    phase_ml(nc, xnd, d, KD, L, y_ml)
    phase_hg(nc, xnd, d, KD, L, y_hg, layer0=layer0)
    return nc


B_CHUNKS = ((0, 528), (528, 500))
B_CORES = 8


def kernel(x, meta, norm_mix, norm_ffn, w_in, b_in, ml_norm, sb_q_norm, sb_k_norm, hg_lb, hg_norm,
           rg_conv_w, rg_conv_b, rg_wa, rg_ba, rg_wx, rg_bx, rg_lambda, w_up, w_out,
           w_ffn_gate, w_ffn_up, w_ffn_down):
    f32 = np.float32
    x = np.asarray(x, f32)
    Bn, S, D = x.shape
    KD = D // 128
    L = S + N_META
    NF = w_ffn_gate.shape[2] // 128
    depth = w_in.shape[0]
    h = np.concatenate([np.broadcast_to(np.asarray(meta, f32)[None], (Bn, N_META, D)), x], axis=1)
    TT = sum(n for _, n in B_CHUNKS)
    SPB = B_CORES // Bn
    assert SPB * TT == L
    for l in range(depth):
        P = {"ml_norm": np.asarray(ml_norm[l], f32), "sb_q_norm": np.asarray(sb_q_norm[l], f32), "sb_k_norm": np.asarray(sb_k_norm[l], f32),
             "hg_lb_all": np.asarray(hg_lb, f32), "hg_norm": np.asarray(hg_norm[l], f32),
             "rg_conv_w": np.asarray(rg_conv_w[l], f32), "rg_conv_b": np.asarray(rg_conv_b[l], f32),
             "rg_wa": np.asarray(rg_wa[l], f32), "rg_ba": np.asarray(rg_ba[l], f32), "rg_wx": np.asarray(rg_wx[l], f32),
             "rg_bx": np.asarray(rg_bx[l], f32), "rg_lambda": np.asarray(rg_lambda[l], f32)}
        wl = np.asarray(w_in[l], f32)
        bl = np.asarray(b_in[l], f32)
        gmix = np.ascontiguousarray(np.asarray(norm_mix[l], f32).reshape(KD, 128).T)
        hTs = [np.ascontiguousarray(h[b].T) for b in range(Bn)]
        maps = []
        for c in range(8):
            b, g = c // 4, c % 4
            dd = prep_A(wl, bl, g, l, P, KD)
            maps.append(dict(dd, hT=hTs[b], gm=gmix))
        key = ("A", l == 0)
        if key not in _CACHE:
            _CACHE[key] = _build_A(KD, L, {k: v.shape for k, v in maps[0].items() if k not in ("hT", "gm")}, l == 0)
        res = run_bass_kernel_spmd(_CACHE[key], maps, core_ids=list(range(8))).results
        del maps
        yT = np.empty((Bn, D, L), f32)
        for c in range(8):
            b, g = c // 4, c % 4
            r = res[c]
            yT[b, 0 * 1024 + g * 256:0 * 1024 + (g + 1) * 256] = np.asarray(r["y_ml"]).T
            yT[b, 1 * 1024 + g * 256:1 * 1024 + (g + 1) * 256] = np.asarray(r["y_sb"])
            yT[b, 2 * 1024 + g * 256:2 * 1024 + (g + 1) * 256] = np.asarray(r["y_hg"]).T
            yT[b, 3 * 1024 + g * 256:3 * 1024 + (g + 1) * 256] = np.asarray(r["y_rg"])
        del res
        W = prep_B_weights(wl[:, MIXW:], bl[MIXW:], np.asarray(w_up[l], f32), np.asarray(w_out[l], f32),
                           np.asarray(norm_mix[l], f32), np.asarray(norm_ffn[l], f32),
                           np.asarray(w_ffn_gate[l], f32), np.asarray(w_ffn_up[l], f32), np.asarray(w_ffn_down[l], f32), KD, NF)
        if "B" not in _CACHE:
            _CACHE["B"] = build_B(KD, NF, B_CHUNKS)
        maps = []
        for c in range(B_CORES):
            b, s = c // SPB, c % SPB
            maps.append(dict(W, hT=np.ascontiguousarray(hTs[b][:, s * TT:(s + 1) * TT]),
                             yT=np.ascontiguousarray(yT[b][:, s * TT:(s + 1) * TT])))
        res = run_bass_kernel_spmd(_CACHE["B"], maps, core_ids=list(range(B_CORES))).results
        del maps, W
        for c in range(B_CORES):
            b, s = c // SPB, c % SPB
            h[b, s * TT:(s + 1) * TT] = np.asarray(res[c]["hout"]).T
        del res, yT, hTs
    return np.ascontiguousarray(h[:, N_META:]).astype(f32)
```

```python
import time
import numpy as np
from contextlib import ExitStack
import concourse.bass as bass
import concourse.mybir as mybir
from concourse.bass_utils import run_bass_kernel_spmd

F32 = mybir.dt.float32
BF16 = mybir.dt.bfloat16
AF = mybir.ActivationFunctionType
ALU = mybir.AluOpType
AX = mybir.AxisListType

ENGS = ("pe", "act", "dve", "pool", "sp")
SAME_ENGINE_SYNC = True


class Buf:
    __slots__ = ("name", "w", "r")

    def __init__(self, name):
        self.name = name
        self.w = None
        self.r = []


class Prog:
    _count = 0

    def __init__(self, nc, same_engine_sync=SAME_ENGINE_SYNC):
        Prog._count += 1
        self.pfx = f"P{Prog._count}_"
        self.nc = nc
        self.es = ExitStack()
        self.ops = {e: [] for e in ENGS}
        self.ninc = {e: 0 for e in ENGS}
        self.same = same_engine_sync
        self.sems = []
        self.esem = {e: self._sem(self.pfx + "s_" + e) for e in ENGS}
        self.dsems = []
        self.nbuf = 0
        self.gseq = 0

    def _sem(self, name):
        h = self.nc.alloc_semaphore(name=name)
        self.sems.append(h)
        return h

    def sb(self, name, shape, dt):
        return self.es.enter_context(self.nc.sbuf_tensor(self.pfx + name, list(shape), dt))

    def ps(self, name, shape, dt=F32):
        return self.es.enter_context(self.nc.psum_tensor(self.pfx + name, list(shape), dt))

    def buf(self, name=None):
        self.nbuf += 1
        return Buf(name or f"b{self.nbuf}")

    def dsem(self, name):
        h = self._sem(self.pfx + name)
        self.dsems.append([h, 0])
        return len(self.dsems) - 1

    def _deps(self, eng, reads, writes):
        deps = []
        for b in reads:
            if b.w is not None:
                deps.append(b.w)
        for b in writes:
            if b.w is not None:
                deps.append(b.w)
            deps.extend(b.r)
        return deps

    def op(self, eng, fn, reads=(), writes=(), inc=True):
        deps = self._deps(eng, reads, writes)
        idx = len(self.ops[eng])
        self.gseq += 1
        self.ops[eng].append({"fn": fn, "deps": deps, "inc": inc, "dma": None, "g": self.gseq})
        tok = ("e", eng, idx)
        for b in reads:
            b.r.append(tok)
        for b in writes:
            b.w = tok
            b.r = []
        return tok

    def dma(self, eng, fn, sem, reads=(), writes=()):
        deps = self._deps(eng, reads, writes)
        if self.dsems[sem][1] > 0:
            deps.append(("d", sem, self.dsems[sem][1]))
        self.dsems[sem][1] += 16
        val = self.dsems[sem][1]
        self.gseq += 1
        self.ops[eng].append({"fn": fn, "deps": deps, "inc": False, "dma": sem, "g": self.gseq})
        tok = ("d", sem, val)
        for b in reads:
            b.r.append(tok)
        for b in writes:
            b.w = tok
            b.r = []
        return tok

    def wait_all(self, eng, bufs):
        deps = self._deps(eng, (), bufs)
        self.gseq += 1
        self.ops[eng].append({"fn": None, "deps": deps, "inc": False, "dma": None, "g": self.gseq})

    def emit(self):
        nc = self.nc
        incval = {}
        for e in ENGS:
            vals = [0] * len(self.ops[e])
            c = 0
            for i, o in enumerate(self.ops[e]):
                if o["inc"]:
                    c += 1
                vals[i] = c
            need = [None] * len(self.ops[e])
            nxt = None
            for i in range(len(self.ops[e]) - 1, -1, -1):
                if self.ops[e][i]["inc"]:
                    nxt = (vals[i], i)
                need[i] = nxt
            incval[e] = need

        def run(e, h):
            waited = {}
            for i, o in enumerate(self.ops[e]):
                for d in o["deps"]:
                    if d[0] == "e":
                        _, e2, i2 = d
                        if e2 == e:
                            if not self.same:
                                continue
                        assert incval[e2][i2] is not None, (e, i, d)
                        v, j2 = incval[e2][i2]
                        if e2 == e and j2 >= i:
                            continue
                        assert self.ops[e2][j2]["g"] < o["g"], ("deadlock: wait on future op", e, i, e2, i2, j2)
                        key = ("e", e2)
                        sem = self.esem[e2]
                    else:
                        _, s, v = d
                        key = ("d", s)
                        sem = self.dsems[s][0]
                    if waited.get(key, 0) >= v:
                        continue
                    waited[key] = v
                    h.wait_ge(sem, v)
                if o["fn"] is None:
                    continue
                ins = o["fn"](h)
                if o["dma"] is not None:
                    ins.then_inc(self.dsems[o["dma"]][0], 16)
                elif o["inc"]:
                    ins.then_inc(self.esem[e], 1)

        with nc.Block() as block:
            @block.tensor
            def _(h):
                run("pe", h)

            @block.scalar
            def _(h):
                run("act", h)

            @block.vector
            def _(h):
                run("dve", h)

            @block.gpsimd
            def _(h):
                run("pool", h)

            @block.sync
            def _(h):
                run("sp", h)
        self.nc.clear_and_free_semaphores(self.sems)
        self.nc.all_engine_barrier()
        self.es.close()


EPS = 1e-6


class Ring:
    def __init__(self, p, name, n, shape, dt):
        self.t = [p.sb(f"{name}{i}", shape, dt) for i in range(n)]
        self.b = [p.buf() for _ in range(n)]
        self.s = [p.dsem(f"{name}s{i}") for i in range(n)]
        self.i = 0

    def next(self):
        i = self.i % len(self.t)
        self.i += 1
        return self.t[i], self.b[i], self.s[i]


def segs_of(n):
    out = []
    o = 0
    while o < n:
        m = min(512, n - o)
        out.append((o, m))
        o += m
    return out


def build_B(KD=32, NF=86, chunks=((0, 528), (528, 512)), stop=99):
    D = 128 * KD
    TT = sum(n for _, n in chunks)
    NH = NF // 2
    CB = KD // 4
    nc = bass.Bass("TRN2", target_bir_lowering=False)
    hT = nc.dram_tensor("hT", [D, TT], F32, kind="ExternalInput").ap()
    yT = nc.dram_tensor("yT", [D, TT], F32, kind="ExternalInput").ap()
    wg = nc.dram_tensor("wg", [KD * 4, 128, KD, 128], F32, kind="ExternalInput").ap()
    bg = nc.dram_tensor("bg", [128, KD * 4], F32, kind="ExternalInput").ap()
    wu = nc.dram_tensor("wu", [KD * 4, 128, CB, 128], F32, kind="ExternalInput").ap()
    wo = nc.dram_tensor("wo", [KD, 128, KD, 128], F32, kind="ExternalInput").ap()
    gm = nc.dram_tensor("gm", [128, KD], F32, kind="ExternalInput").ap()
    gf = nc.dram_tensor("gf", [128, KD], F32, kind="ExternalInput").ap()
    wfg = nc.dram_tensor("wfg", [NF, 128, KD, 128], F32, kind="ExternalInput").ap()
    wfu = nc.dram_tensor("wfu", [NF, 128, KD, 128], F32, kind="ExternalInput").ap()
    wfd = nc.dram_tensor("wfd", [KD, 128, NF, 128], F32, kind="ExternalInput").ap()
    hout = nc.dram_tensor("hout", [D, TT], F32, kind="ExternalOutput").ap()

    p = Prog(nc)
    NMAX = max(n for _, n in chunks)
    WK = max(KD, NH)
    xn = p.sb("xn", [128, KD, NMAX], BF16)
    R = p.sb("R", [128, 2 * KD, NMAX], BF16)
    yt = R[:, 0:KD, :]
    mg = R[:, KD:2 * KD, :]
    at = R[:, 0:NH, :]
    assert NH <= 2 * KD
    wring = Ring(p, "w", 4, [128, WK, 128], BF16)
    uring = Ring(p, "wu", 3, [128, CB, 128], BF16)
    hring = Ring(p, "hs", 4, [128, NMAX], F32)
    oring = Ring(p, "ho", 3, [128, NMAX], F32)
    sqring = [(p.sb(f"sq{i}", [128, NMAX], BF16), p.buf()) for i in range(2)]
    sgring = [(p.sb(f"sg{i}", [128, NMAX], F32), p.buf()) for i in range(3)]
    tmring = [(p.sb(f"tm{i}", [128, NMAX], F32), p.buf()) for i in range(2)]
    macc = p.sb("macc", [128, NMAX], F32); b_macc = p.buf()
    rstd = p.sb("rstd", [128, NMAX], F32); b_rstd = p.buf()
    ones = p.sb("ones", [128, 128], BF16); b_ones = p.buf()
    gmt = p.sb("gmt", [128, KD], F32); gft = p.sb("gft", [128, KD], F32); b_g = p.buf()
    bgt = p.sb("bgt", [128, KD * 4], F32)
    accs = [(p.ps(f"acc{i}", [128, 1024], F32), p.buf()) for i in range(4)]
    cnt = {"acc": 0, "sq": 0, "sg": 0, "tm": 0}

    def nxt(lst, key):
        i = cnt[key] % len(lst)
        cnt[key] += 1
        return lst[i]

    s_c = p.dsem("const")
    p.op("dve", lambda e: e.memset(ones[:], 1.0), writes=[b_ones])
    p.dma("sp", lambda e: e.dma_start(out=gmt[:], in_=gm[:, :]), s_c, writes=[b_g])
    p.dma("sp", lambda e: e.dma_start(out=gft[:], in_=gf[:, :]), s_c, writes=[b_g])
    p.dma("sp", lambda e: e.dma_start(out=bgt[:], in_=bg[:, :]), s_c, writes=[b_g])

    b_xn = [p.buf() for _ in range(KD)]
    b_yt = p.buf()
    b_mg = [p.buf() for _ in range(KD)]
    b_at = [p.buf() for _ in range(NH)]
    out_toks = []

    def mm_group(acc, lhs_list, rhs_fn, n, reads):
        a, b_a = acc
        sg = segs_of(n)
        L = len(lhs_list)
        for i, lt in enumerate(lhs_list):
            for si, (o, m) in enumerate(sg):
                last = (i == L - 1) and (si == len(sg) - 1)
                p.op("pe", (lambda e, lt=lt, i=i, o=o, m=m: e.matmul(
                    a[:, o:o + m], lt, rhs_fn(i, o, m), start=(i == 0), stop=(i == L - 1))),
                    reads=reads, writes=[b_a], inc=last)

    def rmsnorm(src, src_toks, c0, n, gt):
        acc = nxt(accs, "acc")
        a, b_a = acc
        sg = segs_of(n)
        for kc in range(KD):
            ht, b_h, s_h = hring.next()
            p.dma("sp", (lambda e, ht=ht, kc=kc: e.dma_start(out=ht[:, 0:n], in_=src[kc * 128:(kc + 1) * 128, c0:c0 + n])),
                  s_h, reads=[src_toks[kc]] if src_toks else [], writes=[b_h])
            sq, b_sq = nxt(sqring, "sq")
            p.op("act", (lambda e, sq=sq, ht=ht: e.activation(sq[:, 0:n], ht[:, 0:n], AF.Square)),
                 reads=[b_h], writes=[b_sq])
            for si, (o, m) in enumerate(sg):
                last = (si == len(sg) - 1)
                p.op("pe", (lambda e, sq=sq, o=o, m=m, kc=kc: e.matmul(
                    a[:, o:o + m], ones[:], sq[:, o:o + m], start=(kc == 0), stop=(kc == KD - 1))),
                    reads=[b_sq, b_ones], writes=[b_a], inc=last)
        p.op("act", lambda e: e.activation(rstd[:, 0:n], a[:, 0:n], AF.Sqrt, bias=EPS, scale=1.0 / D),
             reads=[b_a], writes=[b_rstd])
        p.op("dve", lambda e: e.reciprocal(rstd[:, 0:n], rstd[:, 0:n]),
             reads=[b_rstd], writes=[b_rstd])
        for kc in range(KD):
            ht, b_h, s_h = hring.next()
            p.dma("sp", (lambda e, ht=ht, kc=kc: e.dma_start(out=ht[:, 0:n], in_=src[kc * 128:(kc + 1) * 128, c0:c0 + n])),
                  s_h, reads=[src_toks[kc]] if src_toks else [], writes=[b_h])
            p.op("dve", (lambda e, ht=ht, kc=kc: e.scalar_tensor_tensor(
                xn[:, kc, 0:n], ht[:, 0:n], gt[:, kc:kc + 1], rstd[:, 0:n], ALU.mult, ALU.mult)),
                reads=[b_h, b_rstd, b_g], writes=[b_xn[kc]])

    def do_chunk(c0, n):
        nonlocal out_toks
        p.wait_all("pool", b_at + [b_yt])
        p.wait_all("dve", b_at + b_mg)
        rmsnorm(hT, None, c0, n, gmt)
        if stop <= 1:
            return
        s_y = p.dsem(f"y{c0}")
        for k in range(4):
            p.dma("pool", (lambda e, k=k: e.dma_start(
                out=yt[:, k * CB:(k + 1) * CB, 0:n],
                in_=yT[k * CB * 128:(k + 1) * CB * 128, c0:c0 + n].rearrange("(c p) t -> p c t", p=128))),
                s_y, writes=[b_yt])
        for m in range(KD):
            for k in range(4):
                u = m * 4 + k
                wt, b_w, s_w = wring.next()
                p.dma("pool", (lambda e, wt=wt, u=u: e.dma_start(out=wt[:, 0:KD, :], in_=wg[u])), s_w, writes=[b_w])
                ut, b_u, s_u = uring.next()
                p.dma("pool", (lambda e, ut=ut, u=u: e.dma_start(out=ut[:, :, :], in_=wu[u])), s_u, writes=[b_u])
                ga = nxt(accs, "acc")
                mm_group(ga, [wt[:, kc, :] for kc in range(KD)], lambda i, o, mm_: xn[:, i, o:o + mm_], n,
                         reads=[b_w] + b_xn)
                sg, b_sg = nxt(sgring, "sg")
                p.op("act", (lambda e, sg=sg, ga=ga, u=u: e.activation(
                    sg[:, 0:n], ga[0][:, 0:n], AF.Sigmoid, bias=bgt[:, u:u + 1], scale=1.0)),
                    reads=[ga[1], b_g], writes=[b_sg])
                ua = nxt(accs, "acc")
                mm_group(ua, [ut[:, cc, :] for cc in range(CB)],
                         lambda i, o, mm_, k=k: yt[:, k * CB + i, o:o + mm_], n, reads=[b_u, b_yt])
                if k == 0:
                    p.op("dve", (lambda e, sg=sg, ua=ua: e.tensor_tensor(macc[:, 0:n], ua[0][:, 0:n], sg[:, 0:n], ALU.mult)),
                         reads=[ua[1], b_sg], writes=[b_macc])
                else:
                    tm, b_tm = nxt(tmring, "tm")
                    p.op("dve", (lambda e, sg=sg, ua=ua, tm=tm: e.tensor_tensor(tm[:, 0:n], ua[0][:, 0:n], sg[:, 0:n], ALU.mult)),
                         reads=[ua[1], b_sg], writes=[b_tm])
                    if k < 3:
                        p.op("dve", (lambda e, tm=tm: e.tensor_tensor(macc[:, 0:n], macc[:, 0:n], tm[:, 0:n], ALU.add)),
                             reads=[b_tm, b_macc], writes=[b_macc])
                    else:
                        p.op("dve", (lambda e, tm=tm, m=m: e.tensor_tensor(mg[:, m, 0:n], macc[:, 0:n], tm[:, 0:n], ALU.add)),
                             reads=[b_tm, b_macc], writes=[b_mg[m]])
        if stop <= 2:
            return
        b_ho = [p.buf() for _ in range(KD)]
        for nt in range(KD):
            wt, b_w, s_w = wring.next()
            p.dma("pool", (lambda e, wt=wt, nt=nt: e.dma_start(out=wt[:, 0:KD, :], in_=wo[nt])), s_w, writes=[b_w])
            a = nxt(accs, "acc")
            mm_group(a, [wt[:, mc, :] for mc in range(KD)], lambda i, o, mm_: mg[:, i, o:o + mm_], n,
                     reads=[b_w] + b_mg)
            ht, b_h, s_h = hring.next()
            p.dma("sp", (lambda e, ht=ht, nt=nt: e.dma_start(out=ht[:, 0:n], in_=hT[nt * 128:(nt + 1) * 128, c0:c0 + n])),
                  s_h, writes=[b_h])
            ot, b_o, s_o = oring.next()
            p.op("dve", (lambda e, ot=ot, ht=ht, a=a: e.tensor_tensor(ot[:, 0:n], a[0][:, 0:n], ht[:, 0:n], ALU.add)),
                 reads=[a[1], b_h], writes=[b_o])
            p.dma("sp", (lambda e, ot=ot, nt=nt: e.dma_start(out=hout[nt * 128:(nt + 1) * 128, c0:c0 + n], in_=ot[:, 0:n])),
                  s_o, reads=[b_o], writes=[b_ho[nt]])
        if stop <= 3:
            out_toks += b_ho
            return
        rmsnorm(hout, b_ho, c0, n, gft)
        if stop <= 4:
            out_toks += b_ho
            return
        for half in range(2):
            if half == 0:
                p.wait_all("dve", [b_yt] + b_mg)
            for fi in range(NH):
                f = half * NH + fi
                wt, b_w, s_w = wring.next()
                p.dma("pool", (lambda e, wt=wt, f=f: e.dma_start(out=wt[:, 0:KD, :], in_=wfg[f])), s_w, writes=[b_w])
                wt2, b_w2, s_w2 = wring.next()
                p.dma("pool", (lambda e, wt2=wt2, f=f: e.dma_start(out=wt2[:, 0:KD, :], in_=wfu[f])), s_w2, writes=[b_w2])
                ga = nxt(accs, "acc")
                mm_group(ga, [wt[:, kc, :] for kc in range(KD)], lambda i, o, mm_: xn[:, i, o:o + mm_], n,
                         reads=[b_w] + b_xn)
                sg, b_sg = nxt(sgring, "sg")
                p.op("act", (lambda e, sg=sg, ga=ga: e.activation(sg[:, 0:n], ga[0][:, 0:n], AF.Silu)),
                     reads=[ga[1]], writes=[b_sg])
                ua = nxt(accs, "acc")
                mm_group(ua, [wt2[:, kc, :] for kc in range(KD)], lambda i, o, mm_: xn[:, i, o:o + mm_], n,
                         reads=[b_w2] + b_xn)
                p.op("dve", (lambda e, sg=sg, ua=ua, fi=fi: e.tensor_tensor(at[:, fi, 0:n], ua[0][:, 0:n], sg[:, 0:n], ALU.mult)),
                     reads=[ua[1], b_sg], writes=[b_at[fi]])
            for nt in range(KD):
                wt, b_w, s_w = wring.next()
                p.dma("pool", (lambda e, wt=wt, nt=nt, half=half: e.dma_start(
                    out=wt[:, 0:NH, :], in_=wfd[nt][:, half * NH:(half + 1) * NH, :])), s_w, writes=[b_w])
                a = nxt(accs, "acc")
                mm_group(a, [wt[:, fi, :] for fi in range(NH)], lambda i, o, mm_: at[:, i, o:o + mm_], n,
                         reads=[b_w] + b_at)
                ht, b_h, s_h = hring.next()
                p.dma("sp", (lambda e, ht=ht, nt=nt: e.dma_start(out=ht[:, 0:n], in_=hout[nt * 128:(nt + 1) * 128, c0:c0 + n])),
                      s_h, reads=[b_ho[nt]], writes=[b_h])
                ot, b_o, s_o = oring.next()
                p.op("dve", (lambda e, ot=ot, ht=ht, a=a: e.tensor_tensor(ot[:, 0:n], a[0][:, 0:n], ht[:, 0:n], ALU.add)),
                     reads=[a[1], b_h], writes=[b_o])
                p.dma("sp", (lambda e, ot=ot, nt=nt: e.dma_start(out=hout[nt * 128:(nt + 1) * 128, c0:c0 + n], in_=ot[:, 0:n])),
                      s_o, reads=[b_o], writes=[b_ho[nt]])
        out_toks += b_ho

    for (c0_, n_) in chunks:
        do_chunk(c0_, n_)
    p.wait_all("sp", out_toks)
    p.emit()
    return nc


def tile_w(w, nk, nm):
    return np.ascontiguousarray(w.reshape(nk, 128, nm, 128).transpose(2, 1, 0, 3))


def prep_B_weights(wgate, bgate, wup, wout, gmix, gffn, wfg, wfu, wfd, KD, NF):
    D = 128 * KD
    CB = KD // 4
    g = wgate.reshape(KD, 128, 4, KD, 128)
    wg = np.ascontiguousarray(g.transpose(3, 2, 1, 0, 4)).reshape(KD * 4, 128, KD, 128)
    bg = np.ascontiguousarray(bgate.reshape(4, KD, 128).transpose(2, 1, 0)).reshape(128, KD * 4)
    u = wup.reshape(4, CB, 128, KD, 128)
    wu = np.ascontiguousarray(u.transpose(3, 0, 2, 1, 4)).reshape(KD * 4, 128, CB, 128)
    return {
        "wg": wg, "bg": bg, "wu": wu, "wo": tile_w(wout, KD, KD),
        "gm": np.ascontiguousarray(gmix.reshape(KD, 128).T), "gf": np.ascontiguousarray(gffn.reshape(KD, 128).T),
        "wfg": tile_w(wfg, KD, NF), "wfu": tile_w(wfu, KD, NF), "wfd": tile_w(wfd, NF, KD),
    }


def chunks_of(L, c):
    return [(o, min(c, L - o)) for o in range(0, L, c)]


HG_MUL_ENG = "pool"


class Lane:
    def __init__(self):
        self.ops = []

    def op(self, *a, **k):
        self.ops.append((a, k))


def interleave(p, lanes):
    n = max(len(l.ops) for l in lanes)
    for i in range(n):
        for l in lanes:
            if i < len(l.ops):
                a, k = l.ops[i]
                p.op(*a, **k)


def phase_norm(nc, hT, xnd, gm, KD, L):
    D = 128 * KD
    p = Prog(nc)
    NP = 4 if KD % 4 == 0 else 1
    KP = KD // NP
    HR = [p.sb(f"hres{i}", [128, KD, 512], F32) for i in range(2)]
    b_HR = [[p.buf() for _ in range(NP)] for _ in range(2)]
    s_HR = [[p.dsem(f"hr{i}_{j}") for j in range(NP)] for i in range(2)]
    XO = [p.sb(f"xo{i}", [128, KD, 512], BF16) for i in range(2)]
    b_XO = [[p.buf() for _ in range(NP)] for _ in range(2)]
    s_XO = [[p.dsem(f"xo{i}_{j}") for j in range(NP)] for i in range(2)]
    sqr = [(p.sb(f"sq{i}", [128, 512], BF16), p.buf()) for i in range(3)]
    rstd = [(p.sb(f"rstd{i}", [128, 512], F32), p.buf()) for i in range(2)]
    ones = p.sb("ones", [128, 128], BF16); b_c = p.buf()
    gmt = p.sb("gmt", [128, KD], F32)
    accs = [(p.ps(f"acc{i}", [128, 512], F32), p.buf()) for i in range(2)]
    s_c = p.dsem("c")
    p.op("dve", lambda e: e.memset(ones[:], 1.0), writes=[b_c])
    p.dma("sp", lambda e: e.dma_start(out=gmt[:], in_=gm[:, :]), s_c, writes=[b_c])
    outs = []

    def chunk(ci, c0, n):
        sl = ci % 2
        hr = HR[sl]; xo = XO[sl]
        acc, b_acc = accs[sl]
        rs, b_rs = rstd[sl]
        for j in range(NP):
            q = "sp" if j % 2 == 0 else "act"
            p.dma(q, (lambda e, j=j: e.dma_start(out=hr[:, j * KP:(j + 1) * KP, 0:n],
                                                 in_=hT[j * KP * 128:(j + 1) * KP * 128, c0:c0 + n].rearrange("(k q) t -> q k t", q=128))),
                  s_HR[sl][j], writes=[b_HR[sl][j]])
        for kc in range(KD):
            sq, b_sq = sqr[kc % 3]
            p.op("act", (lambda e, sq=sq, kc=kc: e.activation(sq[:, 0:n], hr[:, kc, 0:n], AF.Square)), reads=[b_HR[sl][kc // KP]], writes=[b_sq])
            p.op("pe", (lambda e, sq=sq, kc=kc: e.matmul(acc[:, 0:n], ones[:], sq[:, 0:n], start=(kc == 0), stop=(kc == KD - 1))),
                 reads=[b_sq, b_c], writes=[b_acc])
        p.op("act", lambda e: e.activation(rs[:, 0:n], acc[:, 0:n], AF.Sqrt, bias=EPS, scale=1.0 / D), reads=[b_acc], writes=[b_rs])
        p.op("dve", lambda e: e.reciprocal(rs[:, 0:n], rs[:, 0:n]), reads=[b_rs], writes=[b_rs])
        for kc in range(KD):
            p.op("dve", (lambda e, kc=kc: e.scalar_tensor_tensor(xo[:, kc, 0:n], hr[:, kc, 0:n], gmt[:, kc:kc + 1], rs[:, 0:n], ALU.mult, ALU.mult)),
                 reads=[b_HR[sl][kc // KP], b_rs, b_c], writes=[b_XO[sl][kc // KP]])
        for j in range(NP):
            b_x = p.buf()
            p.dma("sp", (lambda e, j=j: e.dma_start(out=xnd[j * KP * 128:(j + 1) * KP * 128, c0:c0 + n].rearrange("(k q) t -> q k t", q=128),
                                                     in_=xo[:, j * KP:(j + 1) * KP, 0:n])),
                  s_XO[sl][j], reads=[b_XO[sl][j]], writes=[b_x])
            outs.append(b_x)

    for ci, (c0, n) in enumerate(chunks_of(L, 512)):
        chunk(ci, c0, n)
    p.wait_all("sp", outs)
    p.emit()


class XnStream:
    def __init__(self, p, xnd, KD, nslots=2, cs=256):
        self.p, self.xnd, self.KD, self.cs = p, xnd, KD, cs
        self.ring = Ring(p, "xn", nslots, [128, KD, cs], BF16)

    def load(self, c0, n):
        p, xnd, KD = self.p, self.xnd, self.KD
        xt, b_x, s_x = self.ring.next()
        p.dma("sp", lambda e: e.dma_start(out=xt[:, :, 0:n], in_=xnd[:, c0:c0 + n].rearrange("(k q) t -> q k t", q=128)), s_x, writes=[b_x])
        return xt, b_x


def load_w(p, name, wd, ntiles, KD, sem):
    wt = p.sb(name, [128, ntiles, KD, 128], BF16)
    b = p.buf()
    sem = p.dsem("w_" + name)
    for j in range(ntiles):
        p.dma("pool", (lambda e, j=j: e.dma_start(out=wt[:, j, :, :], in_=wd[j])), sem, writes=[b])
    return wt, b


def phase_rg(nc, xnd, d, KD, L, y_rg):
    for CT in range(2):
        _phase_rg_ct(nc, xnd, d, KD, L, y_rg, CT)


def _phase_rg_ct(nc, xnd, d, KD, L, y_rg, CT):
    p = Prog(nc)
    s_c = p.dsem("c")
    w, b_w = load_w(p, "w", d["w_rg_fm"], 4, KD, s_c)
    b_c = p.buf()
    small = {}
    for nm, shp in (("b_rg_fm", [128, 4]), ("rg_cw", [128, 8]), ("rg_cb", [128, 2]), ("rg_ba", [128, 2]),
                    ("rg_bx", [128, 2]), ("rg_lam", [128, 2])):
        t = p.sb("s_" + nm, shp, F32)
        p.dma("sp", (lambda e, t=t, nm=nm: e.dma_start(out=t[:], in_=d[nm][:, :])), s_c, writes=[b_c])
        small[nm] = t
    wab = p.sb("wab", [128, 2, 128], BF16); wxb = p.sb("wxb", [128, 2, 128], BF16)
    s_p = p.dsem("cp"); b_cp = p.buf()
    for ct in (CT,):
        p.dma("pool", (lambda e, ct=ct: e.dma_start(out=wab[:, ct, :], in_=d["rg_wa"][ct])), s_p, writes=[b_cp])
        p.dma("pool", (lambda e, ct=ct: e.dma_start(out=wxb[:, ct, :], in_=d["rg_wx"][ct])), s_p, writes=[b_cp])
    XB = p.sb("XB", [128, 1, L + 3], F32)
    YB = p.sb("YB", [128, 1, L], F32)
    XC = p.sb("XC", [128, 1, L], F32)
    XCr = [(p.sb(f"XCr{i}", [128, 512], BF16), p.buf()) for i in range(2)]
    RR = p.sb("RR", [128, 1, L], F32)
    IG = p.sb("IG", [128, 1, L], F32)
    c8 = p.sb("c8", [128, 2], F32)
    b_XB = [p.buf(), p.buf()]; b_YB = [p.buf(), p.buf()]; b_XC = [p.buf(), p.buf()]
    b_RR = [p.buf(), p.buf()]; b_IG = [p.buf(), p.buf()]; b_c8 = p.buf()
    accs = [(p.ps(f"acc{i}", [128, 512], F32), p.buf()) for i in range(4)]
    ai = [0]

    def nacc():
        a = accs[ai[0] % 4]; ai[0] += 1
        return a
    bfm = small["b_rg_fm"]
    p.op("act", lambda e: e.activation(c8[:], small["rg_lam"][:], AF.Exp, scale=-1.0), reads=[b_c], writes=[b_c8])
    p.op("act", lambda e: e.activation(c8[:], c8[:], AF.Ln, bias=1.0, scale=1.0), reads=[b_c8], writes=[b_c8])
    p.op("dve", lambda e: e.tensor_scalar(c8[:], c8[:], -8.0, None, ALU.mult), reads=[b_c8], writes=[b_c8])
    for ct in (CT,):
        p.op("dve", (lambda e, ct=ct: e.memset(XB[:, 0, 0:3], 0.0)), writes=[b_XB[ct]])
    xs = XnStream(p, xnd, KD)
    for (c0, n) in chunks_of(L, xs.cs):
        xt, b_x = xs.load(c0, n)
        for j in (CT, 2 + CT):
            a, b_a = nacc()
            for kc in range(KD):
                p.op("pe", (lambda e, a=a, j=j, kc=kc, xt=xt, n=n: e.matmul(a[:, 0:n], w[:, j, kc, :], xt[:, kc, 0:n], start=(kc == 0), stop=(kc == KD - 1))),
                     reads=[b_w, b_x], writes=[b_a], inc=(kc == KD - 1))
            ct = j % 2
            if j < 2:
                p.op("act", (lambda e, a=a, j=j, ct=ct, c0=c0, n=n: e.activation(XB[:, 0, 3 + c0:3 + c0 + n], a[:, 0:n], AF.Identity, bias=bfm[:, j:j + 1], scale=1.0)),
                     reads=[b_a, b_c], writes=[b_XB[ct]])
            else:
                p.op("act", (lambda e, a=a, j=j, ct=ct, c0=c0, n=n: e.activation(YB[:, 0, c0:c0 + n], a[:, 0:n], AF.Identity, bias=bfm[:, j:j + 1], scale=1.0)),
                     reads=[b_a, b_c], writes=[b_YB[ct]])
    cw = small["rg_cw"]
    for ct in (CT,):
        p.op("dve", (lambda e, ct=ct: e.tensor_scalar(XC[:, 0, :], XB[:, 0, 0:L], cw[:, ct * 4:ct * 4 + 1], small["rg_cb"][:, ct:ct + 1], ALU.mult, ALU.add)),
             reads=[b_XB[ct], b_c], writes=[b_XC[ct]])
        for j in range(1, 4):
            p.op("dve", (lambda e, ct=ct, j=j: e.scalar_tensor_tensor(XC[:, 0, :], XB[:, 0, j:j + L], cw[:, ct * 4 + j:ct * 4 + j + 1], XC[:, 0, :], ALU.mult, ALU.add)),
                 reads=[b_XB[ct], b_XC[ct], b_c], writes=[b_XC[ct]])
    xi = [0]
    for ct in (CT,):
        for (c0, n) in chunks_of(L, 512):
            xcb, b_xcb = XCr[xi[0] % 2]; xi[0] += 1
            p.op("act", (lambda e, xcb=xcb, ct=ct, c0=c0, n=n: e.activation(xcb[:, 0:n], XC[:, 0, c0:c0 + n], AF.Copy)), reads=[b_XC[ct]], writes=[b_xcb])
            a, b_a = nacc()
            p.op("pe", (lambda e, a=a, ct=ct, xcb=xcb, n=n: e.matmul(a[:, 0:n], wab[:, ct, :], xcb[:, 0:n], start=True, stop=True)),
                 reads=[b_cp, b_xcb], writes=[b_a])
            p.op("act", (lambda e, a=a, ct=ct, c0=c0, n=n: e.activation(RR[:, 0, c0:c0 + n], a[:, 0:n], AF.Sigmoid, bias=small["rg_ba"][:, ct:ct + 1], scale=1.0)),
                 reads=[b_a, b_c], writes=[b_RR[ct]])
            a, b_a = nacc()
            p.op("pe", (lambda e, a=a, ct=ct, xcb=xcb, n=n: e.matmul(a[:, 0:n], wxb[:, ct, :], xcb[:, 0:n], start=True, stop=True)),
                 reads=[b_cp, b_xcb], writes=[b_a])
            p.op("act", (lambda e, a=a, ct=ct, c0=c0, n=n: e.activation(IG[:, 0, c0:c0 + n], a[:, 0:n], AF.Sigmoid, bias=small["rg_bx"][:, ct:ct + 1], scale=1.0)),
                 reads=[b_a, b_c], writes=[b_IG[ct]])
    s_o = p.dsem("o")
    outs = []
    for ct in (CT,):
        A_ = RR[:, 0, :]; U_ = IG[:, 0, :]; X_ = XC[:, 0, :]; Y_ = YB[:, 0, :]; T_ = XB[:, 0, 0:L]
        bA, bU, bX, bY, bT = b_RR[ct], b_IG[ct], b_XC[ct], b_YB[ct], b_XB[ct]
        p.op("act", (lambda e, A_=A_, ct=ct: e.activation(A_, A_, AF.Exp, scale=c8[:, ct:ct + 1])), reads=[bA, b_c8], writes=[bA])
        p.op("dve", (lambda e, T_=T_, A_=A_: e.tensor_tensor(T_, A_, A_, ALU.mult)), reads=[bA], writes=[bT])
        p.op("dve", (lambda e, T_=T_: e.tensor_scalar(T_, T_, -1.0, 1.0, ALU.mult, ALU.add)), reads=[bT], writes=[bT])
        p.op("act", (lambda e, T_=T_: e.activation(T_, T_, AF.Sqrt)), reads=[bT], writes=[bT])
        p.op("dve", (lambda e, U_=U_, X_=X_: e.tensor_tensor(U_, U_, X_, ALU.mult)), reads=[bU, bX], writes=[bU])
        p.op("dve", (lambda e, U_=U_, T_=T_: e.tensor_tensor(U_, U_, T_, ALU.mult)), reads=[bU, bT], writes=[bU])
        for si, (s0, sn) in enumerate(chunks_of(L, 2048)):
            init = 0.0 if si == 0 else XC[:, 0, s0 - 1:s0]
            p.op("dve", (lambda e, ct=ct, s0=s0, sn=sn, init=init: e.tensor_tensor_scan(
                XC[:, 0, s0:s0 + sn], RR[:, 0, s0:s0 + sn], IG[:, 0, s0:s0 + sn], init, ALU.mult, ALU.add)),
                reads=[bA, bU, bX], writes=[bX])
        p.op("dve", (lambda e, T_=T_, Y_=Y_: e.tensor_tensor(T_, Y_, Y_, ALU.mult)), reads=[bY], writes=[bT])
        p.op("dve", (lambda e, T_=T_: e.tensor_scalar(T_, T_, 0.044715, 1.0, ALU.mult, ALU.add)), reads=[bT], writes=[bT])
        p.op("dve", (lambda e, T_=T_, Y_=Y_: e.tensor_tensor(T_, T_, Y_, ALU.mult)), reads=[bT, bY], writes=[bT])
        p.op("act", (lambda e, T_=T_: e.activation(T_, T_, AF.Sigmoid, scale=1.5957691216057308)), reads=[bT], writes=[bT])
        p.op("dve", (lambda e, T_=T_, Y_=Y_: e.tensor_tensor(T_, T_, Y_, ALU.mult)), reads=[bT, bY], writes=[bT])
        p.op("dve", (lambda e, T_=T_, X_=X_: e.tensor_tensor(T_, T_, X_, ALU.mult)), reads=[bT, bX], writes=[bT])
        b_o = p.buf()
        p.dma("sp", (lambda e, ct=ct, T_=T_: e.dma_start(out=y_rg[ct * 128:(ct + 1) * 128, :], in_=T_)), s_o, reads=[bT], writes=[b_o])
        outs.append(b_o)
    p.wait_all("sp", outs)
    p.emit()


def make_masks(p, strict):
    ones = p.sb("mones", [128, 512], F32)
    b = p.buf()
    p.op("pool", lambda e: e.memset(ones[:], 1.0), writes=[b])
    ms = []
    for r in range(4):
        m = p.sb(f"mask{r}", [128, 512], F32)
        p.op("pool", (lambda e, m=m, r=r: e.affine_select(m[:], ones[:], [[1, 512]], ALU.is_gt if strict else ALU.is_ge, 0.0,
                                                           base=-128 * r, channel_multiplier=-1)), reads=[b], writes=[b])
        ms.append(m)
    return ms, b


def phase_sb(nc, xnd, d, KD, L, y_sb):
    p = Prog(nc)
    NTL = (L + 127) // 128
    s_c = p.dsem("c")
    wf, b_wf = load_w(p, "wf", d["w_sb_fm"], 4, KD, s_c)
    wt, b_wt = load_w(p, "wt", d["w_sb_tm"], 2, KD, s_c)
    b_c = p.buf()
    bfm = p.sb("bfm", [128, 4], F32); btm = p.sb("btm", [128, 256], F32); gqk = p.sb("gqk", [128, 2], F32)
    p.dma("sp", lambda e: e.dma_start(out=bfm[:], in_=d["b_sb_fm"][:, :]), s_c, writes=[b_c])
    p.dma("sp", lambda e: e.dma_start(out=btm[:], in_=d["b_sb_tm"][:, :]), s_c, writes=[b_c])
    p.dma("sp", lambda e: e.dma_start(out=gqk[:], in_=d["sb_gqk"][:, :]), s_c, writes=[b_c])
    ones_b = p.sb("ones_b", [128, 128], BF16)
    onesf = p.sb("onesf", [128, 128], F32); utri = p.sb("utri", [128, 128], F32)
    p.op("dve", lambda e: e.memset(ones_b[:], 1.0), writes=[b_c])
    p.op("dve", lambda e: e.memset(onesf[:], 1.0), writes=[b_c])
    p.op("pool", lambda e: e.affine_select(utri[:], onesf[:], [[-1, 128]], ALU.is_ge, 0.0, base=0, channel_multiplier=1), reads=[b_c], writes=[b_c])
    masks, b_m = make_masks(p, strict=True)
    p.op("dve", lambda e: e.tensor_scalar(gqk[:, 0:1], gqk[:, 0:1], 128.0 ** -0.5, None, ALU.mult), reads=[b_c], writes=[b_c])
    QK = p.sb("QK", [128, 4, L], BF16)
    V = p.sb("V", [128, NTL, 256], BF16)
    b_QK = [p.buf() for _ in range(4)]; b_V = p.buf()
    accs = [(p.ps(f"acc{i}", [128, 512], F32), p.buf()) for i in range(6)]
    ai = [0]

    def nacc():
        a = accs[ai[0] % 6]; ai[0] += 1
        return a
    tq = [(p.sb(f"tq{i}", [128, 256], F32), p.buf()) for i in range(4)]
    ts = [(p.sb(f"ts{i}", [128, 256], BF16), p.buf()) for i in range(4)]
    rs = [(p.sb(f"rs{i}", [128, 256], F32), p.buf()) for i in range(4)]
    xs = XnStream(p, xnd, KD)
    ti = [0]
    for (c0, n) in chunks_of(L, xs.cs):
        xt, b_x = xs.load(c0, n)
        for j in range(4):
            a, b_a = nacc()
            for kc in range(KD):
                p.op("pe", (lambda e, a=a, j=j, kc=kc, xt=xt, n=n: e.matmul(a[:, 0:n], wf[:, j, kc, :], xt[:, kc, 0:n], start=(kc == 0), stop=(kc == KD - 1))),
                     reads=[b_wf, b_x], writes=[b_a], inc=(kc == KD - 1))
            q, b_q = tq[ti[0] % 4]; sq, b_sq = ts[ti[0] % 4]; r, b_r = rs[ti[0] % 4]; ti[0] += 1
            p.op("act", (lambda e, a=a, q=q, j=j, n=n: e.activation(q[:, 0:n], a[:, 0:n], AF.Identity, bias=bfm[:, j:j + 1], scale=1.0)),
                 reads=[b_a, b_c], writes=[b_q])
            p.op("act", (lambda e, q=q, sq=sq, n=n: e.activation(sq[:, 0:n], q[:, 0:n], AF.Square)), reads=[b_q], writes=[b_sq])
            a2, b_a2 = nacc()
            p.op("pe", (lambda e, a2=a2, sq=sq, n=n: e.matmul(a2[:, 0:n], ones_b[:], sq[:, 0:n], start=True, stop=True)),
                 reads=[b_sq, b_c], writes=[b_a2])
            p.op("act", (lambda e, a2=a2, r=r, n=n: e.activation(r[:, 0:n], a2[:, 0:n], AF.Sqrt, bias=EPS, scale=1.0 / 128)), reads=[b_a2], writes=[b_r])
            p.op("dve", (lambda e, r=r, n=n: e.reciprocal(r[:, 0:n], r[:, 0:n])), reads=[b_r], writes=[b_r])
            gcol = 0 if j < 2 else 1
            p.op("dve", (lambda e, q=q, r=r, j=j, c0=c0, n=n, gcol=gcol: e.scalar_tensor_tensor(QK[:, j, c0:c0 + n], q[:, 0:n], gqk[:, gcol:gcol + 1], r[:, 0:n], ALU.mult, ALU.mult)),
                 reads=[b_q, b_r, b_c], writes=[b_QK[j]])
        for (o, m) in chunks_of(n, 128):
            tl = (c0 + o) // 128
            a, b_a = nacc()
            for kc in range(KD):
                p.op("pe", (lambda e, a=a, kc=kc, xt=xt, o=o, m=m: e.matmul(a[0:m, 0:128], xt[:, kc, o:o + m], wt[:, 0, kc, :], start=(kc == 0), stop=(kc == KD - 1))),
                     reads=[b_wt, b_x], writes=[b_a], inc=False)
            for kc in range(KD):
                p.op("pe", (lambda e, a=a, kc=kc, xt=xt, o=o, m=m: e.matmul(a[0:m, 256:384], xt[:, kc, o:o + m], wt[:, 1, kc, :], start=(kc == 0), stop=(kc == KD - 1))),
                     reads=[b_wt, b_x], writes=[b_a], inc=(kc == KD - 1))
            p.op("dve", (lambda e, a=a, tl=tl, m=m: e.tensor_tensor(V[0:m, tl, 0:128], a[0:m, 0:128], btm[0:m, 0:128], ALU.add)), reads=[b_a, b_c], writes=[b_V])
            p.op("dve", (lambda e, a=a, tl=tl, m=m: e.tensor_tensor(V[0:m, tl, 128:256], a[0:m, 256:384], btm[0:m, 128:256], ALU.add)), reads=[b_a, b_c], writes=[b_V])
    NR = 3
    EZ = [(p.sb(f"ez{i}", [128, 512], F32), p.buf()) for i in range(NR)]
    SP = [(p.sb(f"sp{i}", [128, 512], F32), p.buf()) for i in range(NR)]
    ER = [(p.sb(f"er{i}", [128, 512], F32), p.buf()) for i in range(NR)]
    AA = [(p.sb(f"aa{i}", [128, 512], BF16), p.buf()) for i in range(NR)]
    ACC = p.sb("ACC", [128, 512], F32); b_ACC = p.buf()
    OT = [(p.sb(f"ot{i}", [128, 512], F32), p.buf()) for i in range(2)]
    pos = [(p.ps(f"po{i}", [128, 512], F32), p.buf()) for i in range(2)]
    s_o = p.dsem("o")
    outs = []
    items = []
    gi = 0
    for h in range(2):
        for (c0, n) in chunks_of(L, 512):
            jmax = (c0 + n - 1) // 128
            jl = list(range(jmax, -1, -1))
            for idx, j in enumerate(jl):
                k0 = j * 128
                kb = min(128, L - k0)
                diag = (k0 + kb - 1) >= c0
                items.append(dict(h=h, c0=c0, n=n, j=j, k0=k0, kb=kb, diag=diag, r=((k0 - c0) // 128 if diag else None),
                                  first=(idx == 0), last=(idx == len(jl) - 1), slot=len(items) % NR, grp=gi))
            gi += 1

    def st1(it):
        h, c0, n, k0, kb = it["h"], it["c0"], it["n"], it["k0"], it["kb"]
        ez, b_ez = EZ[it["slot"]]; sp, b_sp = SP[it["slot"]]
        z, b_z = nacc()
        p.op("pe", (lambda e: e.matmul(z[0:kb, 0:n], QK[:, 2 + h, k0:k0 + kb], QK[:, h, c0:c0 + n], start=True, stop=True)),
             reads=[b_QK[2 + h], b_QK[h]], writes=[b_z])
        p.op("act", (lambda e: e.activation(ez[0:kb, 0:n], z[0:kb, 0:n], AF.Exp)), reads=[b_z], writes=[b_ez])
        p.op("act", (lambda e: e.activation(sp[0:kb, 0:n], ez[0:kb, 0:n], AF.Ln, bias=1.0, scale=1.0)), reads=[b_ez], writes=[b_sp])
        if it["diag"]:
            mk = masks[it["r"]]
            p.op("dve", (lambda e: e.tensor_tensor(sp[0:kb, 0:n], sp[0:kb, 0:n], mk[0:kb, 0:n], ALU.mult)), reads=[b_sp, b_m], writes=[b_sp])

    def st2(it):
        n, kb, first = it["n"], it["kb"], it["first"]
        ez, b_ez = EZ[it["slot"]]; sp, b_sp = SP[it["slot"]]; er, b_er = ER[it["slot"]]; aa, b_aa = AA[it["slot"]]
        rp, b_rp = nacc()
        p.op("pe", (lambda e: e.matmul(rp[0:kb, 0:n], utri[0:kb, 0:kb], sp[0:kb, 0:n], start=True, stop=first)),
             reads=[b_sp, b_c], writes=[b_rp], inc=first)
        if not first:
            p.op("pe", (lambda e: e.matmul(rp[0:kb, 0:n], onesf[:, 0:kb], ACC[:, 0:n], start=False, stop=True)),
                 reads=[b_ACC, b_c], writes=[b_rp])
        if first:
            if kb < 128:
                p.op("pool", (lambda e: e.memset(ACC[:, 0:n], 0.0)), writes=[b_ACC])
            p.op("pool", (lambda e: e.tensor_copy(ACC[0:kb, 0:n], sp[0:kb, 0:n])), reads=[b_sp], writes=[b_ACC])
        elif not it["last"]:
            p.op("pool", (lambda e: e.tensor_tensor(ACC[0:kb, 0:n], ACC[0:kb, 0:n], sp[0:kb, 0:n], ALU.add)), reads=[b_sp, b_ACC], writes=[b_ACC])
        p.op("act", (lambda e: e.activation(er[0:kb, 0:n], rp[0:kb, 0:n], AF.Exp, scale=-1.0)), reads=[b_rp], writes=[b_er])
        if it["diag"]:
            mk = masks[it["r"]]
            p.op("dve", (lambda e: e.tensor_tensor(er[0:kb, 0:n], ez[0:kb, 0:n], er[0:kb, 0:n], ALU.mult)), reads=[b_ez, b_er], writes=[b_er])
            p.op("dve", (lambda e: e.tensor_tensor(aa[0:kb, 0:n], er[0:kb, 0:n], mk[0:kb, 0:n], ALU.mult)), reads=[b_er, b_m], writes=[b_aa])
        else:
            p.op("dve", (lambda e: e.tensor_tensor(aa[0:kb, 0:n], ez[0:kb, 0:n], er[0:kb, 0:n], ALU.mult)), reads=[b_ez, b_er], writes=[b_aa])

    def st3(it):
        h, c0, n, j, kb = it["h"], it["c0"], it["n"], it["j"], it["kb"]
        aa, b_aa = AA[it["slot"]]
        po, b_po = pos[it["grp"] % 2]
        p.op("pe", (lambda e: e.matmul(po[:, 0:n], V[0:kb, j, h * 128:(h + 1) * 128], aa[0:kb, 0:n], start=it["first"], stop=it["last"])),
             reads=[b_aa, b_V], writes=[b_po], inc=True)
        if it["last"]:
            ot, b_ot = OT[it["grp"] % 2]
            p.op("act", (lambda e: e.activation(ot[:, 0:n], po[:, 0:n], AF.Copy)), reads=[b_po], writes=[b_ot])
            b_o = p.buf()
            p.dma("sp", (lambda e: e.dma_start(out=y_sb[h * 128:(h + 1) * 128, c0:c0 + n], in_=ot[:, 0:n])), s_o, reads=[b_ot], writes=[b_o])
            outs.append(b_o)

    stages = [st1, st2, st3]
    for t in range(len(items) + len(stages) - 1):
        for si, fn in enumerate(stages):
            i = t - si
            if 0 <= i < len(items):
                fn(items[i])
    p.wait_all("sp", outs)
    p.emit()


def phase_ml(nc, xnd, d, KD, L, y_ml):
    NTL = (L + 127) // 128
    with nc.sbuf_tensor("mlQK", [128, 4, L], BF16) as QK, \
            nc.sbuf_tensor("mlV1", [128, NTL, 257], BF16) as V1, \
            nc.sbuf_tensor("mlOG", [128, NTL, 256], BF16) as OG, \
            nc.sbuf_tensor("mlIF", [1, 2, L], F32) as IF:
        _ml_proj_fm(nc, xnd, d, KD, L, QK, IF)
        _ml_proj_tm(nc, xnd, d, KD, L, V1, OG)
        _ml_attn(nc, d, KD, L, QK, V1, OG, IF, y_ml)


def _ml_proj_fm(nc, xnd, d, KD, L, QK, IF):
    p = Prog(nc)
    wf, b_wf = load_w(p, "wf", d["w_ml_fm"], 4, KD, None)
    s_c = p.dsem("c"); b_c = p.buf()
    wif = p.sb("wif", [128, KD, 2], BF16)
    s_p = p.dsem("cp"); b_cp = p.buf()
    p.dma("pool", lambda e: e.dma_start(out=wif[:], in_=d["w_ml_if"][:, :, :]), s_p, writes=[b_cp])
    bfm = p.sb("bfm", [128, 4], F32); bif = p.sb("bif", [1, 2], F32)
    p.dma("sp", lambda e: e.dma_start(out=bfm[:], in_=d["b_ml_fm"][:, :]), s_c, writes=[b_c])
    p.dma("sp", lambda e: e.dma_start(out=bif[:], in_=d["b_ml_if"][:, :]), s_c, writes=[b_c])
    accs = [(p.ps(f"acc{i}", [128, 512], F32), p.buf()) for i in range(4)]
    ai = [0]

    def nacc():
        a = accs[ai[0] % 4]; ai[0] += 1
        return a
    b_QK = [p.buf() for _ in range(4)]; b_IF = p.buf()
    xs = XnStream(p, xnd, KD)
    for (c0, n) in chunks_of(L, xs.cs):
        xt, b_x = xs.load(c0, n)
        for j in range(4):
            a, b_a = nacc()
            for kc in range(KD):
                p.op("pe", (lambda e, a=a, j=j, kc=kc, xt=xt, n=n: e.matmul(a[:, 0:n], wf[:, j, kc, :], xt[:, kc, 0:n], start=(kc == 0), stop=(kc == KD - 1))),
                     reads=[b_wf, b_x], writes=[b_a], inc=(kc == KD - 1))
            p.op("act", (lambda e, a=a, j=j, c0=c0, n=n: e.activation(QK[:, j, c0:c0 + n], a[:, 0:n], AF.Identity, bias=bfm[:, j:j + 1], scale=1.0)),
                 reads=[b_a, b_c], writes=[b_QK[j]])
        for q in range(2):
            a, b_a = nacc()
            for kc in range(KD):
                p.op("pe", (lambda e, a=a, q=q, kc=kc, xt=xt, n=n: e.matmul(a[0:1, 0:n], wif[:, kc, q:q + 1], xt[:, kc, 0:n], start=(kc == 0), stop=(kc == KD - 1))),
                     reads=[b_cp, b_x], writes=[b_a], inc=(kc == KD - 1))
            p.op("act", (lambda e, a=a, q=q, c0=c0, n=n: e.activation(IF[0:1, q, c0:c0 + n], a[0:1, 0:n], AF.Identity, bias=bif[0:1, q:q + 1], scale=1.0)),
                 reads=[b_a, b_c], writes=[b_IF])
    p.wait_all("act", b_QK + [b_IF])
    p.emit()


def _ml_proj_tm(nc, xnd, d, KD, L, V1, OG):
    p = Prog(nc)
    wt, b_wt = load_w(p, "wt", d["w_ml_tm"], 4, KD, None)
    s_c = p.dsem("c"); b_c = p.buf()
    btm = p.sb("btm", [128, 512], F32)
    p.dma("sp", lambda e: e.dma_start(out=btm[:], in_=d["b_ml_tm"][:, :]), s_c, writes=[b_c])
    accs = [(p.ps(f"acc{i}", [128, 512], F32), p.buf()) for i in range(4)]
    ai = [0]

    def nacc():
        a = accs[ai[0] % 4]; ai[0] += 1
        return a
    b_V = p.buf(); b_O = p.buf()
    tmp = [(p.sb(f"tmp{i}", [128, 256], F32), p.buf()) for i in range(2)]
    p.op("dve", lambda e: e.memset(V1[:, :, 256:257], 1.0), writes=[b_V])
    xs = XnStream(p, xnd, KD)
    for (c0, n) in chunks_of(L, xs.cs):
        xt, b_x = xs.load(c0, n)
        for (o, m) in chunks_of(n, 128):
            tl = (c0 + o) // 128
            a, b_a = nacc()
            for j in range(4):
                for kc in range(KD):
                    p.op("pe", (lambda e, a=a, j=j, kc=kc, xt=xt, o=o, m=m: e.matmul(a[0:m, j * 128:(j + 1) * 128], xt[:, kc, o:o + m], wt[:, j, kc, :], start=(kc == 0), stop=(kc == KD - 1))),
                         reads=[b_wt, b_x], writes=[b_a], inc=(kc == KD - 1 and j == 3))
            p.op("dve", (lambda e, a=a, tl=tl, m=m: e.tensor_tensor(V1[0:m, tl, 0:256], a[0:m, 0:256], btm[0:m, 0:256], ALU.add)), reads=[b_a, b_c], writes=[b_V])
            t, b_t = tmp[tl % 2]
            p.op("dve", (lambda e, a=a, t=t, m=m: e.tensor_tensor(t[0:m, :], a[0:m, 256:512], btm[0:m, 256:512], ALU.add)), reads=[b_a, b_c], writes=[b_t])
            p.op("act", (lambda e, t=t, tl=tl, m=m: e.activation(OG[0:m, tl, :], t[0:m, :], AF.Sigmoid)), reads=[b_t], writes=[b_O])
    p.wait_all("act", [b_V, b_O])
    p.wait_all("dve", [b_V, b_O])
    p.emit()


def _ml_attn(nc, d, KD, L, QK, V1, OG, IF, y_ml):
    p = Prog(nc)
    NTL = (L + 127) // 128
    s_c = p.dsem("c"); b_c = p.buf()
    gt = p.sb("gt", [128, 256], F32)
    p.dma("sp", lambda e: e.dma_start(out=gt[:], in_=d["ml_g"][:, :]), s_c, writes=[b_c])
    masks, b_m = make_masks(p, strict=False)
    R1 = p.sb("R1", [1, L], F32); R2 = p.sb("R2", [1, L], F32)
    irow = IF[0:1, 0, :]; frow = IF[0:1, 1, :]
    b_i, b_f, b_r1, b_r2 = p.buf(), p.buf(), p.buf(), p.buf()
    one1 = p.sb("one1", [1, 128], F32); b_one = p.buf()
    p.op("dve", lambda e: e.memset(one1[:], 1.0), writes=[b_one])
    p.op("dve", lambda e: e.memset(R2[:], 1.0), writes=[b_r2])
    p.op("act", lambda e: e.activation(frow, frow, AF.Exp, scale=-1.0), writes=[b_f])
    p.op("act", lambda e: e.activation(frow, frow, AF.Ln, bias=1.0, scale=1.0), reads=[b_f], writes=[b_f])
    for si, (s0, sn) in enumerate(chunks_of(L, 2048)):
        init = 0.0 if si == 0 else R1[0:1, s0 - 1:s0]
        p.op("dve", (lambda e, s0=s0, sn=sn, init=init: e.tensor_tensor_scan(R1[0:1, s0:s0 + sn], R2[0:1, s0:s0 + sn], IF[0:1, 1, s0:s0 + sn], init, ALU.mult, ALU.add)),
             reads=[b_f, b_r2, b_r1], writes=[b_r1])
    p.op("dve", lambda e: e.tensor_tensor(irow, irow, R1[0:1, :], ALU.add), reads=[b_r1], writes=[b_i])
    for si, (s0, sn) in enumerate(chunks_of(L, 2048)):
        init = 0.0 if si == 0 else IF[0:1, 1, s0 - 1:s0]
        p.op("dve", (lambda e, s0=s0, sn=sn, init=init: e.tensor_tensor_scan(IF[0:1, 1, s0:s0 + sn], IF[0:1, 0, s0:s0 + sn], IF[0:1, 0, s0:s0 + sn], init, ALU.max, ALU.max)),
             reads=[b_i, b_f, b_r1], writes=[b_f])
    p.op("dve", lambda e: e.tensor_tensor(R2[0:1, :], R1[0:1, :], frow, ALU.subtract), reads=[b_r1, b_f], writes=[b_r2])
    p.op("act", lambda e: e.activation(R2[0:1, :], R2[0:1, :], AF.Exp), reads=[b_r2], writes=[b_r2])
    p.op("dve", lambda e: e.tensor_scalar(frow, frow, -1.0, None, ALU.mult), reads=[b_f, b_r2], writes=[b_f])
    accs = [(p.ps(f"acc{i}", [128, 512], F32), p.buf()) for i in range(3)]
    ai = [0]

    def nacc():
        a = accs[ai[0] % 3]; ai[0] += 1
        return a
    NMX = p.sb("NMX", [128, L], F32); b_nmx = p.buf()
    ATK = p.sb("ATK", [128, NTL], F32); ETK = p.sb("ETK", [128, NTL], F32); b_tk = p.buf()
    for (c0, n) in chunks_of(L, 512):
        a, b_a = nacc()
        p.op("pe", (lambda e, a=a, c0=c0, n=n: e.matmul(a[:, 0:n], one1[0:1, :], IF[0:1, 1, c0:c0 + n], start=True, stop=True)),
             reads=[b_f, b_one], writes=[b_a])
        p.op("act", (lambda e, a=a, c0=c0, n=n: e.activation(NMX[:, c0:c0 + n], a[:, 0:n], AF.Copy)), reads=[b_a], writes=[b_nmx])
    for (src, dst, bsrc) in ((IF[0:1, 0, :], ATK, b_i), (R2[0:1, :], ETK, b_r2)):
        a, b_a = nacc()
        for tl, (k0, m) in enumerate(chunks_of(L, 128)):
            p.op("pe", (lambda e, a=a, src=src, tl=tl, k0=k0, m=m: e.matmul(a[0:m, tl:tl + 1], src[0:1, k0:k0 + m], one1[0:1, 0:1], start=True, stop=True)),
                 reads=[bsrc, b_one], writes=[b_a], inc=(tl == NTL - 1))
        p.op("dve", lambda e, dst=dst: e.memset(dst[:], 0.0), writes=[b_tk])
        for tl, (k0, m) in enumerate(chunks_of(L, 128)):
            if m == 128 and tl > 0:
                continue
            if tl == 0:
                nf = L // 128
                p.op("dve", (lambda e, a=a, dst=dst, nf=nf: e.tensor_copy(dst[:, 0:nf], a[:, 0:nf])), reads=[b_a], writes=[b_tk])
            else:
                p.op("dve", (lambda e, a=a, dst=dst, tl=tl, m=m: e.tensor_copy(dst[0:m, tl:tl + 1], a[0:m, tl:tl + 1])), reads=[b_a], writes=[b_tk])
    NR = 3
    WW = [(p.sb(f"ww{i}", [128, 512], F32), p.buf()) for i in range(NR)]
    PP = [(p.sb(f"pp{i}", [128, 512], BF16), p.buf()) for i in range(NR)]
    nums = [(p.ps(f"num{i}", [128, 512], F32), p.buf()) for i in range(4)]
    HH = [(p.sb(f"hh{i}", [128, 256], F32), p.buf()) for i in range(4)]
    SQ = [(p.sb(f"sqh{i}", [128, 256], F32), p.buf()) for i in range(4)]
    SM = [(p.sb(f"sm{i}", [128, 4], F32), p.buf()) for i in range(4)]
    YO = Ring(p, "yo", 4, [128, 256], F32)
    outs = []
    fi = [0]
    b_QK = p.buf(); b_V = p.buf(); b_O = p.buf()
    items = []
    for (c0, n) in chunks_of(L, 512):
        qbs = chunks_of(n, 128)
        jmax = (c0 + n - 1) // 128
        for j in range(jmax + 1):
            k0 = j * 128
            kb = min(128, L - k0)
            diag = (k0 + kb - 1) >= c0
            items.append(dict(c0=c0, n=n, j=j, k0=k0, kb=kb, diag=diag, r=((k0 - c0) // 128 if diag else None),
                              qbs=qbs, last=(j == jmax), slot=len(items) % NR))

    def st1(it):
        c0, n, j, k0, kb = it["c0"], it["n"], it["j"], it["k0"], it["kb"]
        ww, b_ww = WW[it["slot"]]
        st, b_st = nacc()
        it["st"] = (st, b_st)
        for dt in range(2):
            p.op("pe", (lambda e, dt=dt: e.matmul(st[0:kb, 0:n], QK[:, 2 + dt, k0:k0 + kb], QK[:, dt, c0:c0 + n], start=(dt == 0), stop=(dt == 1))),
                 reads=[b_QK], writes=[b_st], inc=(dt == 1))
        p.op("act", (lambda e: e.activation(ww[0:kb, 0:n], NMX[0:kb, c0:c0 + n], AF.Exp, bias=ATK[0:kb, j:j + 1], scale=1.0)),
             reads=[b_nmx, b_tk], writes=[b_ww])

    def st2(it):
        n, kb = it["n"], it["kb"]
        ww, b_ww = WW[it["slot"]]; pp, b_pp = PP[it["slot"]]
        st, b_st = it["st"]
        if it["diag"]:
            mk = masks[it["r"]]
            p.op("dve", (lambda e: e.scalar_tensor_tensor(ww[0:kb, 0:n], st[0:kb, 0:n], 0.0625, ww[0:kb, 0:n], ALU.mult, ALU.mult)),
                 reads=[b_st, b_ww], writes=[b_ww])
            p.op("dve", (lambda e: e.tensor_tensor(pp[0:kb, 0:n], ww[0:kb, 0:n], mk[0:kb, 0:n], ALU.mult)), reads=[b_ww, b_m], writes=[b_pp])
        else:
            p.op("dve", (lambda e: e.scalar_tensor_tensor(pp[0:kb, 0:n], st[0:kb, 0:n], 0.0625, ww[0:kb, 0:n], ALU.mult, ALU.mult)),
                 reads=[b_st, b_ww], writes=[b_pp])

    def st3(it):
        c0, n, j, kb = it["c0"], it["n"], it["j"], it["kb"]
        pp, b_pp = PP[it["slot"]]
        for qb, (qo, mq) in enumerate(it["qbs"]):
            q0 = c0 + qo
            jl = (q0 + mq - 1) // 128
            if j > jl:
                continue
            nu, b_nu = nums[qb]
            p.op("pe", (lambda e, nu=nu, qo=qo, mq=mq, jl=jl: e.matmul(nu[0:mq, 0:257], pp[0:kb, qo:qo + mq], V1[0:kb, j, :], start=(j == 0), stop=(j == jl))),
                 reads=[b_pp, b_V], writes=[b_nu])
        if not it["last"]:
            return
        lanes = []
        for qb, (qo, mq) in enumerate(it["qbs"]):
            ln_ = Lane(); lanes.append(ln_)
            q0 = c0 + qo
            tl = q0 // 128
            nu, b_nu = nums[qb]
            hh, b_hh = HH[fi[0] % 4]; sq, b_sq = SQ[fi[0] % 4]; sm, b_sm = SM[fi[0] % 4]; fi[0] += 1
            ln_.op("act", (lambda e, nu=nu, sm=sm, mq=mq: e.activation(sm[0:mq, 0:1], nu[0:mq, 256:257], AF.Abs)), reads=[b_nu, b_tk], writes=[b_sm])
            ln_.op("dve", (lambda e, sm=sm, mq=mq, tl=tl: e.tensor_tensor(sm[0:mq, 0:1], sm[0:mq, 0:1], ETK[0:mq, tl:tl + 1], ALU.max)), reads=[b_sm, b_tk], writes=[b_sm])
            ln_.op("dve", (lambda e, sm=sm, mq=mq: e.reciprocal(sm[0:mq, 0:1], sm[0:mq, 0:1])), reads=[b_sm], writes=[b_sm])
            ln_.op("act", (lambda e, nu=nu, hh=hh, sm=sm, mq=mq: e.activation(hh[0:mq, :], nu[0:mq, 0:256], AF.Copy, scale=sm[0:mq, 0:1])), reads=[b_nu, b_sm], writes=[b_hh])
            ln_.op("dve", (lambda e, hh=hh, sq=sq, mq=mq: e.tensor_tensor(sq[0:mq, :], hh[0:mq, :], hh[0:mq, :], ALU.mult)), reads=[b_hh], writes=[b_sq])
            ln_.op("dve", (lambda e, sq=sq, sm=sm, mq=mq: e.reduce_sum(sm[0:mq, 1:2], sq[0:mq, :], AX.X)), reads=[b_sq, b_sm], writes=[b_sm])
            ln_.op("act", (lambda e, sm=sm, mq=mq: e.activation(sm[0:mq, 2:3], sm[0:mq, 1:2], AF.Sqrt, bias=EPS, scale=1.0 / 256)), reads=[b_sm], writes=[b_sm])
            ln_.op("dve", (lambda e, sm=sm, mq=mq: e.reciprocal(sm[0:mq, 2:3], sm[0:mq, 2:3])), reads=[b_sm], writes=[b_sm])
            ln_.op("dve", (lambda e, hh=hh, sq=sq, sm=sm, mq=mq: e.scalar_tensor_tensor(sq[0:mq, :], hh[0:mq, :], sm[0:mq, 2:3], gt[0:mq, :], ALU.mult, ALU.mult)),
                   reads=[b_hh, b_sm, b_c], writes=[b_sq])
            yo, b_yo, s_yo = YO.next()
            ln_.op("dve", (lambda e, sq=sq, yo=yo, tl=tl, mq=mq: e.tensor_tensor(yo[0:mq, :], sq[0:mq, :], OG[0:mq, tl, :], ALU.mult)), reads=[b_sq, b_O], writes=[b_yo])
            it.setdefault("fin", []).append((yo, b_yo, s_yo, q0, mq))
        interleave(p, lanes)
        for (yo, b_yo, s_yo, q0, mq) in it["fin"]:
            b_o = p.buf()
            p.dma("sp", (lambda e, yo=yo, q0=q0, mq=mq: e.dma_start(out=y_ml[q0:q0 + mq, :], in_=yo[0:mq, :])), s_yo, reads=[b_yo], writes=[b_o])
            outs.append(b_o)

    stages = [st1, st2, st3]
    for t in range(len(items) + len(stages) - 1):
        for si, fn in enumerate(stages):
            i = t - si
            if 0 <= i < len(items):
                fn(items[i])
    p.wait_all("sp", outs)
    p.emit()


def phase_hg(nc, xnd, d, KD, L, y_hg, layer0=True):
    with nc.sbuf_tensor("hgQS", [128, 2, L], BF16) as QS, \
            nc.sbuf_tensor("hgKK", [128, 2, L], BF16) as KK, \
            nc.sbuf_tensor("hgLF", [128, 2, L], F32) as LF:
        _hg_proj(nc, xnd, d, KD, L, QS, KK, LF, layer0)
        _hg_main(nc, xnd, d, KD, L, QS, KK, LF, y_hg)


def _hg_proj(nc, xnd, d, KD, L, QS, KK, LF, layer0):
    p = Prog(nc)
    wf, b_wf = load_w(p, "wf", d["w_hg_fm"], 4, KD, None)
    s_c = p.dsem("c"); b_c = p.buf()
    bfm = p.sb("bfm", [128, 4], F32); lbt = p.sb("lbt", [128, 4], F32)
    p.dma("sp", lambda e: e.dma_start(out=bfm[:], in_=d["b_hg_fm"][:, :]), s_c, writes=[b_c])
    p.dma("sp", lambda e: e.dma_start(out=lbt[:], in_=d["hg_lb"][:, :]), s_c, writes=[b_c])
    lb = p.sb("lb", [128, 2], F32); oml = p.sb("oml", [128, 2], F32); noml = p.sb("noml", [128, 2], F32); b_lb = p.buf()
    if layer0:
        p.op("dve", lambda e: e.memset(lb[:], 0.0), writes=[b_lb])
    else:
        for h in range(2):
            p.op("dve", (lambda e, h=h: e.tensor_tensor(lb[:, h:h + 1], lbt[:, 2 * h + 1:2 * h + 2], lbt[:, 2 * h:2 * h + 1], ALU.subtract)), reads=[b_c], writes=[b_lb])
        p.op("act", lambda e: e.activation(lb[:], lb[:], AF.Sigmoid), reads=[b_lb], writes=[b_lb])
        p.op("dve", lambda e: e.tensor_scalar(lb[:], lb[:], 0.999, None, ALU.min), reads=[b_lb], writes=[b_lb])
    p.op("dve", lambda e: e.tensor_scalar(oml[:], lb[:], -1.0, 1.0, ALU.mult, ALU.add), reads=[b_lb], writes=[b_lb])
    p.op("dve", lambda e: e.tensor_scalar(noml[:], oml[:], -1.0, None, ALU.mult), reads=[b_lb], writes=[b_lb])
    accs = [(p.ps(f"acc{i}", [128, 512], F32), p.buf()) for i in range(4)]
    ai = [0]

    def nacc():
        a = accs[ai[0] % 4]; ai[0] += 1
        return a
    sg = [(p.sb(f"sg{i}", [128, 256], F32), p.buf()) for i in range(2)]
    fv = [(p.sb(f"fv{i}", [128, 256], F32), p.buf()) for i in range(2)]
    b_QS, b_KK, b_LF = p.buf(), p.buf(), p.buf()
    xs = XnStream(p, xnd, KD)
    it = [0]
    for (c0, n) in chunks_of(L, xs.cs):
        xt, b_x = xs.load(c0, n)
        for j in range(4):
            h = j % 2
            a, b_a = nacc()
            for kc in range(KD):
                p.op("pe", (lambda e, a=a, j=j, kc=kc, xt=xt, n=n: e.matmul(a[:, 0:n], wf[:, j, kc, :], xt[:, kc, 0:n], start=(kc == 0), stop=(kc == KD - 1))),
                     reads=[b_wf, b_x], writes=[b_a], inc=(kc == KD - 1))
            if j < 2:
                p.op("act", (lambda e, a=a, j=j, h=h, c0=c0, n=n: e.activation(QS[:, h, c0:c0 + n], a[:, 0:n], AF.Silu, bias=bfm[:, j:j + 1], scale=1.0)),
                     reads=[b_a, b_c], writes=[b_QS])
            else:
                s_, b_s = sg[it[0] % 2]; f_, b_f = fv[it[0] % 2]; it[0] += 1
                p.op("act", (lambda e, a=a, j=j, s_=s_, n=n: e.activation(s_[:, 0:n], a[:, 0:n], AF.Sigmoid, bias=bfm[:, j:j + 1], scale=1.0)),
                     reads=[b_a, b_c], writes=[b_s])
                p.op("dve", (lambda e, s_=s_, f_=f_, h=h, n=n: e.tensor_scalar(f_[:, 0:n], s_[:, 0:n], oml[:, h:h + 1], lb[:, h:h + 1], ALU.mult, ALU.add)),
                     reads=[b_s, b_lb], writes=[b_f])
                p.op("act", (lambda e, f_=f_, h=h, c0=c0, n=n: e.activation(LF[:, h, c0:c0 + n], f_[:, 0:n], AF.Ln)), reads=[b_f], writes=[b_LF])
                p.op("dve", (lambda e, s_=s_, h=h, c0=c0, n=n: e.tensor_scalar(KK[:, h, c0:c0 + n], s_[:, 0:n], noml[:, h:h + 1], oml[:, h:h + 1], ALU.mult, ALU.add)),
                     reads=[b_s, b_lb], writes=[b_KK])
    p.wait_all("act", [b_QS, b_KK, b_LF])
    p.wait_all("dve", [b_QS, b_KK, b_LF])
    p.emit()


def _hg_main(nc, xnd, d, KD, L, QS, KK, LF, y_hg):
    p = Prog(nc)
    s_c = p.dsem("c"); b_c = p.buf()
    s_p = p.dsem("cp"); b_wt = p.buf()
    wt = p.sb("wt", [128, KD, 4, 128], BF16)
    for j in range(4):
        p.dma("pool", (lambda e, j=j: e.dma_start(out=wt[:, :, j, :], in_=d["w_hg_tm"][j])), s_p, writes=[b_wt])
    btm = p.sb("btm", [64, 512], F32); gt = p.sb("gt", [64, 256], F32)
    p.dma("sp", lambda e: e.dma_start(out=btm[:], in_=d["b_hg_tm"][0:64, :]), s_c, writes=[b_c])
    p.dma("sp", lambda e: e.dma_start(out=gt[:], in_=d["hg_g"][0:64, :]), s_c, writes=[b_c])
    onesf = p.sb("onesf", [128, 64], F32); onesb = p.sb("onesb", [128, 128], BF16); ident = p.sb("ident", [128, 128], BF16)
    m64 = p.sb("m64", [64, 64], F32)
    p.op("pool", lambda e: e.memset(onesf[:], 1.0), writes=[b_c])
    p.op("pool", lambda e: e.memset(onesb[:], 1.0), reads=[b_c], writes=[b_c])
    p.op("pool", lambda e: e.affine_select(ident[:], onesb[:], [[-1, 128]], ALU.is_equal, 0.0, base=0, channel_multiplier=1), reads=[b_c], writes=[b_c])
    p.op("pool", lambda e: e.affine_select(m64[:], onesf[0:64, :], [[1, 64]], ALU.is_ge, 0.0, base=0, channel_multiplier=-1), reads=[b_c], writes=[b_c])
    Sf = p.sb("Sf", [128, 2, 128], F32); Sb = p.sb("Sb", [128, 2, 128], BF16)
    b_Sf = [p.buf(), p.buf()]; b_Sb = [p.buf(), p.buf()]
    for h in range(2):
        p.op("dve", (lambda e, h=h: e.memset(Sf[:, h, :], 0.0)), writes=[b_Sf[h]])
        p.op("dve", (lambda e, h=h: e.memset(Sb[:, h, :], 0.0)), writes=[b_Sb[h]])
    ps_ig = (p.ps("ps_ig", [128, 512], F32), p.buf())
    ps_a = [(p.ps(f"ps_a{i}", [128, 512], F32), p.buf()) for i in range(2)]
    ps_t = [(p.ps(f"ps_t{i}", [128, 1024], BF16), p.buf()) for i in range(1)]
    ps_o = [(p.ps(f"ps_o{i}", [128, 512], F32), p.buf()) for i in range(2)]
    ps_s = [(p.ps(f"ps_s{i}", [128, 512], F32), p.buf()) for i in range(2)]
    IT = [(p.sb(f"IT{i}", [64, 256], BF16), p.buf()) for i in range(2)]
    GT = [(p.sb(f"GT{i}", [64, 256], F32), p.buf()) for i in range(2)]
    def mk(name, shape, dt, n=2):
        return [(p.sb(f"{name}{i}", shape, dt), p.buf()) for i in range(n)]
    BT = mk("BT", [128, 64], F32, 4); EX = mk("EX", [128, 4, 64], F32, 4)
    QT = mk("QT", [128, 64], BF16, 4); KT = mk("KT", [128, 64], BF16, 4); QH = mk("QH", [128, 64], BF16, 4); KH = mk("KH", [128, 64], BF16, 4)
    NB = mk("NB", [128, 2], F32, 4)
    AM = mk("AM", [64, 64], BF16, 4); KHT = mk("KHT", [64, 128], BF16, 4)
    OO = mk("OO", [64, 128], F32, 4); SQ = mk("SQ", [64, 128], F32, 4); SM = mk("SM", [64, 4], F32, 4)
    YT = Ring(p, "yt", 2, [64, 256], F32)
    b_in = p.buf()
    xs = XnStream(p, xnd, KD, nslots=3, cs=256)
    outs = []
    cks = chunks_of(L, 64)
    it = [0]
    st = {}

    xq = {}

    grps = chunks_of(L, 256)

    def prefetch(gi_):
        if gi_ < len(grps) and gi_ not in xq:
            xq[gi_] = xs.load(grps[gi_][0], grps[gi_][1])

    def pre(ci):
        t0, m = cks[ci]
        gi_ = t0 // 256
        for a_ in range(3):
            prefetch(gi_ + a_)
        xt_full, b_x = xq[gi_]
        xo_ = t0 - gi_ * 256
        xt = xt_full[:, :, xo_:xo_ + 64]
        a, b_a = ps_ig
        for kc in range(KD):
            p.op("pe", (lambda e, kc=kc: e.matmul(a[0:m, 0:512], xt[:, kc, 0:m], wt[:, kc, :, :].rearrange("p j c -> p (j c)"), start=(kc == 0), stop=(kc == KD - 1))),
                 reads=[b_wt, b_x], writes=[b_a], inc=(kc == KD - 1))
        it_, b_it = IT[ci % 2]; g_, b_g = GT[ci % 2]
        p.op("dve", (lambda e: e.tensor_tensor(it_[0:m, :], a[0:m, 0:256], btm[0:m, 0:256], ALU.add)), reads=[b_a, b_c], writes=[b_it])
        p.op("dve", (lambda e: e.tensor_tensor(g_[0:m, :], a[0:m, 256:512], btm[0:m, 256:512], ALU.add)), reads=[b_a, b_c], writes=[b_g])
        p.op("act", (lambda e: e.activation(g_[0:m, :], g_[0:m, :], AF.Sigmoid)), reads=[b_g], writes=[b_g])
        ks = []
        lanes = []
        for h in range(2):
            ln_ = Lane(); lanes.append(ln_)
            k = it[0] % 4; it[0] += 1
            ks.append(k)
            bt, b_bt = BT[k]; ex, b_ex = EX[k]; b_ex1, b_ex2, b_ex3 = EXT[k]; qt, b_qt = QT[k]; kt, b_kt = KT[k]; qh, b_qh = QH[k]; kh, b_kh = KH[k]
            nb, b_nb = NB[k]
            mid = max(m // 2 - 1, 0)
            ln_.op("dve", (lambda e, bt=bt, h=h: e.tensor_tensor_scan(bt[:, 0:m], onesf[:, 0:m], LF[:, h, t0:t0 + m], 0.0, ALU.mult, ALU.add)),
                 reads=[b_in, b_c], writes=[b_bt])
            ln_.op("dve", (lambda e, bt=bt, nb=nb: e.tensor_scalar(nb[:, 0:1], bt[:, mid:mid + 1], -1.0, None, ALU.mult)), reads=[b_bt], writes=[b_nb])
            ln_.op("act", (lambda e, ex=ex, bt=bt, nb=nb: e.activation(ex[:, 0, 0:m], bt[:, 0:m], AF.Exp, bias=nb[:, 0:1], scale=1.0)), reads=[b_bt, b_nb], writes=[b_ex])
            ln_.op(HG_MUL_ENG, (lambda e, ex=ex, qt=qt, h=h: e.tensor_tensor(qt[:, 0:m], ex[:, 0, 0:m], QS[:, h, t0:t0 + m], ALU.mult)), reads=[b_ex, b_in], writes=[b_qt])
            ln_.op("act", (lambda e, ex=ex, bt=bt: e.activation(ex[:, 1, 0:m], bt[:, 0:m], AF.Exp, bias=bt[:, mid:mid + 1], scale=-1.0)), reads=[b_bt], writes=[b_ex1])
            ln_.op(HG_MUL_ENG, (lambda e, ex=ex, kt=kt, h=h: e.tensor_tensor(kt[:, 0:m], ex[:, 1, 0:m], KK[:, h, t0:t0 + m], ALU.mult)), reads=[b_ex1, b_in], writes=[b_kt])
            ln_.op("act", (lambda e, ex=ex, bt=bt: e.activation(ex[:, 2, 0:m], bt[:, 0:m], AF.Exp, bias=bt[:, m - 1:m], scale=-1.0)), reads=[b_bt], writes=[b_ex2])
            ln_.op(HG_MUL_ENG, (lambda e, ex=ex, kh=kh, h=h: e.tensor_tensor(kh[:, 0:m], ex[:, 2, 0:m], KK[:, h, t0:t0 + m], ALU.mult)), reads=[b_ex2, b_in], writes=[b_kh])
            ln_.op("act", (lambda e, ex=ex, bt=bt: e.activation(ex[:, 3, 0:m], bt[:, 0:m], AF.Exp)), reads=[b_bt], writes=[b_ex3])
            ln_.op(HG_MUL_ENG, (lambda e, ex=ex, qh=qh, h=h: e.tensor_tensor(qh[:, 0:m], ex[:, 3, 0:m], QS[:, h, t0:t0 + m], ALU.mult)), reads=[b_ex3, b_in], writes=[b_qh])
        st[ci] = ks
        return lanes

    def tail(ci):
        t0, m = cks[ci]
        lastc = (ci == len(cks) - 1)
        it_, b_it = IT[ci % 2]; g_, b_g = GT[ci % 2]
        yt, b_yt, s_yt = YT.next()
        lanes = []
        for h in range(2):
            ln_ = Lane(); lanes.append(ln_)
            k = st[ci][h]
            ex, b_ex = EX[k]; b_ex1, b_ex2, b_ex3 = EXT[k]; qt, b_qt = QT[k]; kt, b_kt = KT[k]; qh, b_qh = QH[k]; kh, b_kh = KH[k]
            am, b_am = AM[k]; kht, b_kht = KHT[k]; oo, b_oo = OO[k]; sq, b_sq = SQ[k]; sm, b_sm = SM[k]
            pa, b_pa = ps_a[h]
            ln_.op("pe", (lambda e, pa=pa, kt=kt, qt=qt: e.matmul(pa[0:m, 0:m], kt[:, 0:m], qt[:, 0:m], start=True, stop=True)), reads=[b_kt, b_qt], writes=[b_pa])
            ln_.op("dve", (lambda e, pa=pa, am=am: e.tensor_tensor(am[0:m, 0:m], pa[0:m, 0:m], m64[0:m, 0:m], ALU.mult)), reads=[b_pa, b_c], writes=[b_am])
            po, b_po = ps_o[h]
            ln_.op("pe", (lambda e, po=po, am=am, h=h: e.matmul(po[0:m, 0:128], am[0:m, 0:m], it_[0:m, h * 128:(h + 1) * 128], start=True, stop=False)),
                 reads=[b_am, b_it], writes=[b_po], inc=False)
            ln_.op("pe", (lambda e, po=po, qh=qh, h=h: e.matmul(po[0:m, 0:128], qh[:, 0:m], Sb[:, h, :], start=False, stop=True)),
                 reads=[b_qh, b_Sb[h]], writes=[b_po])
            if not lastc:
                pt, b_pt = ps_t[0]
                ln_.op("pe", (lambda e, pt=pt, kh=kh, h=h: e.transpose(pt[0:m, h * 128:(h + 1) * 128], kh[:, 0:m], ident[:])), reads=[b_kh, b_c], writes=[b_pt])
                ln_.op("act", (lambda e, pt=pt, kht=kht, h=h: e.activation(kht[0:m, :], pt[0:m, h * 128:(h + 1) * 128], AF.Copy)), reads=[b_pt], writes=[b_kht])
                pss, b_pss = ps_s[h]
                ln_.op("pe", (lambda e, pss=pss, kht=kht, h=h: e.matmul(pss[:, 0:128], kht[0:m, :], it_[0:m, h * 128:(h + 1) * 128], start=True, stop=True)),
                     reads=[b_kht, b_it], writes=[b_pss])
                ln_.op("dve", (lambda e, pss=pss, ex=ex, h=h: e.scalar_tensor_tensor(Sf[:, h, :], Sf[:, h, :], ex[:, 3, m - 1:m], pss[:, 0:128], ALU.mult, ALU.add)),
                     reads=[b_pss, b_ex3, b_Sf[h]], writes=[b_Sf[h]])
                ln_.op("act", (lambda e, h=h: e.activation(Sb[:, h, :], Sf[:, h, :], AF.Copy)), reads=[b_Sf[h]], writes=[b_Sb[h]])
            ln_.op("act", (lambda e, po=po, oo=oo: e.activation(oo[0:m, :], po[0:m, 0:128], AF.Copy)), reads=[b_po], writes=[b_oo])
            ln_.op("dve", (lambda e, oo=oo, sq=sq: e.tensor_tensor(sq[0:m, :], oo[0:m, :], oo[0:m, :], ALU.mult)), reads=[b_oo], writes=[b_sq])
            ln_.op("dve", (lambda e, sq=sq, sm=sm: e.reduce_sum(sm[0:m, 0:1], sq[0:m, :], AX.X)), reads=[b_sq], writes=[b_sm])
            ln_.op("act", (lambda e, sm=sm: e.activation(sm[0:m, 1:2], sm[0:m, 0:1], AF.Sqrt, bias=EPS, scale=1.0 / 128)), reads=[b_sm], writes=[b_sm])
            ln_.op("dve", (lambda e, sm=sm: e.reciprocal(sm[0:m, 1:2], sm[0:m, 1:2])), reads=[b_sm], writes=[b_sm])
            ln_.op("dve", (lambda e, oo=oo, sq=sq, sm=sm, h=h: e.scalar_tensor_tensor(sq[0:m, :], oo[0:m, :], sm[0:m, 1:2], gt[0:m, h * 128:(h + 1) * 128], ALU.mult, ALU.mult)),
                 reads=[b_oo, b_sm, b_c], writes=[b_sq])
            ln_.op("dve", (lambda e, sq=sq, yt=yt, h=h: e.tensor_tensor(yt[0:m, h * 128:(h + 1) * 128], sq[0:m, :], g_[0:m, h * 128:(h + 1) * 128], ALU.mult)),
                 reads=[b_sq, b_g], writes=[b_yt])
        def fin():
            b_o = p.buf()
            p.dma("sp", (lambda e: e.dma_start(out=y_hg[t0:t0 + m, :], in_=yt[0:m, :])), s_yt, reads=[b_yt], writes=[b_o])
            outs.append(b_o)
        return lanes, fin

    EXT = [(p.buf(), p.buf(), p.buf()) for _ in range(4)]
    interleave(p, pre(0))
    for ci in range(len(cks)):
        pl = pre(ci + 1) if ci + 1 < len(cks) else []
        tl, fin = tail(ci)
        interleave(p, pl + tl)
        fin()
    p.wait_all("sp", outs)
    p.emit()


def prep_A(w_in, b_in, g, l, P, KD):
    D = 128 * KD
    def cols(base, width=256):
        return slice(base + g * width, base + (g + 1) * width)
    def fm(sl):
        return tile_w(w_in[:, sl], KD, 2)
    def bfm(sl):
        return np.ascontiguousarray(b_in[sl].reshape(2, 128).T)
    def rep(v, n=128):
        return np.ascontiguousarray(np.broadcast_to(v[None, :], (n, v.shape[0])))
    d = {}
    d["w_ml_fm"] = np.concatenate([fm(cols(0)), fm(cols(1024))], 0)
    d["b_ml_fm"] = np.concatenate([bfm(cols(0)), bfm(cols(1024))], 1)
    d["w_ml_tm"] = np.concatenate([fm(cols(2048)), fm(cols(3072))], 0)
    d["b_ml_tm"] = rep(np.concatenate([b_in[cols(2048)], b_in[cols(3072)]]))
    ifc = [4096 + g, 4100 + g]
    d["w_ml_if"] = np.ascontiguousarray(w_in[:, ifc].reshape(KD, 128, 2).transpose(1, 0, 2))
    d["b_ml_if"] = np.ascontiguousarray(b_in[ifc].reshape(1, 2))
    d["ml_g"] = rep(P["ml_norm"][g])
    sb = 4104
    d["w_sb_fm"] = np.concatenate([fm(cols(sb)), fm(cols(sb + 1024))], 0)
    d["b_sb_fm"] = np.concatenate([bfm(cols(sb)), bfm(cols(sb + 1024))], 1)
    d["w_sb_tm"] = fm(cols(sb + 2048))
    d["b_sb_tm"] = rep(b_in[cols(sb + 2048)])
    d["sb_gqk"] = np.ascontiguousarray(np.stack([P["sb_q_norm"], P["sb_k_norm"]], 1))
    hg = 7176
    d["w_hg_fm"] = np.concatenate([fm(cols(hg)), fm(cols(hg + 1024))], 0)
    d["b_hg_fm"] = np.concatenate([bfm(cols(hg)), bfm(cols(hg + 1024))], 1)
    d["w_hg_tm"] = np.concatenate([fm(cols(hg + 2048)), fm(cols(hg + 3072))], 0)
    d["b_hg_tm"] = rep(np.concatenate([b_in[cols(hg + 2048)], b_in[cols(hg + 3072)]]))
    d["hg_lb"] = np.ascontiguousarray(np.stack([P["hg_lb_all"][0][cols(0)].reshape(2, 128).T, P["hg_lb_all"][1][cols(0)].reshape(2, 128).T], 1).reshape(128, 4))
    d["hg_g"] = rep(P["hg_norm"][2 * g:2 * g + 2].reshape(-1))
    rg = 11272
    d["w_rg_fm"] = np.concatenate([fm(cols(rg)), fm(cols(rg + 1024))], 0)
    d["b_rg_fm"] = np.concatenate([bfm(cols(rg)), bfm(cols(rg + 1024))], 1)
    cw = P["rg_conv_w"][:, cols(0)]
    d["rg_cw"] = np.ascontiguousarray(cw.reshape(4, 2, 128).transpose(2, 1, 0).reshape(128, 8))
    for nm, key in (("rg_cb", "rg_conv_b"), ("rg_ba", "rg_ba"), ("rg_bx", "rg_bx"), ("rg_lam", "rg_lambda")):
        d[nm] = np.ascontiguousarray(P[key][cols(0)].reshape(2, 128).T)
    for nm, key in (("rg_wa", "rg_wa"), ("rg_wx", "rg_wx")):
        blk = P[key][4 * g:4 * g + 4]
        bd = np.zeros((2, 128, 128), np.float32)
        for ct in range(2):
            for q in range(2):
                bd[ct, q * 64:(q + 1) * 64, q * 64:(q + 1) * 64] = blk[ct * 2 + q]
        d[nm] = bd
    return d


N_META = 16
MIXW = 13320
_CACHE = {}


def _build_A(KD, L, shapes, layer0):
    D = 128 * KD
    nc = bass.Bass("TRN2", target_bir_lowering=False)
    hT = nc.dram_tensor("hT", [D, L], F32, kind="ExternalInput").ap()
    gm = nc.dram_tensor("gm", [128, KD], F32, kind="ExternalInput").ap()
    d = {k: nc.dram_tensor(k, list(v), F32, kind="ExternalInput").ap() for k, v in shapes.items()}
    xnd = nc.dram_tensor("xnd", [D, L], BF16, kind="Internal").ap()
    y_ml = nc.dram_tensor("y_ml", [L, 256], F32, kind="ExternalOutput").ap()
    y_sb = nc.dram_tensor("y_sb", [256, L], F32, kind="ExternalOutput").ap()
    y_hg = nc.dram_tensor("y_hg", [L, 256], F32, kind="ExternalOutput").ap()
    y_rg = nc.dram_tensor("y_rg", [256, L], F32, kind="ExternalOutput").ap()
    phase_norm(nc, hT, xnd, gm, KD, L)
    phase_rg(nc, xnd, d, KD, L, y_rg)
    phase_sb(nc, xnd, d, KD, L, y_sb)
    phase_ml(nc, xnd, d, KD, L, y_ml)
    phase_hg(nc, xnd, d, KD, L, y_hg, layer0=layer0)
    return nc


B_CHUNKS = ((0, 528), (528, 500))
B_CORES = 8


def kernel(x, meta, norm_mix, norm_ffn, w_in, b_in, ml_norm, sb_q_norm, sb_k_norm, hg_lb, hg_norm,
           rg_conv_w, rg_conv_b, rg_wa, rg_ba, rg_wx, rg_bx, rg_lambda, w_up, w_out,
           w_ffn_gate, w_ffn_up, w_ffn_down):
    f32 = np.float32
    x = np.asarray(x, f32)
    Bn, S, D = x.shape
    KD = D // 128
    L = S + N_META
    NF = w_ffn_gate.shape[2] // 128
    depth = w_in.shape[0]
    h = np.concatenate([np.broadcast_to(np.asarray(meta, f32)[None], (Bn, N_META, D)), x], axis=1)
    TT = sum(n for _, n in B_CHUNKS)
    SPB = B_CORES // Bn
    assert SPB * TT == L
    for l in range(depth):
        P = {"ml_norm": np.asarray(ml_norm[l], f32), "sb_q_norm": np.asarray(sb_q_norm[l], f32), "sb_k_norm": np.asarray(sb_k_norm[l], f32),
             "hg_lb_all": np.asarray(hg_lb, f32), "hg_norm": np.asarray(hg_norm[l], f32),
             "rg_conv_w": np.asarray(rg_conv_w[l], f32), "rg_conv_b": np.asarray(rg_conv_b[l], f32),
             "rg_wa": np.asarray(rg_wa[l], f32), "rg_ba": np.asarray(rg_ba[l], f32), "rg_wx": np.asarray(rg_wx[l], f32),
             "rg_bx": np.asarray(rg_bx[l], f32), "rg_lambda": np.asarray(rg_lambda[l], f32)}
        wl = np.asarray(w_in[l], f32)
        bl = np.asarray(b_in[l], f32)
        gmix = np.ascontiguousarray(np.asarray(norm_mix[l], f32).reshape(KD, 128).T)
        hTs = [np.ascontiguousarray(h[b].T) for b in range(Bn)]
        maps = []
        for c in range(8):
            b, g = c // 4, c % 4
            dd = prep_A(wl, bl, g, l, P, KD)
            maps.append(dict(dd, hT=hTs[b], gm=gmix))
        key = ("A", l == 0)
        if key not in _CACHE:
            _CACHE[key] = _build_A(KD, L, {k: v.shape for k, v in maps[0].items() if k not in ("hT", "gm")}, l == 0)
        res = run_bass_kernel_spmd(_CACHE[key], maps, core_ids=list(range(8))).results
        del maps
        yT = np.empty((Bn, D, L), f32)
        for c in range(8):
            b, g = c // 4, c % 4
            r = res[c]
            yT[b, 0 * 1024 + g * 256:0 * 1024 + (g + 1) * 256] = np.asarray(r["y_ml"]).T
            yT[b, 1 * 1024 + g * 256:1 * 1024 + (g + 1) * 256] = np.asarray(r["y_sb"])
            yT[b, 2 * 1024 + g * 256:2 * 1024 + (g + 1) * 256] = np.asarray(r["y_hg"]).T
            yT[b, 3 * 1024 + g * 256:3 * 1024 + (g + 1) * 256] = np.asarray(r["y_rg"])
        del res
        W = prep_B_weights(wl[:, MIXW:], bl[MIXW:], np.asarray(w_up[l], f32), np.asarray(w_out[l], f32),
                           np.asarray(norm_mix[l], f32), np.asarray(norm_ffn[l], f32),
                           np.asarray(w_ffn_gate[l], f32), np.asarray(w_ffn_up[l], f32), np.asarray(w_ffn_down[l], f32), KD, NF)
        if "B" not in _CACHE:
            _CACHE["B"] = build_B(KD, NF, B_CHUNKS)
        maps = []
        for c in range(B_CORES):
            b, s = c // SPB, c % SPB
            maps.append(dict(W, hT=np.ascontiguousarray(hTs[b][:, s * TT:(s + 1) * TT]),
                             yT=np.ascontiguousarray(yT[b][:, s * TT:(s + 1) * TT])))
        res = run_bass_kernel_spmd(_CACHE["B"], maps, core_ids=list(range(B_CORES))).results
        del maps, W
        for c in range(B_CORES):
            b, s = c // SPB, c % SPB
            h[b, s * TT:(s + 1) * TT] = np.asarray(res[c]["hout"]).T
        del res, yT, hTs
    return np.ascontiguousarray(h[:, N_META:]).astype(f32)
```
